# Optimizing a Trainium2 kernel written in Bass

```python
import math
import jax
import jax.numpy as jnp
from jax import lax
import numpy as np

D_MODEL = 2048
BATCH = 2
SEQ = 16384
DEPTH = 2
DEC_BATCH = 32
DEC_SEQ = 64
PAST_LEN = 4096

CHUNK = 64
EPS = 1e-6
GN_EPS = 64e-5
W_MIX = D_MODEL
W_GRP = W_MIX // 4
M_HEADS = 4
M_DH = W_GRP // M_HEADS
S5_CH = 16
S5_GROUPS = W_GRP // S5_CH
S5_P = 64
R_DH = 64
R_HEADS = W_GRP // R_DH
R_LR_W = 64
R_LR_A = 64
R_LR_G = 128
G_HEADS = 4
G_DH = W_GRP // G_HEADS
CONV_W = 4
D_FF = -(-8 * D_MODEL // (3 * 256)) * 256
N_M = 4 * W_GRP + 2 * M_HEADS
N_S = W_GRP
N_R = 3 * W_GRP + R_LR_W + R_LR_A + R_LR_G
N_G = 4 * W_GRP + 2 * G_HEADS
N_IN = N_M + N_S + N_R + N_G

kernel_name = 'hybrid_stream_mlstm_s5_rwkv7_gdn_step'


def rmsnorm(x, g):
    xf = x.astype(jnp.float32)
    y = xf * lax.rsqrt(jnp.mean(xf * xf, -1, keepdims=True) + EPS) * g.astype(jnp.float32)
    return y.astype(x.dtype)


def l2norm(t):
    return t * lax.rsqrt(jnp.sum(t * t, -1, keepdims=True) + 1e-6)


def to_chunks(a, L):
    B, T = a.shape[:2]
    return jnp.moveaxis(a.reshape((B, T // L, L) + a.shape[2:]), 1, 0)


def from_chunks(a):
    NC, B, L = a.shape[:3]
    return jnp.moveaxis(a, 0, 1).reshape((B, NC * L) + a.shape[3:])


def mlstm_chunkwise(q, k, v, i_pre, f_pre, C0, n0, m0):
    D = q.shape[-1]
    L = min(CHUNK, q.shape[1])
    tri = jnp.tril(jnp.ones((L, L), dtype=bool))
    q = q * (D ** -0.5)
    lf = jax.nn.log_sigmoid(f_pre)

    def step(carry, inp):
        C, n, m = carry
        qc, kc, vc, lic, lfc = inp
        b = jnp.cumsum(jnp.swapaxes(lfc, 1, 2), axis=-1)
        li = jnp.swapaxes(lic, 1, 2)
        logD = jnp.where(tri, b[..., :, None] - b[..., None, :] + li[..., None, :], -jnp.inf)
        inter = b + m[..., None]
        mt = jnp.maximum(inter, jnp.max(logD, axis=-1))
        s = jnp.einsum('bthd,bshd->bhts', qc, kc) * jnp.exp(logD - mt[..., None])
        w_c = jnp.exp(inter - mt)
        num = (jnp.einsum('bhts,bshd->bthd', s, vc)
               + jnp.einsum('bthd,bhde->bthe', qc, C) * jnp.swapaxes(w_c, 1, 2)[..., None])
        den = jnp.sum(s, -1) + w_c * jnp.einsum('bthd,bhd->bht', qc, n)
        h = num / jnp.swapaxes(jnp.maximum(jnp.abs(den), jnp.exp(-mt)), 1, 2)[..., None]
        bL = b[..., -1]
        tail = bL[..., None] - b + li
        m_new = jnp.maximum(bL + m, jnp.max(tail, -1))
        wk = jnp.exp(tail - m_new[..., None])
        sc = jnp.exp(bL + m - m_new)
        C = sc[..., None, None] * C + jnp.einsum('bhs,bshd,bshe->bhde', wk, kc, vc)
        n = sc[..., None] * n + jnp.einsum('bhs,bshd->bhd', wk, kc)
        return (C, n, m_new), h

    xs = tuple(to_chunks(a, L) for a in (q, k, v, i_pre, lf))
    (C, n, m), h = lax.scan(step, (C0, n0, m0), xs)
    return from_chunks(h), C, n, m


def s5_combine(e1, e2):
    a1r, a1i, b1r, b1i = e1
    a2r, a2i, b2r, b2i = e2
    return (a1r * a2r - a1i * a2i, a1r * a2i + a1i * a2r,
            a2r * b1r - a2i * b1i + b2r, a2r * b1i + a2i * b1r + b2i)


def s5_mixer(u, p, h0_re, h0_im):
    B, T, _ = u.shape
    f32 = jnp.float32
    lam_re = p['s5_lam_re'].astype(f32)
    lam_im = p['s5_lam_im'].astype(f32)
    dt = jnp.exp(p['s5_log_dt'].astype(f32))[:, None]
    mag = jnp.exp(lam_re * dt)
    lb_re = mag * jnp.cos(lam_im * dt)
    lb_im = mag * jnp.sin(lam_im * dt)
    nr = lb_re - 1.0
    den = lam_re * lam_re + lam_im * lam_im
    f_re = (nr * lam_re + lb_im * lam_im) / den
    f_im = (lb_im * lam_re - nr * lam_im) / den
    B_re = p['s5_B_re'].astype(f32)
    B_im = p['s5_B_im'].astype(f32)
    Bb_re = f_re[..., None] * B_re - f_im[..., None] * B_im
    Bb_im = f_re[..., None] * B_im + f_im[..., None] * B_re
    ug = u.reshape(B, T, S5_GROUPS, S5_CH)
    bu_re = jnp.einsum('btgc,gpc->btgp', ug, Bb_re)
    bu_im = jnp.einsum('btgc,gpc->btgp', ug, Bb_im)
    a_re = jnp.broadcast_to(lb_re, bu_re.shape)
    a_im = jnp.broadcast_to(lb_im, bu_im.shape)
    A_re, A_im, H_re, H_im = lax.associative_scan(s5_combine, (a_re, a_im, bu_re, bu_im), axis=1)
    h_re = H_re + A_re * h0_re[:, None] - A_im * h0_im[:, None]
    h_im = H_im + A_re * h0_im[:, None] + A_im * h0_re[:, None]
    y = (jnp.einsum('gcp,btgp->btgc', p['s5_C_re'].astype(f32), h_re)
         - jnp.einsum('gcp,btgp->btgc', p['s5_C_im'].astype(f32), h_im))
    y = y.reshape(B, T, W_GRP) + p['s5_D'] * u
    y = jax.nn.gelu(y)
    y = y * jax.nn.sigmoid(y @ p['s5_w_glu'] + p['s5_b_glu'])
    return y, h_re[:, -1], h_im[:, -1]


def rwkv7_recurrence(r, k, v, ld, kk, a, S0):
    def step(S, inp):
        rt, kt, vt, ldt, kkt, at = inp
        S = (S * jnp.exp(ldt)[:, :, None, :]
             - jnp.einsum('bhvk,bhk->bhv', S, kkt)[..., None] * (kkt * at)[:, :, None, :]
             + vt[..., None] * kt[:, :, None, :])
        return S, jnp.einsum('bhvk,bhk->bhv', S, rt)
    xs = tuple(jnp.moveaxis(t, 1, 0) for t in (r, k, v, ld, kk, a))
    S, y = lax.scan(step, S0, xs)
    return jnp.moveaxis(y, 0, 1), S


def rwkv7_mixer(pr, p, S0, prev):
    B, T, _ = pr.shape
    shifted = jnp.concatenate([prev[:, None], pr[:, :-1]], axis=1)
    xm = pr + (shifted - pr) * p['rwkv_mu']
    r, k, v, wl, al, gl = jnp.split(
        xm, [W_GRP, 2 * W_GRP, 3 * W_GRP, 3 * W_GRP + R_LR_W, 3 * W_GRP + R_LR_W + R_LR_A], axis=-1)
    w = -jax.nn.softplus(-(p['rwkv_w0'] + jnp.tanh(wl) @ p['rwkv_w2'])) - 0.5
    log_decay = -jnp.exp(w)
    a = jax.nn.sigmoid(p['rwkv_a0'] + al @ p['rwkv_a2'])
    g = jax.nn.sigmoid(gl) @ p['rwkv_g2']
    hd = lambda t: t.reshape(B, T, R_HEADS, R_DH)
    kk = l2norm(hd(k * p['rwkv_k_k']))
    k = k * (1.0 + (a - 1.0) * p['rwkv_k_a'])
    rh, kh, vh = hd(r), hd(k), hd(v)
    y, S = rwkv7_recurrence(rh, kh, vh, hd(log_decay), kk, hd(a), S0)
    mu = jnp.mean(y, -1, keepdims=True)
    var = jnp.var(y, -1, keepdims=True)
    y = ((y - mu) * lax.rsqrt(var + GN_EPS) * p['rwkv_ln_w'].reshape(R_HEADS, R_DH)
         + p['rwkv_ln_b'].reshape(R_HEADS, R_DH))
    y = y + jnp.sum(rh * kh * p['rwkv_r_k'], -1, keepdims=True) * vh
    return y.reshape(B, T, W_GRP) * g, S, pr[:, -1]


def gdn_chunkwise(q, k, v, g, beta, S0):
    L = min(CHUNK, q.shape[1])
    strict = jnp.tril(jnp.ones((L, L), dtype=bool), -1)
    incl = jnp.tril(jnp.ones((L, L), dtype=bool))
    eye = jnp.eye(L, dtype=jnp.float32)
    Dv = v.shape[-1]

    def step(S, inp):
        qc, kc, vc, gc, bc = inp
        G = jnp.cumsum(jnp.swapaxes(gc, 1, 2), axis=-1)
        bh = jnp.swapaxes(bc, 1, 2)
        diff = G[..., :, None] - G[..., None, :]
        A = bh[..., None] * jnp.einsum('bthd,bshd->bhts', kc, kc) * jnp.exp(jnp.where(strict, diff, -jnp.inf))
        rhs = jnp.concatenate([bh[..., None] * jnp.swapaxes(vc, 1, 2),
                               (bh * jnp.exp(G))[..., None] * jnp.swapaxes(kc, 1, 2)], axis=-1)
        sol = lax.linalg.triangular_solve(eye + A, rhs, left_side=True, lower=True, unit_diagonal=True)
        U = sol[..., :Dv] - jnp.einsum('bhtd,bhde->bhte', sol[..., Dv:], S)
        qk = jnp.einsum('bthd,bshd->bhts', qc, kc) * jnp.exp(jnp.where(incl, diff, -jnp.inf))
        o = (jnp.exp(G)[..., None] * jnp.einsum('bthd,bhde->bhte', qc, S)
             + jnp.einsum('bhts,bhse->bhte', qk, U))
        GL = G[..., -1]
        S = (jnp.exp(GL)[..., None, None] * S
             + jnp.einsum('bhs,bshd,bhse->bhde', jnp.exp(GL[..., None] - G), kc, U))
        return S, jnp.swapaxes(o, 1, 2)

    xs = tuple(to_chunks(a, L) for a in (q, k, v, g, beta))
    S, o = lax.scan(step, S0, xs)
    return from_chunks(o), S


def gdn_mixer(pg, p, S0, conv_buf):
    B, T, _ = pg.shape
    qkv, z, a_in, b_in = jnp.split(pg, [3 * W_GRP, 4 * W_GRP, 4 * W_GRP + G_HEADS], axis=-1)
    xp = jnp.concatenate([conv_buf, qkv], axis=1)
    cw = p['gdn_conv_w']
    acc = xp[:, 0:T] * cw[0]
    for j in range(1, CONV_W):
        acc = acc + xp[:, j:j + T] * cw[j]
    acc = jax.nn.silu(acc)
    q, k, v = jnp.split(acc, [W_GRP, 2 * W_GRP], axis=-1)
    hd = lambda t: t.reshape(B, T, G_HEADS, G_DH)
    q = l2norm(hd(q)) * (G_DH ** -0.5)
    k = l2norm(hd(k))
    g = -jnp.exp(p['gdn_A_log']) * jax.nn.softplus(a_in + p['gdn_dt_bias'])
    beta = jax.nn.sigmoid(b_in)
    o, S = gdn_chunkwise(q, k, hd(v), g, beta, S0)
    o = o * lax.rsqrt(jnp.mean(o * o, -1, keepdims=True) + EPS) * p['gdn_norm_g']
    o = o * jax.nn.silu(hd(z))
    return o.reshape(B, T, W_GRP), S, xp[:, -(CONV_W - 1):]


def zero_state(b):
    z = lambda *s: jnp.zeros(s, jnp.float32)
    return (z(b, M_HEADS, M_DH, M_DH), z(b, M_HEADS, M_DH), z(b, M_HEADS),
            z(b, S5_GROUPS, S5_P), z(b, S5_GROUPS, S5_P),
            z(b, R_HEADS, R_DH, R_DH), z(b, N_R),
            z(b, G_HEADS, G_DH, G_DH), z(b, CONV_W - 1, 3 * W_GRP))


def trunk_layer(x, p, state):
    f32 = jnp.float32
    mC0, mn0, mm0, s5r0, s5i0, rS0, rsh0, gS0, gcv0 = [s.astype(f32) for s in state]
    B, T, _ = x.shape
    h = rmsnorm(x, p['g_pre_mix'])
    proj = (h @ p['w_in']).astype(f32)
    pm, ps, pr, pg = jnp.split(proj, [N_M, N_M + N_S, N_M + N_S + N_R], axis=-1)
    mq, mk, mv, mo, mi, mf = jnp.split(pm, [W_GRP, 2 * W_GRP, 3 * W_GRP, 4 * W_GRP, 4 * W_GRP + M_HEADS], axis=-1)
    mh = lambda t: t.reshape(B, T, M_HEADS, M_DH)
    gb = p['mlstm_gate_bias'].astype(f32)
    hm, mC, mn, mm = mlstm_chunkwise(mh(mq), mh(mk), mh(mv), mi + gb[0], mf + gb[1], mC0, mn0, mm0)
    hm = (hm - jnp.mean(hm, -1, keepdims=True)) * lax.rsqrt(jnp.var(hm, -1, keepdims=True) + EPS)
    y_m = jax.nn.sigmoid(mo) * (hm.reshape(B, T, W_GRP) * p['mlstm_norm_g'])
    y_s, s5r, s5i = s5_mixer(ps, p, s5r0, s5i0)
    y_r, rS, rsh = rwkv7_mixer(pr, p, rS0, rsh0)
    y_g, gS, gcv = gdn_mixer(pg, p, gS0, gcv0)
    mix = jnp.concatenate([y_m, y_s, y_r, y_g], axis=-1).astype(x.dtype)
    x = x + rmsnorm(mix @ p['w_out'], p['g_post_mix'])
    hf = rmsnorm(x, p['g_pre_ffn'])
    ff = (jax.nn.silu(hf @ p['w_gate']) * (hf @ p['w_up'])) @ p['w_down']
    x = x + rmsnorm(ff, p['g_post_ffn'])
    return x, (mC, mn, mm, s5r, s5i, rS, rsh, gS, gcv)


def setup_inputs(seed: int = 0) -> dict:
    key = jax.random.key(seed)
    ks = iter(jax.random.split(key, 64))
    f32 = jnp.float32
    L = DEPTH

    def nrm(shape, scale=1.0):
        return jax.random.normal(next(ks), shape, f32) * scale

    def unif(shape, lo, hi):
        return jax.random.uniform(next(ks), shape, f32, lo, hi)

    x_prompt = nrm((BATCH, SEQ, D_MODEL))
    x_sample = nrm((DEC_BATCH, DEC_SEQ, D_MODEL))
    state_mlstm_C = nrm((L, DEC_BATCH, M_HEADS, M_DH, M_DH), 0.1)
    state_mlstm_n = nrm((L, DEC_BATCH, M_HEADS, M_DH), 0.1)
    state_mlstm_m = nrm((L, DEC_BATCH, M_HEADS), 0.5)
    state_s5_re = nrm((L, DEC_BATCH, S5_GROUPS, S5_P))
    state_s5_im = nrm((L, DEC_BATCH, S5_GROUPS, S5_P))
    state_rwkv_S = nrm((L, DEC_BATCH, R_HEADS, R_DH, R_DH), 0.5)
    state_rwkv_shift = nrm((L, DEC_BATCH, N_R))
    state_gdn_S = nrm((L, DEC_BATCH, G_HEADS, G_DH, G_DH), 0.1)
    state_gdn_conv = nrm((L, DEC_BATCH, CONV_W - 1, 3 * W_GRP))
    g_pre_mix = 1.0 + nrm((L, D_MODEL), 0.01)
    w_in = nrm((L, D_MODEL, N_IN), D_MODEL ** -0.5)
    mlstm_gate_bias = jnp.stack([-2.0 + nrm((L, M_HEADS), 0.1),
                                 jnp.linspace(3.0, 6.0, M_HEADS) + nrm((L, M_HEADS), 0.1)], axis=1)
    mlstm_norm_g = 1.0 + nrm((L, W_GRP), 0.01)
    s5_lam_re = -0.5 + nrm((L, S5_GROUPS, S5_P), 0.01)
    s5_lam_im = jnp.pi * jnp.arange(S5_P, dtype=f32) + nrm((L, S5_GROUPS, S5_P), 0.01)
    s5_log_dt = unif((L, S5_GROUPS), math.log(1e-3), math.log(1e-1))
    s5_B_re = nrm((L, S5_GROUPS, S5_P, S5_CH), (2 * S5_CH) ** -0.5)
    s5_B_im = nrm((L, S5_GROUPS, S5_P, S5_CH), (2 * S5_CH) ** -0.5)
    s5_C_re = nrm((L, S5_GROUPS, S5_CH, S5_P), (2 * S5_P) ** -0.5)
    s5_C_im = nrm((L, S5_GROUPS, S5_CH, S5_P), (2 * S5_P) ** -0.5)
    s5_D = nrm((L, W_GRP))
    s5_w_glu = nrm((L, W_GRP, W_GRP), W_GRP ** -0.5)
    s5_b_glu = nrm((L, W_GRP), 0.01)
    rwkv_mu = unif((L, N_R), 0.0, 1.0)
    rwkv_w0 = jnp.tile(jnp.linspace(-6.0, 1.0, R_DH), R_HEADS) + nrm((L, W_GRP), 0.1)
    rwkv_w2 = nrm((L, R_LR_W, W_GRP), 0.1 * R_LR_W ** -0.5)
    rwkv_a0 = nrm((L, W_GRP), 0.1)
    rwkv_a2 = nrm((L, R_LR_A, W_GRP), 0.1 * R_LR_A ** -0.5)
    rwkv_g2 = nrm((L, R_LR_G, W_GRP), R_LR_G ** -0.5)
    rwkv_k_k = 0.85 + nrm((L, W_GRP), 0.01)
    rwkv_k_a = 1.0 + nrm((L, W_GRP), 0.01)
    rwkv_r_k = nrm((L, R_HEADS, R_DH), 0.1)
    rwkv_ln_w = 1.0 + nrm((L, W_GRP), 0.01)
    rwkv_ln_b = nrm((L, W_GRP), 0.01)
    gdn_conv_w = nrm((L, CONV_W, 3 * W_GRP), CONV_W ** -0.5)
    gdn_A_log = jnp.log(unif((L, G_HEADS), 1.0, 16.0))
    dt = jnp.exp(unif((L, G_HEADS), math.log(1e-3), math.log(1e-1)))
    gdn_dt_bias = dt + jnp.log(-jnp.expm1(-dt))
    gdn_norm_g = 1.0 + nrm((L, G_DH), 0.01)
    w_out = nrm((L, W_MIX, D_MODEL), W_MIX ** -0.5)
    g_post_mix = 1.0 + nrm((L, D_MODEL), 0.01)
    g_pre_ffn = 1.0 + nrm((L, D_MODEL), 0.01)
    w_gate = nrm((L, D_MODEL, D_FF), D_MODEL ** -0.5)
    w_up = nrm((L, D_MODEL, D_FF), D_MODEL ** -0.5)
    w_down = nrm((L, D_FF, D_MODEL), D_FF ** -0.5)
    g_post_ffn = 1.0 + nrm((L, D_MODEL), 0.01)
    return {
        'x_prompt': x_prompt, 'x_sample': x_sample,
        'state_mlstm_C': state_mlstm_C, 'state_mlstm_n': state_mlstm_n, 'state_mlstm_m': state_mlstm_m,
        'state_s5_re': state_s5_re, 'state_s5_im': state_s5_im,
        'state_rwkv_S': state_rwkv_S, 'state_rwkv_shift': state_rwkv_shift,
        'state_gdn_S': state_gdn_S, 'state_gdn_conv': state_gdn_conv,
        'g_pre_mix': g_pre_mix, 'w_in': w_in, 'mlstm_gate_bias': mlstm_gate_bias, 'mlstm_norm_g': mlstm_norm_g,
        's5_lam_re': s5_lam_re, 's5_lam_im': s5_lam_im, 's5_log_dt': s5_log_dt,
        's5_B_re': s5_B_re, 's5_B_im': s5_B_im, 's5_C_re': s5_C_re, 's5_C_im': s5_C_im,
        's5_D': s5_D, 's5_w_glu': s5_w_glu, 's5_b_glu': s5_b_glu,
        'rwkv_mu': rwkv_mu, 'rwkv_w0': rwkv_w0, 'rwkv_w2': rwkv_w2, 'rwkv_a0': rwkv_a0, 'rwkv_a2': rwkv_a2,
        'rwkv_g2': rwkv_g2, 'rwkv_k_k': rwkv_k_k, 'rwkv_k_a': rwkv_k_a, 'rwkv_r_k': rwkv_r_k,
        'rwkv_ln_w': rwkv_ln_w, 'rwkv_ln_b': rwkv_ln_b,
        'gdn_conv_w': gdn_conv_w, 'gdn_A_log': gdn_A_log, 'gdn_dt_bias': gdn_dt_bias, 'gdn_norm_g': gdn_norm_g,
        'w_out': w_out, 'g_post_mix': g_post_mix, 'g_pre_ffn': g_pre_ffn,
        'w_gate': w_gate, 'w_up': w_up, 'w_down': w_down, 'g_post_ffn': g_post_ffn,
    }


def reference(x_prompt, x_sample, state_mlstm_C, state_mlstm_n, state_mlstm_m, state_s5_re, state_s5_im,
              state_rwkv_S, state_rwkv_shift, state_gdn_S, state_gdn_conv,
              g_pre_mix, w_in, mlstm_gate_bias, mlstm_norm_g,
              s5_lam_re, s5_lam_im, s5_log_dt, s5_B_re, s5_B_im, s5_C_re, s5_C_im, s5_D, s5_w_glu, s5_b_glu,
              rwkv_mu, rwkv_w0, rwkv_w2, rwkv_a0, rwkv_a2, rwkv_g2, rwkv_k_k, rwkv_k_a, rwkv_r_k,
              rwkv_ln_w, rwkv_ln_b, gdn_conv_w, gdn_A_log, gdn_dt_bias, gdn_norm_g,
              w_out, g_post_mix, g_pre_ffn, w_gate, w_up, w_down, g_post_ffn):
    caches = (state_mlstm_C, state_mlstm_n, state_mlstm_m, state_s5_re, state_s5_im,
              state_rwkv_S, state_rwkv_shift, state_gdn_S, state_gdn_conv)
    yp = x_prompt
    ys = x_sample
    outs_p = []
    outs_s = []
    for l in range(DEPTH):
        p = {
            'g_pre_mix': g_pre_mix[l], 'w_in': w_in[l],
            'mlstm_gate_bias': mlstm_gate_bias[l], 'mlstm_norm_g': mlstm_norm_g[l],
            's5_lam_re': s5_lam_re[l], 's5_lam_im': s5_lam_im[l], 's5_log_dt': s5_log_dt[l],
            's5_B_re': s5_B_re[l], 's5_B_im': s5_B_im[l], 's5_C_re': s5_C_re[l], 's5_C_im': s5_C_im[l],
            's5_D': s5_D[l], 's5_w_glu': s5_w_glu[l], 's5_b_glu': s5_b_glu[l],
            'rwkv_mu': rwkv_mu[l], 'rwkv_w0': rwkv_w0[l], 'rwkv_w2': rwkv_w2[l], 'rwkv_a0': rwkv_a0[l],
            'rwkv_a2': rwkv_a2[l], 'rwkv_g2': rwkv_g2[l], 'rwkv_k_k': rwkv_k_k[l], 'rwkv_k_a': rwkv_k_a[l],
            'rwkv_r_k': rwkv_r_k[l], 'rwkv_ln_w': rwkv_ln_w[l], 'rwkv_ln_b': rwkv_ln_b[l],
            'gdn_conv_w': gdn_conv_w[l], 'gdn_A_log': gdn_A_log[l], 'gdn_dt_bias': gdn_dt_bias[l],
            'gdn_norm_g': gdn_norm_g[l],
            'w_out': w_out[l], 'g_post_mix': g_post_mix[l], 'g_pre_ffn': g_pre_ffn[l],
            'w_gate': w_gate[l], 'w_up': w_up[l], 'w_down': w_down[l], 'g_post_ffn': g_post_ffn[l],
        }
        yp, st_p = trunk_layer(yp, p, zero_state(x_prompt.shape[0]))
        ys, st_s = trunk_layer(ys, p, tuple(c[l] for c in caches))
        outs_p.append(st_p)
        outs_s.append(st_s)
    mC_p, mn_p, mm_p, s5r_p, s5i_p, rS_p, rsh_p, gS_p, gcv_p = [jnp.stack(t) for t in zip(*outs_p)]
    mC_s, mn_s, mm_s, s5r_s, s5i_s, rS_s, rsh_s, gS_s, gcv_s = [jnp.stack(t) for t in zip(*outs_s)]
    return (yp, ys,
            mC_p, mn_p, mm_p, s5r_p, s5i_p, rS_p, rsh_p, gS_p, gcv_p,
            mC_s, mn_s, mm_s, s5r_s, s5i_s, rS_s, rsh_s, gS_s, gcv_s)
```

```python
import numpy as np
import ml_dtypes
import concourse.bass as bass
import concourse.mybir as mybir
from concourse.bass_utils import run_bass_kernel_spmd
from contextlib import ExitStack

F32 = mybir.dt.float32
BF16 = mybir.dt.bfloat16
ALU = mybir.AluOpType
AF = mybir.ActivationFunctionType
AX = mybir.AxisListType.X

D = 2048
DFF = 5632
NQ = 1796
NS = 16
DEPTH = 2
EPS = 1e-6
O_MQ, O_MK, O_MV, O_MO, O_MI, O_MF = 0, 128, 256, 384, 512, 513
O_SU = 514
O_R = 642
O_GQ, O_GK, O_GV, O_GZ, O_GA, O_GB = 1282, 1410, 1538, 1666, 1794, 1795
NEG = -1.0e30


class Eng:
    def __init__(self, nc, e, name):
        self.e = e
        self.name = name
        self.sem = nc.alloc_semaphore(name + "_prog")
        self.cnt = 0
        self.seen = {}


class Tile:
    def __init__(self, t):
        self.t = t
        self.w = None
        self.r = {}

    def __getitem__(self, idx):
        return V(self, self.t.ap()[idx] if hasattr(self.t, "ap") else self.t[idx])

    @property
    def a(self):
        return V(self, self.t.ap() if hasattr(self.t, "ap") else self.t[:])


class V:
    def __init__(self, tile, ap):
        self.tile = tile
        self.ap = ap

    def __getitem__(self, idx):
        return V(self.tile, self.ap[idx])

    def re(self, pat, **kw):
        return V(self.tile, self.ap.rearrange(pat, **kw))

    def bc(self, shape):
        return V(self.tile, self.ap.to_broadcast(shape))

    def pbc(self, n):
        return V(self.tile, self.ap.partition_broadcast(n))


class KB:
    def __init__(self, nc):
        self.nc = nc
        self.pe = Eng(nc, nc.tensor, "pe")
        self.act = Eng(nc, nc.scalar, "act")
        self.dve = Eng(nc, nc.vector, "dve")
        self.pool = Eng(nc, nc.gpsimd, "pool")
        self.sp = Eng(nc, nc.sync, "sp")
        self.nds = 40
        self.dsem = [nc.alloc_semaphore("dq%d" % i) for i in range(self.nds)]
        self.dval = [0] * self.nds
        self.dnext = 0
        self.ccsem = nc.alloc_semaphore("ccs")
        self.ccval = 0
        self.uid = 0
        self.psb = []
        self.psi = 0

    def sb(self, shape, dt=F32, name=None):
        self.uid += 1
        nm = "%s_%d" % (name or "t", self.uid)
        if getattr(self, "stack", None) is not None:
            return Tile(self.stack.enter_context(self.nc.sbuf_tensor(nm, list(shape), dt)))
        return Tile(self.nc.alloc_sbuf_tensor(nm, list(shape), dt))

    def dram(self, name, shape, dt=F32, kind=None):
        if kind:
            return Tile(self.nc.dram_tensor(name, list(shape), dt, kind=kind))
        return Tile(self.nc.dram_tensor(name, list(shape), dt))

    def ps(self):
        t = self.psb[self.psi % len(self.psb)]
        self.psi += 1
        return t

    def _wait(self, E, tok):
        sem, val, key = tok
        if E.seen.get(key, 0) >= val:
            return
        E.e.wait_ge(sem, val)
        E.seen[key] = val

    def _deps(self, E, reads, writes, skip_self=False):
        for v in reads:
            t = v.tile
            if t.w is not None and not (skip_self and t.w[2] == E.name):
                self._wait(E, t.w)
        for v in writes:
            t = v.tile
            if t.w is not None and not (skip_self and t.w[2] == E.name):
                self._wait(E, t.w)
            for k, tok in t.r.items():
                if not (skip_self and k == E.name):
                    self._wait(E, tok)

    def _done(self, E, ins, reads, writes):
        E.cnt += 1
        ins.then_inc(E.sem, 1)
        tok = (E.sem, E.cnt, E.name)
        E.seen[E.name] = E.cnt if E is self.pe else E.seen.get(E.name, 0)
        for v in reads:
            v.tile.r[E.name] = tok
        for v in writes:
            v.tile.w = tok
            v.tile.r = {}

    def op(self, E, fn, reads, writes):
        reads = [v for v in reads if isinstance(v, V)]
        self._deps(E, reads, writes, skip_self=(E is self.pe))
        ins = fn()
        self._done(E, ins, reads, writes)

    def dma(self, out, in_, E=None):
        E = E or self.sp
        self._deps(E, [in_], [out])
        i = self.dnext % self.nds
        self.dnext += 1
        if self.dval[i] > 0:
            self._wait(E, (self.dsem[i], self.dval[i], "d%d" % i))
        self.dval[i] += 16
        E.e.dma_start(out=out.ap, in_=in_.ap).then_inc(self.dsem[i], 16)
        tok = (self.dsem[i], self.dval[i], "d%d" % i)
        in_.tile.r["d%d" % i] = tok
        out.tile.w = tok
        out.tile.r = {}

    def allgather(self, out_t, in_t, R):
        E = self.pool
        self._deps(E, [in_t.a], [out_t.a])
        n = in_t.t.ap().shape[0] // R
        for i in range(n):
            self.ccval += 1
            E.e.collective_compute("AllGather", ALU.bypass, replica_groups=[[0, 1, 2, 3], [4, 5, 6, 7]],
                                   ins=[in_t.t.ap()[i * R:(i + 1) * R, :]],
                                   outs=[out_t.t.ap()[i * 4 * R:(i + 1) * 4 * R, :]]).then_inc(self.ccsem)
        tok = (self.ccsem, self.ccval, "cc")
        in_t.r["cc"] = tok
        out_t.w = tok
        out_t.r = {}

    def mm(self, out, lhsT, rhs, start=True, stop=True):
        self.op(self.pe, lambda: self.nc.tensor.matmul(out.ap, lhsT.ap, rhs.ap, start=start, stop=stop),
                [lhsT, rhs], [out])

    def tr(self, out, in_, ident):
        self.op(self.pe, lambda: self.nc.tensor.transpose(out.ap, in_.ap, ident.ap), [in_, ident], [out])

    def actf(self, out, in_, func, bias=None, scale=1.0, accum=None):
        kw = {}
        if bias is not None:
            kw["bias"] = bias.ap if isinstance(bias, V) else bias
        if isinstance(scale, V):
            kw["scale"] = scale.ap
        else:
            kw["scale"] = scale
        if accum is not None:
            kw["accum_out"] = accum.ap
        wr = [out] + ([accum] if accum is not None else [])
        self.op(self.act, lambda: self.nc.scalar.activation(out.ap, in_.ap, func, **kw),
                [in_, bias, scale], wr)

    def _sv(self, s):
        return s.ap if isinstance(s, V) else s

    def tt(self, out, a, b, op, E=None):
        E = E or self.dve
        self.op(E, lambda: E.e.tensor_tensor(out.ap, a.ap, b.ap, op), [a, b], [out])

    def ts(self, out, a, s1, s2, op0, op1=None, E=None):
        E = E or self.dve
        if op1 is None:
            self.op(E, lambda: E.e.tensor_scalar(out.ap, a.ap, self._sv(s1), None, op0), [a, s1], [out])
        else:
            self.op(E, lambda: E.e.tensor_scalar(out.ap, a.ap, self._sv(s1), self._sv(s2), op0, op1),
                    [a, s1, s2], [out])

    def stt(self, out, a, s, b, op0, op1):
        self.op(self.dve, lambda: self.nc.vector.scalar_tensor_tensor(out.ap, a.ap, self._sv(s), b.ap, op0, op1),
                [a, s, b], [out])

    def ttr(self, out, a, b, op0, op1, init, accum):
        if op0 == ALU.mult and a is b:
            self.actf(out, a, AF.Square, accum=accum)
        else:
            self.tt(out, a, b, op0)
            self.red(accum, out, op1)

    def red(self, out, a, op):
        self.op(self.dve, lambda: self.nc.vector.tensor_reduce(out.ap, a.ap, AX, op), [a], [out])

    def scan(self, out, d0, d1, init):
        self.op(self.dve, lambda: self.nc.vector.tensor_tensor_scan(out.ap, d0.ap, d1.ap, self._sv(init), ALU.mult, ALU.add),
                [d0, d1, init], [out])

    def recip(self, out, a):
        self.op(self.dve, lambda: self.nc.vector.reciprocal(out.ap, a.ap), [a], [out])

    def cp(self, out, a, E=None):
        E = E or self.dve
        if E is self.act:
            self.op(E, lambda: self.nc.scalar.copy(out.ap, a.ap), [a], [out])
        else:
            self.op(E, lambda: E.e.tensor_copy(out.ap, a.ap), [a], [out])

    def memset(self, out, val, E=None):
        E = E or self.dve
        self.op(E, lambda: E.e.memset(out.ap, val), [], [out])

    def rsqrt(self, out, ssq, scale, eps):
        self.ts(out, ssq, scale, eps, ALU.mult, ALU.add)
        self.actf(out, out, AF.Sqrt)
        self.recip(out, out)

    def barrier(self):
        engs = (self.pe, self.act, self.dve, self.pool, self.sp)
        snap = [(X.sem, X.cnt, X.name) for X in engs if X.cnt > 0]
        dsn = [(self.dsem[i], self.dval[i], "d%d" % i) for i in range(self.nds) if self.dval[i] > 0]
        cc = [(self.ccsem, self.ccval, "cc")] if self.ccval > 0 else []
        for E in engs:
            for tok in snap + dsn + cc:
                if tok[2] != E.name:
                    self._wait(E, tok)

    def final_wait(self):
        E = self.sp
        for i in range(self.nds):
            if self.dval[i] > 0:
                self._wait(E, (self.dsem[i], self.dval[i], "d%d" % i))
        for X in (self.pe, self.act, self.dve, self.pool):
            if X.cnt > 0:
                self._wait(E, (X.sem, X.cnt, X.name))


def _consts():
    c = {}
    i = np.arange(128)
    same = (i[:, None] // 64) == (i[None, :] // 64)
    c["ident"] = np.eye(128, dtype=np.float32)
    c["ones"] = np.ones((128, 128), np.float32)
    c["triBD"] = (same & (i[:, None] <= i[None, :])).astype(np.float32)
    c["sel0"] = np.repeat((i[:, None] < 64), 128, 1).astype(np.float32)
    c["sel1"] = np.repeat((i[:, None] >= 64), 128, 1).astype(np.float32)
    c["mneg_ts_incl"] = np.where(same & (i[None, :] <= i[:, None]), 0.0, NEG).astype(np.float32)
    c["mpos_st_incl"] = np.where(same & (i[:, None] <= i[None, :]), 0.0, -NEG).astype(np.float32)
    c["mneg_st_incl"] = np.where(same & (i[:, None] <= i[None, :]), 0.0, NEG).astype(np.float32)
    c["mneg_st_strict"] = np.where(same & (i[:, None] < i[None, :]), 0.0, NEG).astype(np.float32)
    c["mpos_ts_strict"] = np.where(same & (i[None, :] < i[:, None]), 0.0, -NEG).astype(np.float32)
    c["m01_st_strict"] = (same & (i[:, None] < i[None, :])).astype(np.float32)
    c["m01_st_incl"] = (same & (i[:, None] <= i[None, :])).astype(np.float32)
    c["m01_ts_strict"] = (same & (i[None, :] < i[:, None])).astype(np.float32)
    for d in (1, 2, 3):
        sh = (i[:, None] == i[None, :] - d)
        c["shP%d" % d] = sh.astype(np.float32)
        c["shS%d" % d] = (sh & same).astype(np.float32)
    hs = np.zeros((6, 3, 128), np.float32)
    for d in (1, 2, 3):
        for t in range(d):
            hs[3 + t - d, d - 1, t] = 1.0
            hs[3 + 3 + t - d, d - 1, 64 + t] = 1.0
    c["hsel"] = hs.reshape(6, 384)
    tp = np.zeros((128, 32), np.float32)
    ts_ = np.zeros((128, 32), np.float32)
    for r in range(3):
        tp[125 + r, r] = 1.0
        ts_[61 + r, r] = 1.0
        ts_[125 + r, 3 + r] = 1.0
    c["tailP"] = tp
    c["tailS"] = ts_
    return c


CONST_ORDER = ["ident", "ones", "triBD", "sel0", "sel1", "mneg_ts_incl", "mpos_st_incl", "mneg_st_incl",
               "mneg_st_strict", "mpos_ts_strict", "m01_st_strict", "m01_st_incl", "m01_ts_strict",
               "shP1", "shP2", "shP3", "shS1", "shS2", "shS3"]


def build(SEQ, stop=None, skip=()):
    PT = SEQ // 4
    SL = PT + 256
    NT_P = PT // 128
    NT_S = NT_P + 2
    GT = 4 * SL
    nc = bass.Bass("TRN2", target_bir_lowering=False)
    k = KB(nc)
    ein = lambda n, s, dt=F32: k.dram(n, s, dt, kind="ExternalInput")
    eout = lambda n, s: k.dram(n, s, F32, kind="ExternalOutput")
    xg = ein("xg", [GT, D])
    xo = ein("xo", [SL, D])
    qmask = ein("qmask", [128, 4])
    cmat = ein("cmat", [len(CONST_ORDER), 128, 128])
    hsel_d = ein("hsel", [6, 384])
    tail_d = ein("tails", [2, 128, 32])
    w_in_d = ein("w_in", [DEPTH, D, NQ])
    w_out_d = ein("w_out", [DEPTH, D, D])
    w_gate_d = ein("w_gate", [DEPTH, D, DFF])
    w_up_d = ein("w_up", [DEPTH, D, DFF])
    w_down_d = ein("w_down", [DEPTH, DFF, D])
    w_glu_d = ein("w_glu", [DEPTH, 512, 512])
    gvec = ein("gvec", [DEPTH, 5, D])
    gcols = ein("gcols", [DEPTH, 2, 128, 16])
    pcol = ein("pcol", [DEPTH, 128, 32])
    prow = ein("prow", [DEPTH, 1, 3328])
    s5b = ein("s5b", [DEPTH, 2, 128, 4, 128])
    s5c = ein("s5c", [DEPTH, 2, 128, 4, 128])
    rw2 = ein("rw2", [DEPTH, 128, 128])
    rg2 = ein("rg2", [DEPTH, 128, 128])
    i_mCn = ein("i_mCn", [DEPTH, NS, 128, 129])
    i_mm = ein("i_mm", [DEPTH, NS, 128, 1])
    i_s5 = ein("i_s5", [DEPTH, NS, 128, 8])
    i_rS = ein("i_rS", [DEPTH, NS, 128, 64])
    i_rsh = ein("i_rsh", [DEPTH, NS, 640])
    i_gS = ein("i_gS", [DEPTH, NS, 128, 128])
    i_gcv = ein("i_gcv", [DEPTH, NS, 3, 384])
    y_own = eout("y_own", [SL, D])
    o_mCn = eout("o_mCn", [DEPTH, NS + 1, 128, 129])
    o_mm = eout("o_mm", [DEPTH, NS + 1, 128, 1])
    o_s5 = eout("o_s5", [DEPTH, NS + 1, 128, 8])
    o_rS = eout("o_rS", [DEPTH, NS + 1, 128, 64])
    o_rsh = eout("o_rsh", [DEPTH, NS + 1, 640])
    o_gS = eout("o_gS", [DEPTH, NS + 1, 128, 128])
    o_gcv = eout("o_gcv", [DEPTH, NS + 1, 3, 384])
    mixp = [k.dram("mixp%d" % l, [GT, 512]) for l in range(DEPTH)]
    mixa = [k.dram("mixa%d" % l, [4 * GT, 512]) for l in range(DEPTH)]
    x1own = k.dram("x1own", [SL, D])
    x1all = k.dram("x1all", [GT, D])
    xmid = k.dram("xmid", [SL, D])
    wo_s = [k.dram("wo_s%d" % l, [4, 128, 16, 512], BF16) for l in range(DEPTH)]
    wg_s = [k.dram("wg_s%d" % l, [44, 128, 16, 128], BF16) for l in range(DEPTH)]
    wu_s = [k.dram("wu_s%d" % l, [44, 128, 16, 128], BF16) for l in range(DEPTH)]
    wd_s = [k.dram("wd_s%d" % l, [4, 128, 44, 512], BF16) for l in range(DEPTH)]

    for i in range(7):
        k.psb.append(Tile(nc.alloc_psum_tensor("psb%d" % i, [128, 512], F32)))
    pst = Tile(nc.alloc_psum_tensor("pst", [128, 1024], BF16))

    C = {}
    cs = k.sb([128, len(CONST_ORDER), 128], name="consts")
    k.dma(cs.a, cmat.a.re("n p f -> p n f"))
    for i, n in enumerate(CONST_ORDER):
        C[n] = cs[:, i, :]
    ident = C["ident"]
    identb_t = k.sb([128, 128], BF16, "identb")
    k.cp(identb_t.a, ident)
    identb = identb_t.a
    hsel = k.sb([6, 384], name="hsel")
    k.dma(hsel.a, hsel_d.a)
    tails = k.sb([128, 2, 32], name="tails")
    k.dma(tails.a, tail_d.a.re("n p f -> p n f"))
    qm = k.sb([128, 4], name="qm")
    k.dma(qm.a, qmask.a)
    onescol = C["ones"][:, 0:1]

    xt = [k.sb([128, D], name="xt%d" % i) for i in range(2)]
    hb = k.sb([128, D], BF16, "hb")
    stg = [xt[0], xt[1]]
    stb = [hb, k.sb([128, 2048], BF16, "stb0")]
    junk = stb[1]
    cast_engs = [k.dve, k.act]
    cnt = [0]

    def castcopy(dst_v_fn, src_v, cw=2048):
        i = cnt[0] % 2
        cnt[0] += 1
        k.dma(stg[i][:, 0:cw], src_v)
        k.cp(stb[i][:, 0:cw], stg[i][:, 0:cw], E=cast_engs[i])
        dst_v_fn(stb[i])

    for l in range(DEPTH if "prep" not in skip else 0):
        for kc in range(16):
            castcopy(lambda sb_, kc=kc, l=l: k.dma(wo_s[l].a.re("n p c f -> p n c f")[:, :, kc, :],
                                                   sb_.a.re("p (n f) -> p n f", n=4)),
                     w_out_d[l, kc * 128:(kc + 1) * 128, :])
        for (src, dst) in ((w_gate_d, wg_s), (w_up_d, wu_s)):
            for kc in range(16):
                for cb in range(3):
                    c0 = cb * 2048
                    cw = min(2048, DFF - c0)
                    nf = cw // 128
                    def wr_(sb_, kc=kc, l=l, c0=c0, cw=cw, nf=nf, dst=dst):
                        for q0 in range(0, nf, 4):
                            q1 = min(nf, q0 + 4)
                            k.dma(dst[l].a.re("n p c f -> p n c f")[:, c0 // 128 + q0:c0 // 128 + q1, kc, :],
                                  sb_[:, q0 * 128:q1 * 128].re("p (n f) -> p n f", f=128))
                    castcopy(wr_,
                             src[l, kc * 128:(kc + 1) * 128, c0:c0 + cw], cw)
        for fc in range(44):
            castcopy(lambda sb_, fc=fc, l=l: k.dma(wd_s[l].a.re("n p c f -> p n c f")[:, :, fc, :],
                                                   sb_.a.re("p (n f) -> p n f", n=4)),
                     w_down_d[l, fc * 128:(fc + 1) * 128, :])

    if stop == "prep":
        k.final_wait()
        return nc
    col = lambda n="c": k.sb([128, 8], name=n)

    def T(shape=(128, 128), n="tmp"):
        return k.sb(list(shape), name=n)

    def tmp():
        if tpi[0] >= len(tmp_pool):
            tmp_pool.append(T(n="tp%d" % len(tmp_pool)))
        t = tmp_pool[tpi[0]]
        tpi[0] += 1
        return t

    def ntmp():
        t = npool[npi[0] % 8]
        npi[0] += 1
        return t

    def big():
        t = big_pool[bpi[0] % len(big_pool)]
        bpi[0] += 1
        return t

    def cols():
        t = col_pool[cpi[0] % len(col_pool)]
        cpi[0] += 1
        return t

    tmp_pool = []
    tpi = [0]
    npi = [0]
    bpi = [0]
    col_pool = [col("cp%d" % i) for i in range(24)]
    cpi = [0]

    def layer_setup(l):
        tpi[0] = 0
        k.dma(pc.a, pcol[l])
        k.dma(prb.a, prow[l, 0, :].pbc(128))
        k.dma(gpre.a, gcols[l, 0])
        k.dma(w2a2.a, rw2[l])
        k.dma(g2.a, rg2[l])
        for kc in range(16):
            i = cnt[0] % 2
            cnt[0] += 1
            k.dma(stg[i][:, 0:NQ], w_in_d[l, kc * 128:(kc + 1) * 128, :])
            k.cp(win[:, kc, :], stg[i][:, 0:NQ], E=cast_engs[i])
        c = cols()
        dt = c[:, 0:4]
        k.actf(dt, pc[:, 8:12], AF.Exp)
        c2 = cols()
        mag = c2[:, 0:4]
        k.tt(mag, pc[:, 0:4], dt, ALU.mult)
        k.actf(mag, mag, AF.Exp)
        th = c2[:, 4:8]
        k.tt(th, pc[:, 4:8], dt, ALU.mult)
        c3 = cols()
        sn, cn = c3[:, 0:4], c3[:, 4:8]
        k.ts(sn, th, 1.0 / 32, None, ALU.mult)
        k.ts(cn, th, 1.0 / 32, float(0.5 * np.pi), ALU.mult, ALU.add)
        k.actf(sn, sn, AF.Sin)
        k.actf(cn, cn, AF.Sin)
        for _ in range(5):
            dd_ = cols()
            k.tt(dd_[:, 0:4], cn, cn, ALU.mult)
            k.tt(dd_[:, 4:8], sn, sn, ALU.mult)
            k.tt(sn, sn, cn, ALU.mult)
            k.ts(sn, sn, 2.0, None, ALU.mult)
            k.tt(cn, dd_[:, 0:4], dd_[:, 4:8], ALU.subtract)
        lbr, lbi = s5_lb[:, 0, :], s5_lb[:, 1, :]
        k.tt(lbr, mag, cn, ALU.mult)
        k.tt(lbi, mag, sn, ALU.mult)
        c4 = cols()
        nr, den = c4[:, 0:4], c4[:, 4:8]
        k.ts(nr, lbr, -1.0, None, ALU.add)
        c5 = cols()
        t1, t2 = c5[:, 0:4], c5[:, 4:8]
        k.tt(t1, pc[:, 0:4], pc[:, 0:4], ALU.mult)
        k.tt(t2, pc[:, 4:8], pc[:, 4:8], ALU.mult)
        k.tt(den, t1, t2, ALU.add)
        k.recip(den, den)
        c6 = cols()
        fre, fim = c6[:, 0:4], c6[:, 4:8]
        k.tt(t1, nr, pc[:, 0:4], ALU.mult)
        k.tt(t2, lbi, pc[:, 4:8], ALU.mult)
        k.tt(fre, t1, t2, ALU.add)
        k.tt(fre, fre, den, ALU.mult)
        k.tt(t1, lbi, pc[:, 0:4], ALU.mult)
        k.tt(t2, nr, pc[:, 4:8], ALU.mult)
        k.tt(fim, t1, t2, ALU.subtract)
        k.tt(fim, fim, den, ALU.mult)
        braw = big()
        biraw = big()
        k.dma(braw.a.re("p (j f) -> p j f", j=4), s5b[l, 0])
        k.dma(biraw.a.re("p (j f) -> p j f", j=4), s5b[l, 1])
        for j in range(4):
            bre = ntmp()
            bim = ntmp()
            t3 = ntmp()
            k.ts(bre.a, braw[:, j * 128:(j + 1) * 128], fre[:, j:j + 1], None, ALU.mult)
            k.ts(t3.a, biraw[:, j * 128:(j + 1) * 128], fim[:, j:j + 1], None, ALU.mult)
            k.tt(bre.a, bre.a, t3.a, ALU.subtract)
            k.ts(bim.a, biraw[:, j * 128:(j + 1) * 128], fre[:, j:j + 1], None, ALU.mult)
            k.ts(t3.a, braw[:, j * 128:(j + 1) * 128], fim[:, j:j + 1], None, ALU.mult)
            k.tt(bim.a, bim.a, t3.a, ALU.add)
            for ri, src in ((0, bre), (1, bim)):
                p = k.ps()
                k.tr(p[:, 0:128], src.a, ident)
                k.cp(s5_Bb[:, ri, j, :], p[:, 0:128])
        k.dma(s5_Cb[:, 0, :, :], s5c[l, 0])
        k.dma(s5_Cb[:, 1, :, :], s5c[l, 1])
        k.ts(s5_Cb[:, 1, :, :], s5_Cb[:, 1, :, :], -1.0, None, ALU.mult)
        linv = cols()
        lir, lii = linv[:, 0:4], linv[:, 4:8]
        k.tt(t1, lbr, lbr, ALU.mult)
        k.tt(t2, lbi, lbi, ALU.mult)
        k.tt(t1, t1, t2, ALU.add)
        k.recip(t1, t1)
        k.tt(lir, lbr, t1, ALU.mult)
        k.tt(lii, lbi, t1, ALU.mult)
        k.ts(lii, lii, -1.0, None, ALU.mult)
        for (tab, br0, bi0) in ((s5_P, lbr, lbi), (s5_Pi, lir, lii)):
            pw = cols()
            pr_, pi_ = pw[:, 0:4], pw[:, 4:8]
            k.cp(pr_, br0)
            k.cp(pi_, bi0)
            k.memset(tab[:, 0, :, 0:1], 1.0)
            k.memset(tab[:, 1, :, 0:1], 0.0)
            n = 1
            while n < 64:
                for j in range(4):
                    a_r, a_i = tab[:, 0, j, 0:n], tab[:, 1, j, 0:n]
                    o_r, o_i = tab[:, 0, j, n:2 * n], tab[:, 1, j, n:2 * n]
                    tq = ntmp()
                    k.ts(tq[:, 0:n], a_i, pi_[:, j:j + 1], None, ALU.mult)
                    k.stt(o_r, a_r, pr_[:, j:j + 1], tq[:, 0:n], ALU.mult, ALU.subtract)
                    k.ts(tq[:, 0:n], a_i, pr_[:, j:j + 1], None, ALU.mult)
                    k.stt(o_i, a_r, pi_[:, j:j + 1], tq[:, 0:n], ALU.mult, ALU.add)
                sq = cols()
                k.tt(sq[:, 0:4], pr_, pr_, ALU.mult)
                k.tt(sq[:, 4:8], pi_, pi_, ALU.mult)
                nr2 = cols()
                k.tt(nr2[:, 0:4], sq[:, 0:4], sq[:, 4:8], ALU.subtract)
                k.tt(nr2[:, 4:8], pr_, pi_, ALU.mult)
                k.ts(pi_, nr2[:, 4:8], 2.0, None, ALU.mult)
                k.cp(pr_, nr2[:, 0:4])
                n *= 2
            k.cp(tab[:, :, :, 64:128], tab[:, :, :, 0:64])
        for s_ in (st_mC, st_mm, st_s5, st_rS, st_rsh, st_gS, st_gcv):
            k.memset(s_.a, 0.0)

    def halves():
        return ((0, slice(0, 64)), (1, slice(64, 128)))

    def mixer_tile(l, P, mix, S, sample):
        tpi[0] = 0
        PRI = {0: 0, 2: 1, 3: 2, 4: 3, 5: 4, 6: 5, 7: 6, 8: 7, 13: 8}
        PR = lambda r, w=128: prb[:, PRI[r] * 128:PRI[r] * 128 + w]
        sel = (C["sel0"], C["sel1"])

        c = cols()
        li, lf, bcs, acol = c[:, 0:1], c[:, 1:2], c[:, 2:3], c[:, 3:4]
        k.ts(li, P[:, O_MI:O_MI + 1], pc[:, 13:14], None, ALU.add)
        e = c[:, 4:5]
        k.ts(e, P[:, O_MF:O_MF + 1], pc[:, 14:15], -1.0, ALU.add, ALU.mult)
        k.actf(e, e, AF.Exp)
        k.actf(e, e, AF.Ln, bias=onescol)
        k.ts(lf, e, -1.0, None, ALU.mult)
        p = k.ps()
        k.mm(p[:, 0:1], C["triBD"], lf)
        k.tt(acol, li, p[:, 0:1], ALU.subtract)
        k.cp(bcs, p[:, 0:1])
        da = tmp()
        k.ts(da.a, ident, acol, None, ALU.mult)
        pA = k.ps()
        k.mm(pA[:, 0:128], C["ones"], da.a)
        cm = c[:, 5:6]
        cmL2 = c[:, 6:8]
        k.red(cmL2, pA[:, 0:128].re("p (h s) -> p h s", h=2), ALU.max)
        jk = tmp()
        k.ttr(jk.a, pA[:, 0:128], C["mneg_ts_incl"], ALU.add, ALU.max, NEG, cm)
        dcm = tmp()
        k.ts(dcm.a, ident, cm, None, ALU.mult)
        pB = k.ps()
        k.mm(pB[:, 0:128], C["ones"], dcm.a)
        z = tmp()
        k.stt(z.a, pB[:, 0:128], acol, C["mpos_st_incl"], ALU.subtract, ALU.max)
        DT = tmp()
        k.actf(DT.a, z.a, AF.Exp, scale=-1.0)
        qT, kT = tmp(), tmp()
        for (dst, off) in ((qT, O_MQ), (kT, O_MK)):
            pp = k.ps()
            k.tr(pp[:, 0:128], P[:, off:off + 128], ident)
            k.cp(dst.a, pp[:, 0:128], E=k.act)
        k.ts(qT.a, qT.a, 128 ** -0.5, None, ALU.mult)
        pkq = k.ps()
        k.mm(pkq[:, 0:128], kT.a, qT.a)
        SmT = tmp()
        k.tt(SmT.a, pkq[:, 0:128], DT.a, ALU.mult)
        k.cp(vaug[:, 0:128], P[:, O_MV:O_MV + 128], E=k.pool)
        k.memset(vaug[:, 128:129], 1.0, E=k.pool)
        pin = k.ps()
        k.mm(pin[:, 0:129], SmT.a, vaug.a)
        k.cp(intra.a, pin[:, 0:129], E=k.act)
        hnum = tmp()
        for hf, rs in halves():
            Cin, Cout = S[hf]["mC"]
            min_, mout = S[hf]["mm"]
            cc = cols()
            cmL, bL, ML, Mt, al, om = cc[:, 0:1], cc[:, 1:2], cc[:, 2:3], cc[:, 3:4], cc[:, 4:5], cc[:, 5:6]
            k.cp(cmL, cmL2[:, hf:hf + 1])
            pb = k.ps()
            k.mm(pb[:, 0:1], sel[hf], lf)
            k.cp(bL, pb[:, 0:1])
            k.tt(ML, cmL, min_.a, ALU.max)
            k.tt(Mt[rs], cm[rs], min_[rs, :], ALU.max)
            k.tt(al[rs], cm[rs], Mt[rs], ALU.subtract)
            k.actf(al[rs], al[rs], AF.Exp)
            k.tt(om[rs], min_[rs, :], Mt[rs], ALU.subtract)
            k.actf(om[rs], om[rs], AF.Exp)
            pq = k.ps()
            k.mm(pq[:, 0:129], qT.a, Cin.a)
            t1 = mt1
            k.ts(t1[rs, :], pq[rs, 0:129], om[rs], None, ALU.mult)
            k.stt(t1[rs, :], intra[rs, :], al[rs], t1[rs, :], ALU.mult, ALU.add)
            dd = cols()
            dn, ex = dd[:, 0:1], dd[:, 1:2]
            k.ts(dn[rs], t1[rs, 128:129], -1.0, None, ALU.mult)
            k.tt(dn[rs], dn[rs], t1[rs, 128:129], ALU.max)
            k.tt(ex[rs], bcs[rs], Mt[rs], ALU.add)
            k.actf(ex[rs], ex[rs], AF.Exp, scale=-1.0)
            k.tt(dn[rs], dn[rs], ex[rs], ALU.max)
            k.recip(dn[rs], dn[rs])
            k.ts(hnum[rs, :], t1[rs, 0:128], dn[rs], None, ALU.mult)
            wk, sc = dd[:, 2:3], dd[:, 3:4]
            k.tt(wk[rs], acol[rs], ML[rs], ALU.subtract)
            k.actf(wk[rs], wk[rs], AF.Exp)
            k.tt(sc, min_.a, ML, ALU.subtract)
            k.actf(sc, sc, AF.Exp)
            kw = tmp()
            k.ts(kw[rs, :], P[rs, O_MK:O_MK + 128], wk[rs], None, ALU.mult)
            pd = k.ps()
            k.mm(pd[:, 0:129], kw[rs, :], vaug[rs, :])
            k.stt(Cout.a, Cin.a, sc, pd[:, 0:129], ALU.mult, ALU.add)
            k.tt(mout.a, bL, ML, ALU.add)
        cst = cols()
        mu, var = cst[:, 0:1], cst[:, 1:2]
        k.red(mu, hnum.a, ALU.add)
        k.ts(mu, mu, -1.0 / 128, None, ALU.mult)
        hc = tmp()
        k.ts(hc.a, hnum.a, mu, None, ALU.add)
        sqj = tmp()
        k.actf(sqj.a, hc.a, AF.Square, accum=var)
        k.rsqrt(var, var, 1.0 / 128, EPS)
        sg = tmp()
        k.actf(sg.a, P[:, O_MO:O_MO + 128], AF.Sigmoid)
        k.stt(hc.a, hc.a, var, PR(0), ALU.mult, ALU.mult)
        k.tt(mix[:, 0:128], hc.a, sg.a, ALU.mult)

        if "m1" in skip:
            return
        tpi[0] = 0
        pu = k.ps()
        k.tr(pu[:, 0:128], P[:, O_SU:O_SU + 128], ident)
        uT = tmp()
        k.cp(uT.a, pu[:, 0:128], E=k.act)
        BU = [big(), big()]
        for ri in range(2):
            pp = k.ps()
            for j in range(4):
                k.mm(pp[:, j * 128:(j + 1) * 128], s5_Bb[:, ri, j, :], uT.a)
            k.cp(BU[ri].a, pp.a, E=k.act)
        Xr, Xi, tb = big(), big(), big()
        Pir = s5_Pi[:, 0, :, :].re("p j t -> p (j t)")
        Pii = s5_Pi[:, 1, :, :].re("p j t -> p (j t)")
        k.tt(Xr.a, BU[0].a, Pir, ALU.mult)
        k.tt(tb.a, BU[1].a, Pii, ALU.mult, E=k.pool)
        k.tt(Xr.a, Xr.a, tb.a, ALU.subtract)
        k.tt(Xi.a, BU[0].a, Pii, ALU.mult)
        k.tt(tb.a, BU[1].a, Pir, ALU.mult, E=k.pool)
        k.tt(Xi.a, Xi.a, tb.a, ALU.add)
        Gr, Gi = big(), big()
        Hr, Hi = BU[0], BU[1]
        Ptr = s5_P[:, 0, :, :]
        Pti = s5_P[:, 1, :, :]
        for hf, rs in halves():
            sin_, sout = S[hf]["s5"]
            ts_ = slice(hf * 64, (hf + 1) * 64)
            ci = cols()
            ir, ii_, t1_, t2_ = ci[:, 0:4], ci[:, 4:8], cols(), cols()
            k.tt(t1_[:, 0:4], sin_[:, 0, :], s5_lb[:, 0, :], ALU.mult)
            k.tt(t2_[:, 0:4], sin_[:, 1, :], s5_lb[:, 1, :], ALU.mult)
            k.tt(ir, t1_[:, 0:4], t2_[:, 0:4], ALU.subtract)
            k.tt(t1_[:, 4:8], sin_[:, 0, :], s5_lb[:, 1, :], ALU.mult)
            k.tt(t2_[:, 4:8], sin_[:, 1, :], s5_lb[:, 0, :], ALU.mult)
            k.tt(ii_, t1_[:, 4:8], t2_[:, 4:8], ALU.add)
            for j in range(4):
                fs = slice(j * 128 + hf * 64, j * 128 + hf * 64 + 64)
                k.scan(Gr[:, fs], C["ones"][:, 0:64], Xr[:, fs], ir[:, j:j + 1])
                k.scan(Gi[:, fs], C["ones"][:, 0:64], Xi[:, fs], ii_[:, j:j + 1])
            G3r = Gr.a.re("p (j t) -> p j t", j=4)[:, :, ts_]
            G3i = Gi.a.re("p (j t) -> p j t", j=4)[:, :, ts_]
            H3r = Hr.a.re("p (j t) -> p j t", j=4)[:, :, ts_]
            H3i = Hi.a.re("p (j t) -> p j t", j=4)[:, :, ts_]
            T3 = tb.a.re("p (j t) -> p j t", j=4)[:, :, ts_]
            k.tt(H3r, G3r, Ptr[:, :, ts_], ALU.mult)
            k.tt(T3, G3i, Pti[:, :, ts_], ALU.mult)
            k.tt(H3r, H3r, T3, ALU.subtract)
            k.tt(H3i, G3r, Pti[:, :, ts_], ALU.mult)
            k.tt(T3, G3i, Ptr[:, :, ts_], ALU.mult)
            k.tt(H3i, H3i, T3, ALU.add)
            last = hf * 64 + 63
            k.cp(sout[:, 0, :], Hr.a.re("p (j t) -> p j t", j=4)[:, :, last])
            k.cp(sout[:, 1, :], Hi.a.re("p (j t) -> p j t", j=4)[:, :, last])
        py = k.ps()
        n_ = 0
        for ri, Hh in ((0, Hr), (1, Hi)):
            for j in range(4):
                k.mm(py[:, 0:128], s5_Cb[:, ri, j, :], Hh[:, j * 128:(j + 1) * 128], start=(n_ == 0), stop=(n_ == 7))
                n_ += 1
        yT = tmp()
        k.stt(yT.a, uT.a, pc[:, 12:13], py[:, 0:128], ALU.mult, ALU.add)
        g1 = tmp()
        k.tt(g1.a, yT.a, yT.a, ALU.mult)
        k.ts(g1.a, g1.a, 0.044715, 1.0, ALU.mult, ALU.add)
        k.tt(g1.a, g1.a, yT.a, ALU.mult)
        k.actf(g1.a, g1.a, AF.Sigmoid, scale=1.5957691216057308)
        k.tt(yT.a, yT.a, g1.a, ALU.mult)
        pyt = k.ps()
        k.tr(pyt[:, 0:128], yT.a, ident)
        k.cp(mix[:, 128:256], pyt[:, 0:128], E=k.act)

        if "m2" in skip:
            return
        def shifted(pso, src_v, width, d, halo_v, hrows):
            sh = C[("shS%d" if sample else "shP%d") % d]
            k.mm(pso, sh, src_v, start=True, stop=False)
            k.mm(pso, hsel[0:hrows, (d - 1) * 128:d * 128] if hrows == 6 else hsel1[0:2, :], halo_v, start=False, stop=True)

        tpi[0] = 0
        R0 = O_R
        halo_in = S[0]["rsh_in"]
        for (c0, c1) in ((0, 512), (512, 640)):
            pp = k.ps()
            shifted(pp[:, 0:c1 - c0], P[:, R0 + c0:R0 + c1], c1 - c0, 1, halo_in[0:2, c0:c1], 2)
            k.tt(sh1[:, c0:c1], pp[:, 0:c1 - c0], P[:, R0 + c0:R0 + c1], ALU.subtract)
        k.tt(sh1.a, sh1.a, prb[:, 1152:1792], ALU.mult)
        k.tt(sh1.a, sh1.a, P[:, R0:R0 + 640], ALU.add)
        xr, xk, xv = sh1[:, 0:128], sh1[:, 128:256], sh1[:, 256:384]
        pl1 = k.ps()
        k.tr(pl1[:, 0:128], sh1[:, 384:512], ident)
        l1 = tmp()
        k.actf(l1[0:64, :], pl1[0:64, 0:128], AF.Tanh)
        k.cp(l1[64:128, :], pl1[64:128, 0:128], E=k.act)
        pl2 = k.ps()
        k.tr(pl2[:, 0:128], sh1[:, 512:640], ident)
        l2 = tmp()
        k.actf(l2.a, pl2[:, 0:128], AF.Sigmoid)
        pw = k.ps()
        k.mm(pw[:, 0:128], l1[0:64, :], w2a2[0:64, :])
        pa = k.ps()
        k.mm(pa[:, 0:128], l1[64:128, :], w2a2[64:128, :])
        pg = k.ps()
        k.mm(pg[:, 0:128], l2.a, g2.a)
        ld = tmp()
        k.tt(ld.a, pw[:, 0:128], PR(2), ALU.add)
        k.actf(ld.a, ld.a, AF.Sigmoid)
        k.ts(ld.a, ld.a, -float(np.exp(-0.5)), None, ALU.mult)
        av = tmp()
        k.tt(av.a, pa[:, 0:128], PR(3), ALU.add)
        k.actf(av.a, av.a, AF.Sigmoid)
        gg = tmp()
        k.cp(gg.a, pg[:, 0:128], E=k.act)
        if "r1" in skip:
            return
        kk = tmp()
        k.tt(kk.a, xk, PR(4), ALU.mult)
        sq = tmp()
        k.tt(sq.a, kk.a, kk.a, ALU.mult)
        cr = cols()
        k.red(cr[:, 0:2], sq.a.re("p (h c) -> p h c", h=2), ALU.add)
        k.rsqrt(cr[:, 0:2], cr[:, 0:2], 1.0, 1e-6)
        k.tt(kk.a.re("p (h c) -> p h c", h=2), kk.a.re("p (h c) -> p h c", h=2),
             cr[:, 0:2].re("p (h o) -> p h o", o=1).bc([128, 2, 64]), ALU.mult)
        kf = tmp()
        k.ts(kf.a, av.a, -1.0, None, ALU.add)
        k.tt(kf.a, kf.a, PR(5), ALU.mult)
        k.ts(kf.a, kf.a, 1.0, None, ALU.add)
        k.tt(kf.a, kf.a, xk, ALU.mult)
        bn = tmp()
        k.tt(bn.a, xr, kf.a, ALU.mult)
        k.tt(bn.a, bn.a, PR(6), ALU.mult)
        k.red(cr[:, 2:4], bn.a.re("p (h c) -> p h c", h=2), ALU.add)
        pgm = k.ps()
        k.mm(pgm[:, 0:128], C["triBD"], ld.a)
        Gm, Gi_, Gp = tmp(), tmp(), tmp()
        k.actf(Gm.a, pgm[:, 0:128], AF.Exp)
        k.actf(Gi_.a, pgm[:, 0:128], AF.Exp, scale=-1.0)
        k.tt(Gp.a, pgm[:, 0:128], ld.a, ALU.subtract)
        k.actf(Gp.a, Gp.a, AF.Exp)
        at, bt, kt, rt = tmp(), tmp(), tmp(), tmp()
        k.stt(at.a, kk.a, -1.0, Gp.a, ALU.mult, ALU.mult)
        k.tt(bt.a, kk.a, av.a, ALU.mult)
        k.tt(bt.a, bt.a, Gi_.a, ALU.mult)
        k.tt(kt.a, kf.a, Gi_.a, ALU.mult)
        k.tt(rt.a, xr, Gm.a, ALU.mult)
        if "r2" in skip:
            return
        y_r = tmp()
        fm = {}
        for nm, src in (("a", at), ("b", bt), ("k", kt), ("r", rt)):
            pp = k.ps()
            k.tr(pp[:, 0:128], src.a, ident)
            d_ = tmp()
            k.cp(d_.a, pp[:, 0:128], E=k.act)
            fm[nm] = d_
        tbase = tpi[0]
        for h in range(2):
            tpi[0] = tbase
            hs_ = slice(h * 64, (h + 1) * 64)
            hp = hs_
            pn, pnt, pak, prb_, prk = k.ps(), k.ps(), k.ps(), k.ps(), k.ps()
            k.mm(pn[:, 0:128], fm["b"][hp, :], fm["a"][hp, :])
            k.mm(pnt[:, 0:128], fm["a"][hp, :], fm["b"][hp, :])
            k.mm(pak[:, 0:128], fm["k"][hp, :], fm["a"][hp, :])
            k.mm(prb_[:, 0:128], fm["b"][hp, :], fm["r"][hp, :])
            k.mm(prk[:, 0:128], fm["k"][hp, :], fm["r"][hp, :])
            N, NT, AkT, RBT, RKT = tmp(), tmp(), tmp(), tmp(), tmp()
            k.tt(N.a, pn[:, 0:128], C["m01_st_strict"], ALU.mult)
            k.tt(NT.a, pnt[:, 0:128], C["m01_ts_strict"], ALU.mult)
            k.tt(AkT.a, pak[:, 0:128], C["m01_st_strict"], ALU.mult)
            k.tt(RBT.a, prb_[:, 0:128], C["m01_st_incl"], ALU.mult)
            k.tt(RKT.a, prk[:, 0:128], C["m01_st_incl"], ALU.mult)
            if "r3" in skip:
                return
            TT = neumann(N, NT)
            vh = xv[:, hs_]
            pav = k.ps()
            k.mm(pav[:, 0:64], AkT.a, vh)
            AkV = tmp()
            k.cp(AkV[:, 0:64], pav[:, 0:64], E=k.act)
            pw1 = k.ps()
            k.mm(pw1[hp, 0:128], at[:, hs_], TT.a)
            W1T = tmp()
            k.cp(W1T[hp, :], pw1[hp, 0:128], E=k.act)
            pU = k.ps()
            k.mm(pU[:, 0:64], TT.a, AkV[:, 0:64])
            U0 = tmp()
            k.cp(U0[:, 0:64], pU[:, 0:64])
            pY0 = k.ps()
            k.mm(pY0[:, 0:64], RKT.a, vh)
            Y0 = tmp()
            k.cp(Y0[:, 0:64], pY0[:, 0:64], E=k.act)
            if "r4" in skip:
                return
            U = tmp()
            for hf, rs in halves():
                Sin, Sout = S[hf]["rS"]
                pUs = k.ps()
                k.mm(pUs[rs, 0:64], W1T[hp, rs], Sin[hp, :])
                k.tt(U[rs, 0:64], U0[rs, 0:64], pUs[rs, 0:64], ALU.add)
                if "r4a" in skip:
                    return
                pYa, pYb = k.ps(), k.ps()
                k.mm(pYa[rs, 0:64], RBT[rs, rs], U[rs, 0:64])
                k.mm(pYb[rs, 0:64], fm["r"][hp, rs], Sin[hp, :])
                k.tt(y_r[rs, hs_], Y0[rs, 0:64], pYa[rs, 0:64], ALU.add)
                k.tt(y_r[rs, hs_], y_r[rs, hs_], pYb[rs, 0:64], ALU.add)
                if "r4b" in skip:
                    return
                pS = k.ps()
                k.mm(pS[hp, 0:64], bt[rs, hs_], U[rs, 0:64], start=True, stop=False)
                k.mm(pS[hp, 0:64], kt[rs, hs_], vh[rs, :], start=False, stop=True)
                if "r4c" in skip:
                    return
                pgl = k.ps()
                k.mm(pgl[hp, 0:1], ld[rs, hs_], onescol[rs, :])
                gl_ = cols()
                k.actf(gl_[hp, 0:1], pgl[hp, 0:1], AF.Exp)
                if "r4d" in skip:
                    return
                tS = tmp()
                k.ts(tS[hp, 0:64], Sin[hp, :], gl_[hp, 0:1], None, ALU.mult)
                k.stt(Sout[hp, :], pS[hp, 0:64], gl_[hp, 0:1], tS[hp, 0:64], ALU.mult, ALU.add)
        if "r5" in skip:
            return
        k.red(cr[:, 4:6], y_r.a.re("p (h c) -> p h c", h=2), ALU.add)
        k.ts(cr[:, 4:6], cr[:, 4:6], -1.0 / 64, None, ALU.mult)
        yc = tmp()
        k.tt(yc.a.re("p (h c) -> p h c", h=2), y_r.a.re("p (h c) -> p h c", h=2),
             cr[:, 4:6].re("p (h o) -> p h o", o=1).bc([128, 2, 64]), ALU.add)
        k.tt(sq.a, yc.a, yc.a, ALU.mult)
        k.red(cr[:, 6:8], sq.a.re("p (h c) -> p h c", h=2), ALU.add)
        k.rsqrt(cr[:, 6:8], cr[:, 6:8], 1.0 / 64, 64e-5)
        k.tt(yc.a.re("p (h c) -> p h c", h=2), yc.a.re("p (h c) -> p h c", h=2),
             cr[:, 6:8].re("p (h o) -> p h o", o=1).bc([128, 2, 64]), ALU.mult)
        k.tt(yc.a, yc.a, PR(7), ALU.mult)
        k.tt(yc.a, yc.a, PR(8), ALU.add)
        if "r5b" in skip:
            return
        bv = tmp()
        k.tt(bv.a.re("p (h c) -> p h c", h=2), xv.re("p (h c) -> p h c", h=2),
             cr[:, 2:4].re("p (h o) -> p h o", o=1).bc([128, 2, 64]), ALU.mult)
        k.tt(yc.a, yc.a, bv.a, ALU.add)
        k.tt(mix[:, 256:384], yc.a, gg.a, ALU.mult)
        if "r6" in skip:
            return
        halo_out = S[0]["rsh_out"]
        for (c0, c1) in ((0, 512), (512, 640)):
            pp = k.ps()
            k.mm(pp[0:32, 0:c1 - c0], tsel_r(sample), P[:, R0 + c0:R0 + c1])
            k.cp(halo_out[0:2, c0:c1], pp[0:2, 0:c1 - c0], E=k.act)

        if "m3" in skip:
            return
        tpi[0] = 0
        gin = S[0]["gcv_in"]
        gout = S[0]["gcv_out"]
        acc = gacc
        raw = P[:, O_GQ:O_GQ + 384]
        k.tt(acc.a, raw, prb[:, 1792 + 3 * 384:1792 + 4 * 384], ALU.mult)
        for d in (1, 2, 3):
            pp = k.ps()
            shifted(pp[:, 0:384], raw, 384, d, gin[0:6, :], 6)
            t_ = gtmp
            k.tt(t_.a, pp[:, 0:384], prb[:, 1792 + (3 - d) * 384:1792 + (4 - d) * 384], ALU.mult)
            k.tt(acc.a, acc.a, t_.a, ALU.add)
        pp = k.ps()
        k.mm(pp[0:32, 0:384], tails[:, 1 if sample else 0, :], raw)
        k.cp(gout[0:6, :], pp[0:6, 0:384], E=k.act)
        k.actf(acc.a, acc.a, AF.Silu)
        gq, gk, gv = acc[:, 0:128], acc[:, 128:256], acc[:, 256:384]
        cg = cols()
        jq = tmp()
        k.ttr(jq.a, gq, gq, ALU.mult, ALU.add, 0.0, cg[:, 0:1])
        k.ttr(jq.a, gk, gk, ALU.mult, ALU.add, 0.0, cg[:, 1:2])
        k.rsqrt(cg[:, 0:2], cg[:, 0:2], 1.0, 1e-6)
        k.ts(cg[:, 0:1], cg[:, 0:1], 128 ** -0.5, None, ALU.mult)
        qn, kn = tmp(), tmp()
        k.ts(qn.a, gq, cg[:, 0:1], None, ALU.mult)
        k.ts(kn.a, gk, cg[:, 1:2], None, ALU.mult)
        xg_, ab, gcol, beta = cg[:, 2:3], cg[:, 3:4], cg[:, 4:5], cg[:, 5:6]
        k.ts(xg_, P[:, O_GA:O_GA + 1], pc[:, 16:17], None, ALU.add)
        k.ts(ab, xg_, -1.0, None, ALU.mult)
        k.tt(ab, ab, xg_, ALU.min)
        k.actf(ab, ab, AF.Exp)
        k.actf(ab, ab, AF.Ln, bias=onescol)
        k.ts(xg_, xg_, 0.0, None, ALU.max)
        k.tt(xg_, xg_, ab, ALU.add)
        k.actf(gcol, pc[:, 15:16], AF.Exp)
        k.tt(gcol, gcol, xg_, ALU.mult)
        k.ts(gcol, gcol, -1.0, None, ALU.mult)
        k.actf(beta, P[:, O_GB:O_GB + 1], AF.Sigmoid)
        pG = k.ps()
        k.mm(pG[:, 0:1], C["triBD"], gcol)
        Gc, eG = cg[:, 6:7], cg[:, 7:8]
        k.cp(Gc, pG[:, 0:1])
        k.actf(eG, Gc, AF.Exp)
        dG = tmp()
        k.ts(dG.a, ident, Gc, None, ALU.mult)
        pGb = k.ps()
        k.mm(pGb[:, 0:128], C["ones"], dG.a)
        z1, z2, z3 = tmp(), tmp(), tmp()
        k.stt(z1.a, pGb[:, 0:128], Gc, C["mneg_st_strict"], ALU.subtract, ALU.min)
        k.actf(z1.a, z1.a, AF.Exp)
        k.stt(z2.a, pGb[:, 0:128], Gc, C["mneg_st_incl"], ALU.subtract, ALU.min)
        k.actf(z2.a, z2.a, AF.Exp)
        k.stt(z3.a, pGb[:, 0:128], Gc, C["mpos_ts_strict"], ALU.subtract, ALU.max)
        k.actf(z3.a, z3.a, AF.Exp, scale=-1.0)
        kb = tmp()
        k.ts(kb.a, kn.a, beta, None, ALU.mult)
        qg = tmp()
        k.ts(qg.a, qn.a, eG, None, ALU.mult)
        fT = {}
        for nm, src in (("k", kn), ("kb", kb), ("q", qn), ("qg", qg)):
            pp = k.ps()
            k.tr(pp[:, 0:128], src.a, ident)
            d_ = tmp()
            k.cp(d_.a, pp[:, 0:128], E=k.act)
            fT[nm] = d_
        pn, pnt, pqk = k.ps(), k.ps(), k.ps()
        k.mm(pn[:, 0:128], fT["k"].a, fT["kb"].a)
        k.mm(pnt[:, 0:128], fT["kb"].a, fT["k"].a)
        k.mm(pqk[:, 0:128], fT["k"].a, fT["q"].a)
        N, NT, QKT = tmp(), tmp(), tmp()
        k.stt(N.a, pn[:, 0:128], -1.0, z1.a, ALU.mult, ALU.mult)
        k.stt(NT.a, pnt[:, 0:128], -1.0, z3.a, ALU.mult, ALU.mult)
        k.tt(QKT.a, pqk[:, 0:128], z2.a, ALU.mult)
        TT = neumann(N, NT)
        rhs = grhs
        k.ts(rhs[:, 0:128], gv, beta, None, ALU.mult)
        k.ts(rhs[:, 128:256], kb.a, eG, None, ALU.mult)
        psv = k.ps()
        k.mm(psv[:, 0:128], TT.a, rhs[:, 0:128])
        solV = tmp()
        k.cp(solV.a, psv[:, 0:128], E=k.act)
        pkt = k.ps()
        k.mm(pkt[:, 0:128], rhs[:, 128:256], TT.a)
        solKT = tmp()
        k.cp(solKT.a, pkt[:, 0:128], E=k.act)
        U = tmp()
        o_g = tmp()
        for hf, rs in halves():
            Sin, Sout = S[hf]["gS"]
            pu_ = k.ps()
            k.mm(pu_[:, 0:128], solKT.a, Sin.a)
            k.tt(U[rs, :], solV[rs, :], pu_[rs, 0:128], ALU.subtract)
            poa, pob = k.ps(), k.ps()
            k.mm(poa[rs, 0:128], QKT[rs, rs], U[rs, :])
            k.mm(pob[:, 0:128], fT["qg"].a, Sin.a)
            k.cp(o_g[rs, :], poa[rs, 0:128], E=k.act)
            k.tt(o_g[rs, :], o_g[rs, :], pob[rs, 0:128], ALU.add)
            pgl = k.ps()
            k.mm(pgl[:, 0:1], sel[hf], gcol)
            cgl = cols()
            GL, eGL, kdc = cgl[:, 0:1], cgl[:, 1:2], cgl[:, 2:3]
            k.cp(GL, pgl[:, 0:1])
            k.actf(eGL, GL, AF.Exp)
            k.tt(kdc[rs], GL[rs], Gc[rs], ALU.subtract)
            k.actf(kdc[rs], kdc[rs], AF.Exp)
            kd = tmp()
            k.ts(kd[rs, :], kn[rs, :], kdc[rs], None, ALU.mult)
            pS = k.ps()
            k.mm(pS[:, 0:128], kd[rs, :], U[rs, :])
            k.stt(Sout.a, Sin.a, eGL, pS[:, 0:128], ALU.mult, ALU.add)
        k.actf(jq.a, o_g.a, AF.Square, accum=cg[:, 0:1])
        k.rsqrt(cg[:, 0:1], cg[:, 0:1], 1.0 / 128, EPS)
        k.stt(o_g.a, o_g.a, cg[:, 0:1], PR(13), ALU.mult, ALU.mult)
        sz = tmp()
        k.actf(sz.a, P[:, O_GZ:O_GZ + 128], AF.Silu)
        k.tt(mix[:, 384:512], o_g.a, sz.a, ALU.mult)

    def neumann(N, NT):
        Pm = ntmp()
        k.tt(Pm.a, N.a, ident, ALU.add)
        cur, curT = N, NT
        for j in range(1, 6):
            pnT = k.ps()
            k.mm(pnT[:, 0:128], cur.a, curT.a)
            nT = ntmp()
            k.cp(nT.a, pnT[:, 0:128], E=k.act)
            if j < 5:
                pn_ = k.ps()
                k.mm(pn_[:, 0:128], curT.a, cur.a)
                n_ = ntmp()
                k.cp(n_.a, pn_[:, 0:128])
            pP = k.ps()
            k.mm(pP[:, 0:128], nT.a, Pm.a)
            Pn = ntmp()
            k.tt(Pn.a, pP[:, 0:128], Pm.a, ALU.add)
            Pm = Pn
            if j < 5:
                cur, curT = n_, nT
        return Pm

    hsel1 = k.sb([2, 128], name="hsel1")
    k.memset(hsel1.a, 0.0)
    k.cp(hsel1[0:1, 0:1], onescol[0:1, :])
    tselP = k.sb([128, 32], name="tselP")
    tselS = k.sb([128, 32], name="tselS")
    k.memset(tselP.a, 0.0)
    k.memset(tselS.a, 0.0)
    k.cp(tselP[0:128, 0:1], tails[:, 0, 2:3])
    k.cp(tselS[0:128, 0:1], tails[:, 1, 2:3])
    k.cp(tselS[0:128, 1:2], tails[:, 1, 5:6])
    hsel1b = k.sb([2, 128], name="hsel1b")
    k.dma(hsel1b.a, hsel_d[2:6:3, 0:128])
    k.cp(hsel1.a, hsel1b.a)

    def tsel_r(sample):
        return (tselS if sample else tselP).a

    def phase_a_and_mix(l, xsrc):
        ti = 0
        for s in range(4):
            for j in range(NT_S):
                sample = j >= NT_P
                row0 = s * SL + j * 128
                x = xt[ti % 2]
                P = Pj[0]
                mix = mixt[0]
                ti += 1
                xrow = row0 if l == 0 else (j * 4 + s) * 128
                k.dma(x.a, xsrc[xrow:xrow + 128, :])
                cs_ = cols()
                k.actf(junk.a, x.a, AF.Square, accum=cs_[:, 0:1])
                k.rsqrt(cs_[:, 0:1], cs_[:, 0:1], 1.0 / D, EPS)
                k.ts(hb.a, x.a, cs_[:, 0:1], None, ALU.mult)
                for half in range(2):
                    for c8 in range(8):
                        kc = half * 8 + c8
                        k.tr(pst[:, c8 * 128:(c8 + 1) * 128], hb[:, kc * 128:(kc + 1) * 128], identb)
                    for c8 in range(8):
                        kc = half * 8 + c8
                        k.actf(hT[:, kc, :], pst[:, c8 * 128:(c8 + 1) * 128], AF.Copy, scale=gpre[:, kc:kc + 1])
                for nb in range(4):
                    pp = k.ps()
                    for kc in range(16):
                        k.mm(pp[:, 0:449], hT[:, kc, :], win[:, kc, nb * 449:(nb + 1) * 449], start=(kc == 0), stop=(kc == 15))
                    k.cp(P[:, nb * 449:(nb + 1) * 449], pp[:, 0:449], E=(k.act if nb % 2 else k.dve))
                if not sample:
                    S = [dict(mC=(st_mC, st_mC), mm=(st_mm, st_mm), s5=(st_s5, st_s5), rS=(st_rS, st_rS), gS=(st_gS, st_gS))
                         for _ in range(2)]
                    S[0].update(rsh_in=st_rsh.a, rsh_out=st_rsh.a, gcv_in=st_gcv.a, gcv_out=st_gcv.a)
                else:
                    S = []
                    for hf in range(2):
                        q = 4 * s + 2 * (j - NT_P) + hf
                        k.dma(ss_mC[hf].a, i_mCn[l, q])
                        k.dma(ss_mm[hf].a, i_mm[l, q])
                        k.dma(ss_s5[hf].a.re("p r j -> p (r j)"), i_s5[l, q])
                        k.dma(ss_rS[hf].a, i_rS[l, q])
                        k.dma(ss_rsh[hf:hf + 1, :], i_rsh[l, q:q + 1, :])
                        k.dma(ss_gS[hf].a, i_gS[l, q])
                        k.dma(ss_gcv[3 * hf:3 * hf + 3, :], i_gcv[l, q])
                        S.append(dict(mC=(ss_mC[hf], ss_mC[hf]), mm=(ss_mm[hf], ss_mm[hf]), s5=(ss_s5[hf], ss_s5[hf]),
                                      rS=(ss_rS[hf], ss_rS[hf]), gS=(ss_gS[hf], ss_gS[hf])))
                    S[0].update(rsh_in=ss_rsh.a, rsh_out=ss_rsh.a, gcv_in=ss_gcv.a, gcv_out=ss_gcv.a)
                if "mix" not in skip:
                    mixer_tile(l, P, mix, S, sample)
                k.dma(mixp[l][row0:row0 + 128, :], mix.a)
                if sample:
                    for hf in range(2):
                        q = 1 + 4 * s + 2 * (j - NT_P) + hf
                        k.dma(o_mCn[l, q], ss_mC[hf].a)
                        k.dma(o_mm[l, q], ss_mm[hf].a)
                        k.dma(o_s5[l, q], ss_s5[hf].a.re("p r j -> p (r j)"))
                        k.dma(o_rS[l, q], ss_rS[hf].a)
                        k.dma(o_rsh[l, q:q + 1, :], ss_rsh[hf:hf + 1, :])
                        k.dma(o_gS[l, q], ss_gS[hf].a)
                        k.dma(o_gcv[l, q], ss_gcv[3 * hf:3 * hf + 3, :])
        k.dma(o_mCn[l, 0], st_mC.a)
        k.dma(o_mm[l, 0], st_mm.a)
        k.dma(o_s5[l, 0], st_s5.a.re("p r j -> p (r j)"))
        k.dma(o_rS[l, 0], st_rS.a)
        k.dma(o_rsh[l, 0:1, :], st_rsh[0:1, :])
        k.dma(o_gS[l, 0], st_gS.a)
        k.dma(o_gcv[l, 0], st_gcv[0:3, :])


    def phase_b(l, xres, xdst):
        k.dma(gbc[:, 0, :], gvec[l, 1, :].pbc(128))
        k.dma(gbc[:, 1, :], gvec[l, 3, :].pbc(128))
        k.dma(bglu.a, gvec[l, 4, 0:512].pbc(128))
        k.dma(gffn.a, gcols[l, 1])
        for kc in range(4):
            i = cnt[0] % 2
            cnt[0] += 1
            k.dma(stg[i][:, 0:512], w_glu_d[l, kc * 128:(kc + 1) * 128, :])
            k.cp(wglu[:, kc, :], stg[i][:, 0:512], E=cast_engs[i])
        nblk = (NT_S + 2) // 3
        for blk in range(nblk):
            t0 = blk * 3
            nt = min(3, NT_S - t0)
            TB = nt * 128
            for ti in range(nt):
                msel = scr[:, 0, :]
                x1t = scr[:, 2, :]
                for q in range(4):
                    cand = scr[:, 1, :]
                    r0 = q * SL + (t0 + ti) * 128
                    ch, wi = r0 // 512, r0 % 512
                    k.dma(cand.re("p (s c) -> p s c", s=4),
                          mixa[l].a[ch * 2048:(ch + 1) * 2048, :].re("(s t) c -> t s c", s=4)[wi:wi + 128, :, :])
                    if q == 0:
                        k.ts(msel, cand, qm[:, 0:1], None, ALU.mult)
                    else:
                        k.stt(msel, cand, qm[:, q:q + 1], msel, ALU.mult, ALU.add)
                m4 = msel.re("p (s c) -> p s c", s=4)
                k.cp(ys32.a.re("p (s c) -> p s c", s=4), m4[:, :, 128:256])
                k.cp(ysb.a, ys32.a, E=k.act)
                for c4 in range(4):
                    k.tr(pst[:, c4 * 128:(c4 + 1) * 128], ysb[:, c4 * 128:(c4 + 1) * 128], identb)
                k.cp(ysT.a.re("p c t -> p (c t)"), pst[:, 0:512], E=k.act)
                pp = k.ps()
                for c4 in range(4):
                    k.mm(pp.a, ysT[:, c4, :], wglu[:, c4, :], start=(c4 == 0), stop=(c4 == 3))
                k.tt(sgl.a, pp.a, bglu.a, ALU.add)
                k.actf(sgl.a, sgl.a, AF.Sigmoid)
                k.tt(m4[:, :, 128:256], ys32.a.re("p (s c) -> p s c", s=4), sgl.a.re("p (s c) -> p s c", s=4), ALU.mult)
                k.cp(hb.a, msel, E=k.act)
                for half in range(2):
                    for c8 in range(8):
                        kc = half * 8 + c8
                        k.tr(pst[:, c8 * 128:(c8 + 1) * 128], hb[:, kc * 128:(kc + 1) * 128], identb)
                    k.cp(mixT[:, half * 8:(half + 1) * 8, ti * 128:(ti + 1) * 128],
                         pst.a.re("p (c t) -> p c t", c=8), E=(k.act if half else k.dve))
            if stop == "b1":
                return
            t1all = scr
            wo = wbuf[:, 0:16384].re("p (b c f) -> p b c f", b=2, c=16)
            for nb in range(4):
                k.dma(wo[:, nb % 2, :, :], wo_s[l][nb])
                for ti in range(nt):
                    pp = k.ps()
                    for kc in range(16):
                        k.mm(pp.a, mixT[:, kc, ti * 128:(ti + 1) * 128], wo[:, nb % 2, kc, :], start=(kc == 0), stop=(kc == 15))
                    k.cp(t1all[:, ti, nb * 512:(nb + 1) * 512], pp.a, E=(k.act if ti % 2 else k.dve))
            if stop == "b2":
                return
            for ti in range(nt):
                r0 = (t0 + ti) * 128
                cs_ = cols()
                k.actf(junk.a, t1all[:, ti, :], AF.Square, accum=cs_[:, 0:1])
                k.rsqrt(cs_[:, 0:1], cs_[:, 0:1], 1.0 / D, EPS)
                k.dma(xt[0].a, xres[r0:r0 + 128, :])
                k.stt(t1all[:, ti, :], t1all[:, ti, :], cs_[:, 0:1], gbc[:, 0, :], ALU.mult, ALU.mult)
                k.tt(t1all[:, ti, :], t1all[:, ti, :], xt[0].a, ALU.add)
                k.dma(xmid[r0:r0 + 128, :], t1all[:, ti, :])
                k.actf(junk.a, t1all[:, ti, :], AF.Square, accum=cs_[:, 1:2])
                k.rsqrt(cs_[:, 1:2], cs_[:, 1:2], 1.0 / D, EPS)
                k.ts(hb.a, t1all[:, ti, :], cs_[:, 1:2], None, ALU.mult)
                for half in range(2):
                    for c8 in range(8):
                        kc = half * 8 + c8
                        k.tr(pst[:, c8 * 128:(c8 + 1) * 128], hb[:, kc * 128:(kc + 1) * 128], identb)
                    for c8 in range(8):
                        kc = half * 8 + c8
                        k.actf(mixT[:, kc, ti * 128:(ti + 1) * 128], pst[:, c8 * 128:(c8 + 1) * 128], AF.Copy,
                               scale=gffn[:, kc:kc + 1])
            if stop == "b3":
                return
            for fc in range(44):
                wb = wgu[fc % 2]
                k.dma(wb[:, 0, :, :], wg_s[l][fc])
                k.dma(wb[:, 1, :, :], wu_s[l][fc])
                pg_, pu_ = k.ps(), k.ps()
                for kc in range(16):
                    k.mm(pg_[:, 0:TB], wb[:, 0, kc, :], mixT[:, kc, 0:TB], start=(kc == 0), stop=(kc == 15))
                for kc in range(16):
                    k.mm(pu_[:, 0:TB], wb[:, 1, kc, :], mixT[:, kc, 0:TB], start=(kc == 0), stop=(kc == 15))
                sg_ = sgT
                k.actf(sg_[:, 0:TB], pg_[:, 0:TB], AF.Silu)
                k.tt(actT[:, fc, 0:TB], sg_[:, 0:TB], pu_[:, 0:TB], ALU.mult)
            if stop == "b4":
                return
            wd = wbuf.a.re("p (c f) -> p c f", c=44)
            for nb in range(4):
                k.dma(wd[:, 0:22, :], wd_s[l][nb, :, 0:22, :])
                k.dma(wd[:, 22:44, :], wd_s[l][nb, :, 22:44, :])
                for ti in range(nt):
                    pp = k.ps()
                    for fc in range(44):
                        k.mm(pp.a, actT[:, fc, ti * 128:(ti + 1) * 128], wd[:, fc, :], start=(fc == 0), stop=(fc == 43))
                    k.cp(scr[:, ti, nb * 512:(nb + 1) * 512], pp.a, E=(k.act if ti % 2 else k.dve))
            if stop == "b5":
                return
            for ti in range(nt):
                r0 = (t0 + ti) * 128
                cs_ = cols()
                k.actf(junk.a, scr[:, ti, :], AF.Square, accum=cs_[:, 0:1])
                k.rsqrt(cs_[:, 0:1], cs_[:, 0:1], 1.0 / D, EPS)
                k.dma(xt[1].a, xmid[r0:r0 + 128, :])
                k.stt(scr[:, ti, :], scr[:, ti, :], cs_[:, 0:1], gbc[:, 1, :], ALU.mult, ALU.mult)
                k.tt(scr[:, ti, :], scr[:, ti, :], xt[1].a, ALU.add)
                k.dma(xdst[r0:r0 + 128, :], scr[:, ti, :])

    for l in range(DEPTH):
        with ExitStack() as es:
            k.stack = es
            win = k.sb([128, 16, NQ], BF16, "win")
            pc = k.sb([128, 32], name="pc")
            prb = k.sb([128, 3328], name="prb")
            gpre = k.sb([128, 16], name="gpre")
            s5_P = k.sb([128, 2, 4, 128], name="s5P")
            s5_Pi = k.sb([128, 2, 4, 128], name="s5Pi")
            s5_lb = k.sb([128, 2, 4], name="s5lb")
            s5_Bb = k.sb([128, 2, 4, 128], name="s5Bb")
            s5_Cb = k.sb([128, 2, 4, 128], name="s5Cb")
            w2a2 = k.sb([128, 128], name="w2a2")
            g2 = k.sb([128, 128], name="g2")
            st_mC = k.sb([128, 129], name="st_mC")
            st_mm = k.sb([128, 1], name="st_mm")
            st_s5 = k.sb([128, 2, 4], name="st_s5")
            st_rS = k.sb([128, 64], name="st_rS")
            st_rsh = k.sb([2, 640], name="st_rsh")
            st_gS = k.sb([128, 128], name="st_gS")
            st_gcv = k.sb([6, 384], name="st_gcv")
            ss_mC = [k.sb([128, 129], name="ss_mC%d" % i) for i in range(2)]
            ss_mm = [k.sb([128, 1], name="ss_mm%d" % i) for i in range(2)]
            ss_s5 = [k.sb([128, 2, 4], name="ss_s5%d" % i) for i in range(2)]
            ss_rS = [k.sb([128, 64], name="ss_rS%d" % i) for i in range(2)]
            ss_rsh = k.sb([2, 640], name="ss_rsh")
            ss_gS = [k.sb([128, 128], name="ss_gS%d" % i) for i in range(2)]
            ss_gcv = k.sb([6, 384], name="ss_gcv")

            hT = k.sb([128, 16, 128], BF16, "hT")
            Pj = [k.sb([128, NQ], name="P%d" % i) for i in range(1)]
            mixt = [k.sb([128, 512], name="mixt%d" % i) for i in range(1)]

            npool = [T(n="np%d" % i) for i in range(8)]

            big_pool = [T((128, 512), n="bp%d" % i) for i in range(7)]


            vaug = k.sb([128, 129], name="vaug")
            intra = k.sb([128, 129], name="intra")
            mt1 = k.sb([128, 129], name="mt1")
            sh1 = k.sb([128, 640], name="sh1")
            gacc = k.sb([128, 384], name="gacc")
            gtmp = k.sb([128, 384], name="gtmp")
            grhs = k.sb([128, 256], name="grhs")
            del tmp_pool[:]
            tpi[0] = 0
            layer_setup(l)
            if stop == "setup":
                k.final_wait()
                return nc
            phase_a_and_mix(l, xg if l == 0 else x1all)
            if stop == "A":
                k.final_wait()
                return nc
            k.barrier()
            k.stack = None
        k.allgather(mixa[l], mixp[l], 512)
        if stop == "AG":
            k.final_wait()
            return nc
        with ExitStack() as es:
            k.stack = es
            mixT = k.sb([128, 16, 384], BF16, "mixT")
            scr = k.sb([128, 3, 2048], name="scr")
            actT = k.sb([128, 44, 384], BF16, "actT")
            wbuf = k.sb([128, 22528], BF16, "wbuf")
            wgu = [k.sb([128, 2, 16, 128], BF16, "wgu%d" % i) for i in range(2)]
            gbc = k.sb([128, 2, 2048], name="gbc")
            bglu = k.sb([128, 512], name="bglu")
            gffn = k.sb([128, 16], name="gffn")
            wglu = k.sb([128, 4, 512], BF16, "wglu")
            ysb = k.sb([128, 512], BF16, "ysb")
            ysT = k.sb([128, 4, 128], BF16, "ysT")
            ys32 = k.sb([128, 512], name="ys32")
            sgl = k.sb([128, 512], name="sgl")
            sgT = k.sb([128, 384], name="sgT")
            phase_b(l, xo if l == 0 else x1own, x1own if l == 0 else y_own)
            if stop in ("B", "b1", "b2", "b3", "b4", "b5"):
                k.final_wait()
                return nc
            k.barrier()
            k.stack = None
        if l == 0:
            k.allgather(x1all, x1own, 128)
    k.final_wait()
    return nc


def _qcols(r):
    W = 512
    NM = 4 * W + 8
    NSs = W
    NR = 3 * W + 256
    m0, s0, r0 = 0, NM, NM + NSs
    g0 = NM + NSs + NR
    hd = lambda base, h, w=128: list(range(base + h * w, base + (h + 1) * w))
    cols = []
    for blk in range(4):
        cols += hd(m0 + blk * W, r)
    cols += [m0 + 4 * W + r, m0 + 4 * W + 4 + r]
    cols += hd(s0, r)
    for blk in range(3):
        cols += hd(r0 + blk * W, r)
    cols += list(range(r0 + 3 * W, r0 + 3 * W + 256))
    for blk in range(4):
        cols += hd(g0 + blk * W, r)
    cols += [g0 + 4 * W + r, g0 + 4 * W + 4 + r]
    assert len(cols) == NQ
    return np.array(cols)


_CACHE = {}


def _run(inp, SEQ, stop=None, skip=()):
    f = lambda n: np.asarray(inp[n], np.float32)
    PT = SEQ // 4
    SL = PT + 256
    if SEQ not in _CACHE:
        _CACHE[SEQ] = build(SEQ, stop, skip)
    nc = _CACHE[SEQ]
    cst = _consts()
    cmat = np.stack([cst[n] for n in CONST_ORDER]).astype(np.float32)
    tails = np.stack([cst["tailP"], cst["tailS"]]).astype(np.float32)
    xp, xs = f("x_prompt"), f("x_sample")
    in_maps = []
    h = lambda a, r: a[..., r * 128:(r + 1) * 128]
    for c in range(8):
        b, r = c // 4, c % 4
        slabs = []
        for s in range(4):
            slabs.append(np.concatenate([xp[b, s * PT:(s + 1) * PT],
                                         xs[16 * b + 4 * s:16 * b + 4 * s + 4].reshape(256, D)], 0))
        xgrp = np.concatenate(slabs, 0)
        m = {"xg": xgrp, "xo": slabs[r], "cmat": cmat, "hsel": cst["hsel"], "tails": tails}
        qm = np.zeros((128, 4), np.float32)
        qm[:, r] = 1.0
        m["qmask"] = qm
        qc = _qcols(r)
        m["w_in"] = np.ascontiguousarray(f("w_in")[:, :, qc])
        perm = np.concatenate([np.concatenate([np.arange(g * 512 + q * 128, g * 512 + (q + 1) * 128) for g in range(4)])
                               for q in range(4)])
        m["w_out"] = np.ascontiguousarray(f("w_out")[:, perm, :])
        m["w_gate"], m["w_up"], m["w_down"] = f("w_gate"), f("w_up"), f("w_down")
        m["w_glu"] = f("s5_w_glu")
        gv = np.zeros((DEPTH, 5, D), np.float32)
        gv[:, 0], gv[:, 1], gv[:, 2], gv[:, 3] = f("g_pre_mix"), f("g_post_mix"), f("g_pre_ffn"), f("g_post_ffn")
        gv[:, 4, 0:512] = f("s5_b_glu")
        m["gvec"] = gv
        m["gcols"] = np.stack([f("g_pre_mix").reshape(DEPTH, 16, 128).transpose(0, 2, 1), f("g_pre_ffn").reshape(DEPTH, 16, 128).transpose(0, 2, 1)], 1)
        pcol = np.zeros((DEPTH, 128, 32), np.float32)
        gs = slice(8 * r, 8 * r + 8)
        tile4 = lambda a: a.reshape(DEPTH, 4, 128).transpose(0, 2, 1)
        pcol[:, :, 0:4] = tile4(f("s5_lam_re")[:, gs].reshape(DEPTH, 512))
        pcol[:, :, 4:8] = tile4(f("s5_lam_im")[:, gs].reshape(DEPTH, 512))
        pcol[:, :, 8:12] = tile4(np.repeat(f("s5_log_dt")[:, gs], 64, axis=1))
        pcol[:, :, 12] = h(f("s5_D"), r)
        pcol[:, :, 13] = f("mlstm_gate_bias")[:, 0, r][:, None]
        pcol[:, :, 14] = f("mlstm_gate_bias")[:, 1, r][:, None]
        pcol[:, :, 15] = f("gdn_A_log")[:, r][:, None]
        pcol[:, :, 16] = f("gdn_dt_bias")[:, r][:, None]
        m["pcol"] = pcol
        prow = np.zeros((DEPTH, 1, 3328), np.float32)
        names9 = [h(f("mlstm_norm_g"), r), h(f("rwkv_w0"), r), h(f("rwkv_a0"), r), h(f("rwkv_k_k"), r), h(f("rwkv_k_a"), r),
                  f("rwkv_r_k").reshape(DEPTH, 512)[:, r * 128:(r + 1) * 128], h(f("rwkv_ln_w"), r), h(f("rwkv_ln_b"), r),
                  f("gdn_norm_g")]
        for i, a in enumerate(names9):
            prow[:, 0, i * 128:(i + 1) * 128] = a
        mu = f("rwkv_mu")
        prow[:, 0, 1152:1792] = np.concatenate([h(mu[:, 0:512], r), h(mu[:, 512:1024], r), h(mu[:, 1024:1536], r), mu[:, 1536:]], 1)
        cw = f("gdn_conv_w")
        for j in range(4):
            prow[:, 0, 1792 + j * 384:1792 + (j + 1) * 384] = np.concatenate([h(cw[:, j, 0:512], r), h(cw[:, j, 512:1024], r), h(cw[:, j, 1024:1536], r)], 1)
        m["prow"] = prow
        def blk(a_gpc):
            o = np.zeros((DEPTH, 128, 4, 128), np.float32)
            for g in range(8):
                j, hh = g // 2, g % 2
                o[:, hh * 64:(hh + 1) * 64, j, g * 16:(g + 1) * 16] = a_gpc[:, g]
            return o
        m["s5b"] = np.stack([blk(f("s5_B_re")[:, gs]), blk(f("s5_B_im")[:, gs])], 1)
        m["s5c"] = np.stack([blk(f("s5_C_re")[:, gs].transpose(0, 1, 3, 2)), blk(f("s5_C_im")[:, gs].transpose(0, 1, 3, 2))], 1)
        m["rw2"] = np.concatenate([h(f("rwkv_w2"), r), h(f("rwkv_a2"), r)], 1)
        m["rg2"] = np.ascontiguousarray(h(f("rwkv_g2"), r))
        sq = slice(16 * b, 16 * b + 16)
        m["i_mCn"] = np.concatenate([f("state_mlstm_C")[:, sq, r], f("state_mlstm_n")[:, sq, r][..., None]], -1)
        m["i_mm"] = np.repeat(f("state_mlstm_m")[:, sq, r][..., None, None], 128, axis=2)
        sre = f("state_s5_re")[:, sq, gs].reshape(DEPTH, 16, 4, 128).transpose(0, 1, 3, 2)
        sim = f("state_s5_im")[:, sq, gs].reshape(DEPTH, 16, 4, 128).transpose(0, 1, 3, 2)
        m["i_s5"] = np.concatenate([sre, sim], -1)
        rS = f("state_rwkv_S")[:, sq, 2 * r:2 * r + 2]
        m["i_rS"] = np.ascontiguousarray(rS.transpose(0, 1, 2, 4, 3)).reshape(DEPTH, 16, 128, 64)
        rsh = f("state_rwkv_shift")[:, sq]
        m["i_rsh"] = np.concatenate([h(rsh[..., 0:512], r), h(rsh[..., 512:1024], r), h(rsh[..., 1024:1536], r), rsh[..., 1536:]], -1)
        m["i_gS"] = np.ascontiguousarray(f("state_gdn_S")[:, sq, r])
        gc = f("state_gdn_conv")[:, sq]
        m["i_gcv"] = np.concatenate([h(gc[..., 0:512], r), h(gc[..., 512:1024], r), h(gc[..., 1024:1536], r)], -1)
        in_maps.append({kk_: np.ascontiguousarray(v, dtype=np.float32) for kk_, v in m.items()})
    res = run_bass_kernel_spmd(nc, in_maps, core_ids=list(range(8)))
    R = res.results
    B, NSQ = 2, 32
    yp = np.zeros((B, SEQ, D), np.float32)
    ys = np.zeros((NSQ, 64, D), np.float32)
    def mk(bn):
        return (np.zeros((DEPTH, bn, 4, 128, 128), np.float32), np.zeros((DEPTH, bn, 4, 128), np.float32),
                np.zeros((DEPTH, bn, 4), np.float32), np.zeros((DEPTH, bn, 32, 64), np.float32),
                np.zeros((DEPTH, bn, 32, 64), np.float32), np.zeros((DEPTH, bn, 8, 64, 64), np.float32),
                np.zeros((DEPTH, bn, 1792), np.float32), np.zeros((DEPTH, bn, 4, 128, 128), np.float32),
                np.zeros((DEPTH, bn, 3, 1536), np.float32))
    Pp, Ps = mk(B), mk(NSQ)
    for c in range(8):
        b, r = c // 4, c % 4
        o = R[c]
        y = o["y_own"]
        yp[b, r * PT:(r + 1) * PT] = y[:PT]
        ys[16 * b + 4 * r:16 * b + 4 * r + 4] = y[PT:].reshape(4, 64, D)
        for (dst, idx, src_sl) in ((Pp, b, slice(0, 1)), (Ps, slice(16 * b, 16 * b + 16), slice(1, 17))):
            def put(arr, val):
                if isinstance(idx, int):
                    arr[:, idx] = val[:, 0]
                else:
                    arr[:, idx] = val
            return_none = None
            mCn = o["o_mCn"][:, src_sl]
            tgt = (lambda a: a[:, idx]) if not isinstance(idx, int) else (lambda a: a[:, idx:idx + 1])
            tgt(dst[0])[:, :, r] = mCn[..., 0:128]
            tgt(dst[1])[:, :, r] = mCn[..., 128]
            tgt(dst[2])[:, :, r] = o["o_mm"][:, src_sl][:, :, 0, 0]
            s5 = o["o_s5"][:, src_sl]
            nn = s5.shape[1]
            tgt(dst[3])[:, :, 8 * r:8 * r + 8] = s5[..., 0:4].transpose(0, 1, 3, 2).reshape(DEPTH, nn, 8, 64)
            tgt(dst[4])[:, :, 8 * r:8 * r + 8] = s5[..., 4:8].transpose(0, 1, 3, 2).reshape(DEPTH, nn, 8, 64)
            rS = o["o_rS"][:, src_sl].reshape(DEPTH, nn, 2, 64, 64)
            tgt(dst[5])[:, :, 2 * r:2 * r + 2] = rS.transpose(0, 1, 2, 4, 3)
            rsh = o["o_rsh"][:, src_sl]
            for i3 in range(3):
                tgt(dst[6])[:, :, i3 * 512 + r * 128:i3 * 512 + (r + 1) * 128] = rsh[..., i3 * 128:(i3 + 1) * 128]
            if r == 0:
                tgt(dst[6])[:, :, 1536:] = rsh[..., 384:]
            tgt(dst[7])[:, :, r] = o["o_gS"][:, src_sl]
            gcv = o["o_gcv"][:, src_sl]
            for i3 in range(3):
                tgt(dst[8])[:, :, :, i3 * 512 + r * 128:i3 * 512 + (r + 1) * 128] = gcv[..., i3 * 128:(i3 + 1) * 128]
    return (yp, ys) + Pp + Ps


def kernel(**inputs):
    return _run(inputs, 16384)
```

```python
import numpy as np
import ml_dtypes
import concourse.bass as bass
import concourse.mybir as mybir
from concourse.bass_utils import run_bass_kernel_spmd
from contextlib import ExitStack

F32 = mybir.dt.float32
BF16 = mybir.dt.bfloat16
ALU = mybir.AluOpType
AF = mybir.ActivationFunctionType
AX = mybir.AxisListType.X

D = 2048
DFF = 5632
NQ = 1796
NS = 16
DEPTH = 2
EPS = 1e-6
O_MQ, O_MK, O_MV, O_MO, O_MI, O_MF = 0, 128, 256, 384, 512, 513
O_SU = 514
O_R = 642
O_GQ, O_GK, O_GV, O_GZ, O_GA, O_GB = 1282, 1410, 1538, 1666, 1794, 1795
NEG = -1.0e30


class Eng:
    def __init__(self, nc, e, name):
        self.e = e
        self.name = name
        self.sem = nc.alloc_semaphore(name + "_prog")
        self.cnt = 0
        self.seen = {}


class Tile:
    def __init__(self, t):
        self.t = t
        self.w = None
        self.r = {}

    def __getitem__(self, idx):
        return V(self, self.t.ap()[idx] if hasattr(self.t, "ap") else self.t[idx])

    @property
    def a(self):
        return V(self, self.t.ap() if hasattr(self.t, "ap") else self.t[:])


class V:
    def __init__(self, tile, ap):
        self.tile = tile
        self.ap = ap

    def __getitem__(self, idx):
        return V(self.tile, self.ap[idx])

    def re(self, pat, **kw):
        return V(self.tile, self.ap.rearrange(pat, **kw))

    def bc(self, shape):
        return V(self.tile, self.ap.to_broadcast(shape))

    def pbc(self, n):
        return V(self.tile, self.ap.partition_broadcast(n))


class KB:
    def __init__(self, nc):
        self.nc = nc
        self.pe = Eng(nc, nc.tensor, "pe")
        self.act = Eng(nc, nc.scalar, "act")
        self.dve = Eng(nc, nc.vector, "dve")
        self.pool = Eng(nc, nc.gpsimd, "pool")
        self.sp = Eng(nc, nc.sync, "sp")
        self.nds = 40
        self.dsem = [nc.alloc_semaphore("dq%d" % i) for i in range(self.nds)]
        self.dval = [0] * self.nds
        self.dnext = 0
        self.ccsem = nc.alloc_semaphore("ccs")
        self.ccval = 0
        self.uid = 0
        self.psb = []
        self.psi = 0

    def sb(self, shape, dt=F32, name=None):
        self.uid += 1
        nm = "%s_%d" % (name or "t", self.uid)
        if getattr(self, "stack", None) is not None:
            return Tile(self.stack.enter_context(self.nc.sbuf_tensor(nm, list(shape), dt)))
        return Tile(self.nc.alloc_sbuf_tensor(nm, list(shape), dt))

    def dram(self, name, shape, dt=F32, kind=None):
        if kind:
            return Tile(self.nc.dram_tensor(name, list(shape), dt, kind=kind))
        return Tile(self.nc.dram_tensor(name, list(shape), dt))

    def ps(self):
        t = self.psb[self.psi % len(self.psb)]
        self.psi += 1
        return t

    def _wait(self, E, tok):
        sem, val, key = tok
        if E.seen.get(key, 0) >= val:
            return
        E.e.wait_ge(sem, val)
        E.seen[key] = val

    def _deps(self, E, reads, writes, skip_self=False):
        for v in reads:
            t = v.tile
            if t.w is not None and not (skip_self and t.w[2] == E.name):
                self._wait(E, t.w)
        for v in writes:
            t = v.tile
            if t.w is not None and not (skip_self and t.w[2] == E.name):
                self._wait(E, t.w)
            for k, tok in t.r.items():
                if not (skip_self and k == E.name):
                    self._wait(E, tok)

    def _done(self, E, ins, reads, writes):
        E.cnt += 1
        ins.then_inc(E.sem, 1)
        tok = (E.sem, E.cnt, E.name)
        E.seen[E.name] = E.cnt if E is self.pe else E.seen.get(E.name, 0)
        for v in reads:
            v.tile.r[E.name] = tok
        for v in writes:
            v.tile.w = tok
            v.tile.r = {}

    def op(self, E, fn, reads, writes):
        reads = [v for v in reads if isinstance(v, V)]
        self._deps(E, reads, writes, skip_self=(E is self.pe))
        ins = fn()
        self._done(E, ins, reads, writes)

    def dma(self, out, in_, E=None):
        E = E or self.sp
        self._deps(E, [in_], [out])
        i = self.dnext % self.nds
        self.dnext += 1
        if self.dval[i] > 0:
            self._wait(E, (self.dsem[i], self.dval[i], "d%d" % i))
        self.dval[i] += 16
        E.e.dma_start(out=out.ap, in_=in_.ap).then_inc(self.dsem[i], 16)
        tok = (self.dsem[i], self.dval[i], "d%d" % i)
        in_.tile.r["d%d" % i] = tok
        out.tile.w = tok
        out.tile.r = {}

    def allgather(self, out_t, in_t, R):
        E = self.pool
        self._deps(E, [in_t.a], [out_t.a])
        n = in_t.t.ap().shape[0] // R
        for i in range(n):
            self.ccval += 1
            E.e.collective_compute("AllGather", ALU.bypass, replica_groups=[[0, 1, 2, 3], [4, 5, 6, 7]],
                                   ins=[in_t.t.ap()[i * R:(i + 1) * R, :]],
                                   outs=[out_t.t.ap()[i * 4 * R:(i + 1) * 4 * R, :]]).then_inc(self.ccsem)
        tok = (self.ccsem, self.ccval, "cc")
        in_t.r["cc"] = tok
        out_t.w = tok
        out_t.r = {}

    def mm(self, out, lhsT, rhs, start=True, stop=True):
        self.op(self.pe, lambda: self.nc.tensor.matmul(out.ap, lhsT.ap, rhs.ap, start=start, stop=stop),
                [lhsT, rhs], [out])

    def tr(self, out, in_, ident):
        self.op(self.pe, lambda: self.nc.tensor.transpose(out.ap, in_.ap, ident.ap), [in_, ident], [out])

    def actf(self, out, in_, func, bias=None, scale=1.0, accum=None):
        kw = {}
        if bias is not None:
            kw["bias"] = bias.ap if isinstance(bias, V) else bias
        if isinstance(scale, V):
            kw["scale"] = scale.ap
        else:
            kw["scale"] = scale
        if accum is not None:
            kw["accum_out"] = accum.ap
        wr = [out] + ([accum] if accum is not None else [])
        self.op(self.act, lambda: self.nc.scalar.activation(out.ap, in_.ap, func, **kw),
                [in_, bias, scale], wr)

    def _sv(self, s):
        return s.ap if isinstance(s, V) else s

    def tt(self, out, a, b, op, E=None):
        E = E or self.dve
        self.op(E, lambda: E.e.tensor_tensor(out.ap, a.ap, b.ap, op), [a, b], [out])

    def ts(self, out, a, s1, s2, op0, op1=None, E=None):
        E = E or self.dve
        if op1 is None:
            self.op(E, lambda: E.e.tensor_scalar(out.ap, a.ap, self._sv(s1), None, op0), [a, s1], [out])
        else:
            self.op(E, lambda: E.e.tensor_scalar(out.ap, a.ap, self._sv(s1), self._sv(s2), op0, op1),
                    [a, s1, s2], [out])

    def stt(self, out, a, s, b, op0, op1):
        self.op(self.dve, lambda: self.nc.vector.scalar_tensor_tensor(out.ap, a.ap, self._sv(s), b.ap, op0, op1),
                [a, s, b], [out])

    def ttr(self, out, a, b, op0, op1, init, accum):
        if op0 == ALU.mult and a is b:
            self.actf(out, a, AF.Square, accum=accum)
        else:
            self.tt(out, a, b, op0)
            self.red(accum, out, op1)

    def red(self, out, a, op):
        self.op(self.dve, lambda: self.nc.vector.tensor_reduce(out.ap, a.ap, AX, op), [a], [out])

    def scan(self, out, d0, d1, init):
        self.op(self.dve, lambda: self.nc.vector.tensor_tensor_scan(out.ap, d0.ap, d1.ap, self._sv(init), ALU.mult, ALU.add),
                [d0, d1, init], [out])

    def recip(self, out, a):
        self.op(self.dve, lambda: self.nc.vector.reciprocal(out.ap, a.ap), [a], [out])

    def cp(self, out, a, E=None):
        E = E or self.dve
        if E is self.act:
            self.op(E, lambda: self.nc.scalar.copy(out.ap, a.ap), [a], [out])
        else:
            self.op(E, lambda: E.e.tensor_copy(out.ap, a.ap), [a], [out])

    def memset(self, out, val, E=None):
        E = E or self.dve
        self.op(E, lambda: E.e.memset(out.ap, val), [], [out])

    def rsqrt(self, out, ssq, scale, eps):
        self.ts(out, ssq, scale, eps, ALU.mult, ALU.add)
        self.actf(out, out, AF.Sqrt)
        self.recip(out, out)

    def barrier(self):
        engs = (self.pe, self.act, self.dve, self.pool, self.sp)
        snap = [(X.sem, X.cnt, X.name) for X in engs if X.cnt > 0]
        dsn = [(self.dsem[i], self.dval[i], "d%d" % i) for i in range(self.nds) if self.dval[i] > 0]
        cc = [(self.ccsem, self.ccval, "cc")] if self.ccval > 0 else []
        for E in engs:
            for tok in snap + dsn + cc:
                if tok[2] != E.name:
                    self._wait(E, tok)

    def final_wait(self):
        E = self.sp
        for i in range(self.nds):
            if self.dval[i] > 0:
                self._wait(E, (self.dsem[i], self.dval[i], "d%d" % i))
        for X in (self.pe, self.act, self.dve, self.pool):
            if X.cnt > 0:
                self._wait(E, (X.sem, X.cnt, X.name))


def _consts():
    c = {}
    i = np.arange(128)
    same = (i[:, None] // 64) == (i[None, :] // 64)
    c["ident"] = np.eye(128, dtype=np.float32)
    c["ones"] = np.ones((128, 128), np.float32)
    c["triBD"] = (same & (i[:, None] <= i[None, :])).astype(np.float32)
    c["sel0"] = np.repeat((i[:, None] < 64), 128, 1).astype(np.float32)
    c["sel1"] = np.repeat((i[:, None] >= 64), 128, 1).astype(np.float32)
    c["mneg_ts_incl"] = np.where(same & (i[None, :] <= i[:, None]), 0.0, NEG).astype(np.float32)
    c["mpos_st_incl"] = np.where(same & (i[:, None] <= i[None, :]), 0.0, -NEG).astype(np.float32)
    c["mneg_st_incl"] = np.where(same & (i[:, None] <= i[None, :]), 0.0, NEG).astype(np.float32)
    c["mneg_st_strict"] = np.where(same & (i[:, None] < i[None, :]), 0.0, NEG).astype(np.float32)
    c["mpos_ts_strict"] = np.where(same & (i[None, :] < i[:, None]), 0.0, -NEG).astype(np.float32)
    c["m01_st_strict"] = (same & (i[:, None] < i[None, :])).astype(np.float32)
    c["m01_st_incl"] = (same & (i[:, None] <= i[None, :])).astype(np.float32)
    c["m01_ts_strict"] = (same & (i[None, :] < i[:, None])).astype(np.float32)
    for d in (1, 2, 3):
        sh = (i[:, None] == i[None, :] - d)
        c["shP%d" % d] = sh.astype(np.float32)
        c["shS%d" % d] = (sh & same).astype(np.float32)
    hs = np.zeros((6, 3, 128), np.float32)
    for d in (1, 2, 3):
        for t in range(d):
            hs[3 + t - d, d - 1, t] = 1.0
            hs[3 + 3 + t - d, d - 1, 64 + t] = 1.0
    c["hsel"] = hs.reshape(6, 384)
    tp = np.zeros((128, 32), np.float32)
    ts_ = np.zeros((128, 32), np.float32)
    for r in range(3):
        tp[125 + r, r] = 1.0
        ts_[61 + r, r] = 1.0
        ts_[125 + r, 3 + r] = 1.0
    c["tailP"] = tp
    c["tailS"] = ts_
    return c


CONST_ORDER = ["ident", "ones", "triBD", "sel0", "sel1", "mneg_ts_incl", "mpos_st_incl", "mneg_st_incl",
               "mneg_st_strict", "mpos_ts_strict", "m01_st_strict", "m01_st_incl", "m01_ts_strict",
               "shP1", "shP2", "shP3", "shS1", "shS2", "shS3"]


def build(SEQ, stop=None, skip=()):
    PT = SEQ // 4
    SL = PT + 256
    NT_P = PT // 128
    NT_S = NT_P + 2
    GT = 4 * SL
    nc = bass.Bass("TRN2", target_bir_lowering=False)
    k = KB(nc)
    ein = lambda n, s, dt=F32: k.dram(n, s, dt, kind="ExternalInput")
    eout = lambda n, s: k.dram(n, s, F32, kind="ExternalOutput")
    xg = ein("xg", [GT, D])
    xo = ein("xo", [SL, D])
    qmask = ein("qmask", [128, 4])
    cmat = ein("cmat", [len(CONST_ORDER), 128, 128])
    hsel_d = ein("hsel", [6, 384])
    tail_d = ein("tails", [2, 128, 32])
    w_in_d = ein("w_in", [DEPTH, D, NQ])
    w_out_d = ein("w_out", [DEPTH, D, D])
    w_gate_d = ein("w_gate", [DEPTH, D, DFF])
    w_up_d = ein("w_up", [DEPTH, D, DFF])
    w_down_d = ein("w_down", [DEPTH, DFF, D])
    w_glu_d = ein("w_glu", [DEPTH, 512, 512])
    gvec = ein("gvec", [DEPTH, 5, D])
    gcols = ein("gcols", [DEPTH, 2, 128, 16])
    pcol = ein("pcol", [DEPTH, 128, 32])
    prow = ein("prow", [DEPTH, 1, 3328])
    s5b = ein("s5b", [DEPTH, 2, 128, 4, 128])
    s5c = ein("s5c", [DEPTH, 2, 128, 4, 128])
    rw2 = ein("rw2", [DEPTH, 128, 128])
    rg2 = ein("rg2", [DEPTH, 128, 128])
    i_mCn = ein("i_mCn", [DEPTH, NS, 128, 129])
    i_mm = ein("i_mm", [DEPTH, NS, 128, 1])
    i_s5 = ein("i_s5", [DEPTH, NS, 128, 8])
    i_rS = ein("i_rS", [DEPTH, NS, 128, 64])
    i_rsh = ein("i_rsh", [DEPTH, NS, 640])
    i_gS = ein("i_gS", [DEPTH, NS, 128, 128])
    i_gcv = ein("i_gcv", [DEPTH, NS, 3, 384])
    y_own = eout("y_own", [SL, D])
    o_mCn = eout("o_mCn", [DEPTH, NS + 1, 128, 129])
    o_mm = eout("o_mm", [DEPTH, NS + 1, 128, 1])
    o_s5 = eout("o_s5", [DEPTH, NS + 1, 128, 8])
    o_rS = eout("o_rS", [DEPTH, NS + 1, 128, 64])
    o_rsh = eout("o_rsh", [DEPTH, NS + 1, 640])
    o_gS = eout("o_gS", [DEPTH, NS + 1, 128, 128])
    o_gcv = eout("o_gcv", [DEPTH, NS + 1, 3, 384])
    mixp = [k.dram("mixp%d" % l, [GT, 512]) for l in range(DEPTH)]
    mixa = [k.dram("mixa%d" % l, [4 * GT, 512]) for l in range(DEPTH)]
    x1own = k.dram("x1own", [SL, D])
    x1all = k.dram("x1all", [GT, D])
    xmid = k.dram("xmid", [SL, D])
    wo_s = [k.dram("wo_s%d" % l, [4, 128, 16, 512], BF16) for l in range(DEPTH)]
    wg_s = [k.dram("wg_s%d" % l, [44, 128, 16, 128], BF16) for l in range(DEPTH)]
    wu_s = [k.dram("wu_s%d" % l, [44, 128, 16, 128], BF16) for l in range(DEPTH)]
    wd_s = [k.dram("wd_s%d" % l, [4, 128, 44, 512], BF16) for l in range(DEPTH)]

    for i in range(7):
        k.psb.append(Tile(nc.alloc_psum_tensor("psb%d" % i, [128, 512], F32)))
    pst = Tile(nc.alloc_psum_tensor("pst", [128, 1024], BF16))

    C = {}
    cs = k.sb([128, len(CONST_ORDER), 128], name="consts")
    k.dma(cs.a, cmat.a.re("n p f -> p n f"))
    for i, n in enumerate(CONST_ORDER):
        C[n] = cs[:, i, :]
    ident = C["ident"]
    identb_t = k.sb([128, 128], BF16, "identb")
    k.cp(identb_t.a, ident)
    identb = identb_t.a
    hsel = k.sb([6, 384], name="hsel")
    k.dma(hsel.a, hsel_d.a)
    tails = k.sb([128, 2, 32], name="tails")
    k.dma(tails.a, tail_d.a.re("n p f -> p n f"))
    qm = k.sb([128, 4], name="qm")
    k.dma(qm.a, qmask.a)
    onescol = C["ones"][:, 0:1]

    xt = [k.sb([128, D], name="xt%d" % i) for i in range(2)]
    hb = k.sb([128, D], BF16, "hb")
    stg = [xt[0], xt[1]]
    stb = [hb, k.sb([128, 2048], BF16, "stb0")]
    junk = stb[1]
    cast_engs = [k.dve, k.act]
    cnt = [0]

    def castcopy(dst_v_fn, src_v, cw=2048):
        i = cnt[0] % 2
        cnt[0] += 1
        k.dma(stg[i][:, 0:cw], src_v)
        k.cp(stb[i][:, 0:cw], stg[i][:, 0:cw], E=cast_engs[i])
        dst_v_fn(stb[i])

    for l in range(DEPTH if "prep" not in skip else 0):
        for kc in range(16):
            castcopy(lambda sb_, kc=kc, l=l: k.dma(wo_s[l].a.re("n p c f -> p n c f")[:, :, kc, :],
                                                   sb_.a.re("p (n f) -> p n f", n=4)),
                     w_out_d[l, kc * 128:(kc + 1) * 128, :])
        for (src, dst) in ((w_gate_d, wg_s), (w_up_d, wu_s)):
            for kc in range(16):
                for cb in range(3):
                    c0 = cb * 2048
                    cw = min(2048, DFF - c0)
                    nf = cw // 128
                    def wr_(sb_, kc=kc, l=l, c0=c0, cw=cw, nf=nf, dst=dst):
                        for q0 in range(0, nf, 4):
                            q1 = min(nf, q0 + 4)
                            k.dma(dst[l].a.re("n p c f -> p n c f")[:, c0 // 128 + q0:c0 // 128 + q1, kc, :],
                                  sb_[:, q0 * 128:q1 * 128].re("p (n f) -> p n f", f=128))
                    castcopy(wr_,
                             src[l, kc * 128:(kc + 1) * 128, c0:c0 + cw], cw)
        for fc in range(44):
            castcopy(lambda sb_, fc=fc, l=l: k.dma(wd_s[l].a.re("n p c f -> p n c f")[:, :, fc, :],
                                                   sb_.a.re("p (n f) -> p n f", n=4)),
                     w_down_d[l, fc * 128:(fc + 1) * 128, :])

    if stop == "prep":
        k.final_wait()
        return nc
    col = lambda n="c": k.sb([128, 8], name=n)

    def T(shape=(128, 128), n="tmp"):
        return k.sb(list(shape), name=n)

    def tmp():
        if tpi[0] >= len(tmp_pool):
            tmp_pool.append(T(n="tp%d" % len(tmp_pool)))
        t = tmp_pool[tpi[0]]
        tpi[0] += 1
        return t

    def ntmp():
        t = npool[npi[0] % 8]
        npi[0] += 1
        return t

    def big():
        t = big_pool[bpi[0] % len(big_pool)]
        bpi[0] += 1
        return t

    def cols():
        t = col_pool[cpi[0] % len(col_pool)]
        cpi[0] += 1
        return t

    tmp_pool = []
    tpi = [0]
    npi = [0]
    bpi = [0]
    col_pool = [col("cp%d" % i) for i in range(24)]
    cpi = [0]

    def layer_setup(l):
        tpi[0] = 0
        k.dma(pc.a, pcol[l])
        k.dma(prb.a, prow[l, 0, :].pbc(128))
        k.dma(gpre.a, gcols[l, 0])
        k.dma(w2a2.a, rw2[l])
        k.dma(g2.a, rg2[l])
        for kc in range(16):
            i = cnt[0] % 2
            cnt[0] += 1
            k.dma(stg[i][:, 0:NQ], w_in_d[l, kc * 128:(kc + 1) * 128, :])
            k.cp(win[:, kc, :], stg[i][:, 0:NQ], E=cast_engs[i])
        c = cols()
        dt = c[:, 0:4]
        k.actf(dt, pc[:, 8:12], AF.Exp)
        c2 = cols()
        mag = c2[:, 0:4]
        k.tt(mag, pc[:, 0:4], dt, ALU.mult)
        k.actf(mag, mag, AF.Exp)
        th = c2[:, 4:8]
        k.tt(th, pc[:, 4:8], dt, ALU.mult)
        c3 = cols()
        sn, cn = c3[:, 0:4], c3[:, 4:8]
        k.ts(sn, th, 1.0 / 32, None, ALU.mult)
        k.ts(cn, th, 1.0 / 32, float(0.5 * np.pi), ALU.mult, ALU.add)
        k.actf(sn, sn, AF.Sin)
        k.actf(cn, cn, AF.Sin)
        for _ in range(5):
            dd_ = cols()
            k.tt(dd_[:, 0:4], cn, cn, ALU.mult)
            k.tt(dd_[:, 4:8], sn, sn, ALU.mult)
            k.tt(sn, sn, cn, ALU.mult)
            k.ts(sn, sn, 2.0, None, ALU.mult)
            k.tt(cn, dd_[:, 0:4], dd_[:, 4:8], ALU.subtract)
        lbr, lbi = s5_lb[:, 0, :], s5_lb[:, 1, :]
        k.tt(lbr, mag, cn, ALU.mult)
        k.tt(lbi, mag, sn, ALU.mult)
        c4 = cols()
        nr, den = c4[:, 0:4], c4[:, 4:8]
        k.ts(nr, lbr, -1.0, None, ALU.add)
        c5 = cols()
        t1, t2 = c5[:, 0:4], c5[:, 4:8]
        k.tt(t1, pc[:, 0:4], pc[:, 0:4], ALU.mult)
        k.tt(t2, pc[:, 4:8], pc[:, 4:8], ALU.mult)
        k.tt(den, t1, t2, ALU.add)
        k.recip(den, den)
        c6 = cols()
        fre, fim = c6[:, 0:4], c6[:, 4:8]
        k.tt(t1, nr, pc[:, 0:4], ALU.mult)
        k.tt(t2, lbi, pc[:, 4:8], ALU.mult)
        k.tt(fre, t1, t2, ALU.add)
        k.tt(fre, fre, den, ALU.mult)
        k.tt(t1, lbi, pc[:, 0:4], ALU.mult)
        k.tt(t2, nr, pc[:, 4:8], ALU.mult)
        k.tt(fim, t1, t2, ALU.subtract)
        k.tt(fim, fim, den, ALU.mult)
        braw = big()
        biraw = big()
        k.dma(braw.a.re("p (j f) -> p j f", j=4), s5b[l, 0])
        k.dma(biraw.a.re("p (j f) -> p j f", j=4), s5b[l, 1])
        for j in range(4):
            bre = ntmp()
            bim = ntmp()
            t3 = ntmp()
            k.ts(bre.a, braw[:, j * 128:(j + 1) * 128], fre[:, j:j + 1], None, ALU.mult)
            k.ts(t3.a, biraw[:, j * 128:(j + 1) * 128], fim[:, j:j + 1], None, ALU.mult)
            k.tt(bre.a, bre.a, t3.a, ALU.subtract)
            k.ts(bim.a, biraw[:, j * 128:(j + 1) * 128], fre[:, j:j + 1], None, ALU.mult)
            k.ts(t3.a, braw[:, j * 128:(j + 1) * 128], fim[:, j:j + 1], None, ALU.mult)
            k.tt(bim.a, bim.a, t3.a, ALU.add)
            for ri, src in ((0, bre), (1, bim)):
                p = k.ps()
                k.tr(p[:, 0:128], src.a, ident)
                k.cp(s5_Bb[:, ri, j, :], p[:, 0:128])
        k.dma(s5_Cb[:, 0, :, :], s5c[l, 0])
        k.dma(s5_Cb[:, 1, :, :], s5c[l, 1])
        k.ts(s5_Cb[:, 1, :, :], s5_Cb[:, 1, :, :], -1.0, None, ALU.mult)
        linv = cols()
        lir, lii = linv[:, 0:4], linv[:, 4:8]
        k.tt(t1, lbr, lbr, ALU.mult)
        k.tt(t2, lbi, lbi, ALU.mult)
        k.tt(t1, t1, t2, ALU.add)
        k.recip(t1, t1)
        k.tt(lir, lbr, t1, ALU.mult)
        k.tt(lii, lbi, t1, ALU.mult)
        k.ts(lii, lii, -1.0, None, ALU.mult)
        for (tab, br0, bi0) in ((s5_P, lbr, lbi), (s5_Pi, lir, lii)):
            pw = cols()
            pr_, pi_ = pw[:, 0:4], pw[:, 4:8]
            k.cp(pr_, br0)
            k.cp(pi_, bi0)
            k.memset(tab[:, 0, :, 0:1], 1.0)
            k.memset(tab[:, 1, :, 0:1], 0.0)
            n = 1
            while n < 64:
                for j in range(4):
                    a_r, a_i = tab[:, 0, j, 0:n], tab[:, 1, j, 0:n]
                    o_r, o_i = tab[:, 0, j, n:2 * n], tab[:, 1, j, n:2 * n]
                    tq = ntmp()
                    k.ts(tq[:, 0:n], a_i, pi_[:, j:j + 1], None, ALU.mult)
                    k.stt(o_r, a_r, pr_[:, j:j + 1], tq[:, 0:n], ALU.mult, ALU.subtract)
                    k.ts(tq[:, 0:n], a_i, pr_[:, j:j + 1], None, ALU.mult)
                    k.stt(o_i, a_r, pi_[:, j:j + 1], tq[:, 0:n], ALU.mult, ALU.add)
                sq = cols()
                k.tt(sq[:, 0:4], pr_, pr_, ALU.mult)
                k.tt(sq[:, 4:8], pi_, pi_, ALU.mult)
                nr2 = cols()
                k.tt(nr2[:, 0:4], sq[:, 0:4], sq[:, 4:8], ALU.subtract)
                k.tt(nr2[:, 4:8], pr_, pi_, ALU.mult)
                k.ts(pi_, nr2[:, 4:8], 2.0, None, ALU.mult)
                k.cp(pr_, nr2[:, 0:4])
                n *= 2
            k.cp(tab[:, :, :, 64:128], tab[:, :, :, 0:64])
        for s_ in (st_mC, st_mm, st_s5, st_rS, st_rsh, st_gS, st_gcv):
            k.memset(s_.a, 0.0)

    def halves():
        return ((0, slice(0, 64)), (1, slice(64, 128)))

    def mixer_tile(l, P, mix, S, sample):
        tpi[0] = 0
        PRI = {0: 0, 2: 1, 3: 2, 4: 3, 5: 4, 6: 5, 7: 6, 8: 7, 13: 8}
        PR = lambda r, w=128: prb[:, PRI[r] * 128:PRI[r] * 128 + w]
        sel = (C["sel0"], C["sel1"])

        c = cols()
        li, lf, bcs, acol = c[:, 0:1], c[:, 1:2], c[:, 2:3], c[:, 3:4]
        k.ts(li, P[:, O_MI:O_MI + 1], pc[:, 13:14], None, ALU.add)
        e = c[:, 4:5]
        k.ts(e, P[:, O_MF:O_MF + 1], pc[:, 14:15], -1.0, ALU.add, ALU.mult)
        k.actf(e, e, AF.Exp)
        k.actf(e, e, AF.Ln, bias=onescol)
        k.ts(lf, e, -1.0, None, ALU.mult)
        p = k.ps()
        k.mm(p[:, 0:1], C["triBD"], lf)
        k.tt(acol, li, p[:, 0:1], ALU.subtract)
        k.cp(bcs, p[:, 0:1])
        da = tmp()
        k.ts(da.a, ident, acol, None, ALU.mult)
        pA = k.ps()
        k.mm(pA[:, 0:128], C["ones"], da.a)
        cm = c[:, 5:6]
        cmL2 = c[:, 6:8]
        k.red(cmL2, pA[:, 0:128].re("p (h s) -> p h s", h=2), ALU.max)
        jk = tmp()
        k.ttr(jk.a, pA[:, 0:128], C["mneg_ts_incl"], ALU.add, ALU.max, NEG, cm)
        dcm = tmp()
        k.ts(dcm.a, ident, cm, None, ALU.mult)
        pB = k.ps()
        k.mm(pB[:, 0:128], C["ones"], dcm.a)
        z = tmp()
        k.stt(z.a, pB[:, 0:128], acol, C["mpos_st_incl"], ALU.subtract, ALU.max)
        DT = tmp()
        k.actf(DT.a, z.a, AF.Exp, scale=-1.0)
        qT, kT = tmp(), tmp()
        for (dst, off) in ((qT, O_MQ), (kT, O_MK)):
            pp = k.ps()
            k.tr(pp[:, 0:128], P[:, off:off + 128], ident)
            k.cp(dst.a, pp[:, 0:128], E=k.act)
        k.ts(qT.a, qT.a, 128 ** -0.5, None, ALU.mult)
        pkq = k.ps()
        k.mm(pkq[:, 0:128], kT.a, qT.a)
        SmT = tmp()
        k.tt(SmT.a, pkq[:, 0:128], DT.a, ALU.mult)
        k.cp(vaug[:, 0:128], P[:, O_MV:O_MV + 128], E=k.pool)
        k.memset(vaug[:, 128:129], 1.0, E=k.pool)
        pin = k.ps()
        k.mm(pin[:, 0:129], SmT.a, vaug.a)
        k.cp(intra.a, pin[:, 0:129], E=k.act)
        hnum = tmp()
        for hf, rs in halves():
            Cin, Cout = S[hf]["mC"]
            min_, mout = S[hf]["mm"]
            cc = cols()
            cmL, bL, ML, Mt, al, om = cc[:, 0:1], cc[:, 1:2], cc[:, 2:3], cc[:, 3:4], cc[:, 4:5], cc[:, 5:6]
            k.cp(cmL, cmL2[:, hf:hf + 1])
            pb = k.ps()
            k.mm(pb[:, 0:1], sel[hf], lf)
            k.cp(bL, pb[:, 0:1])
            k.tt(ML, cmL, min_.a, ALU.max)
            k.tt(Mt[rs], cm[rs], min_[rs, :], ALU.max)
            k.tt(al[rs], cm[rs], Mt[rs], ALU.subtract)
            k.actf(al[rs], al[rs], AF.Exp)
            k.tt(om[rs], min_[rs, :], Mt[rs], ALU.subtract)
            k.actf(om[rs], om[rs], AF.Exp)
            pq = k.ps()
            k.mm(pq[:, 0:129], qT.a, Cin.a)
            t1 = mt1
            k.ts(t1[rs, :], pq[rs, 0:129], om[rs], None, ALU.mult)
            k.stt(t1[rs, :], intra[rs, :], al[rs], t1[rs, :], ALU.mult, ALU.add)
            dd = cols()
            dn, ex = dd[:, 0:1], dd[:, 1:2]
            k.ts(dn[rs], t1[rs, 128:129], -1.0, None, ALU.mult)
            k.tt(dn[rs], dn[rs], t1[rs, 128:129], ALU.max)
            k.tt(ex[rs], bcs[rs], Mt[rs], ALU.add)
            k.actf(ex[rs], ex[rs], AF.Exp, scale=-1.0)
            k.tt(dn[rs], dn[rs], ex[rs], ALU.max)
            k.recip(dn[rs], dn[rs])
            k.ts(hnum[rs, :], t1[rs, 0:128], dn[rs], None, ALU.mult)
            wk, sc = dd[:, 2:3], dd[:, 3:4]
            k.tt(wk[rs], acol[rs], ML[rs], ALU.subtract)
            k.actf(wk[rs], wk[rs], AF.Exp)
            k.tt(sc, min_.a, ML, ALU.subtract)
            k.actf(sc, sc, AF.Exp)
            kw = tmp()
            k.ts(kw[rs, :], P[rs, O_MK:O_MK + 128], wk[rs], None, ALU.mult)
            pd = k.ps()
            k.mm(pd[:, 0:129], kw[rs, :], vaug[rs, :])
            k.stt(Cout.a, Cin.a, sc, pd[:, 0:129], ALU.mult, ALU.add)
            k.tt(mout.a, bL, ML, ALU.add)
        cst = cols()
        mu, var = cst[:, 0:1], cst[:, 1:2]
        k.red(mu, hnum.a, ALU.add)
        k.ts(mu, mu, -1.0 / 128, None, ALU.mult)
        hc = tmp()
        k.ts(hc.a, hnum.a, mu, None, ALU.add)
        sqj = tmp()
        k.actf(sqj.a, hc.a, AF.Square, accum=var)
        k.rsqrt(var, var, 1.0 / 128, EPS)
        sg = tmp()
        k.actf(sg.a, P[:, O_MO:O_MO + 128], AF.Sigmoid)
        k.stt(hc.a, hc.a, var, PR(0), ALU.mult, ALU.mult)
        k.tt(mix[:, 0:128], hc.a, sg.a, ALU.mult)

        if "m1" in skip:
            return
        tpi[0] = 0
        pu = k.ps()
        k.tr(pu[:, 0:128], P[:, O_SU:O_SU + 128], ident)
        uT = tmp()
        k.cp(uT.a, pu[:, 0:128], E=k.act)
        BU = [big(), big()]
        for ri in range(2):
            pp = k.ps()
            for j in range(4):
                k.mm(pp[:, j * 128:(j + 1) * 128], s5_Bb[:, ri, j, :], uT.a)
            k.cp(BU[ri].a, pp.a, E=k.act)
        Xr, Xi, tb = big(), big(), big()
        Pir = s5_Pi[:, 0, :, :].re("p j t -> p (j t)")
        Pii = s5_Pi[:, 1, :, :].re("p j t -> p (j t)")
        k.tt(Xr.a, BU[0].a, Pir, ALU.mult)
        k.tt(tb.a, BU[1].a, Pii, ALU.mult, E=k.pool)
        k.tt(Xr.a, Xr.a, tb.a, ALU.subtract)
        k.tt(Xi.a, BU[0].a, Pii, ALU.mult)
        k.tt(tb.a, BU[1].a, Pir, ALU.mult, E=k.pool)
        k.tt(Xi.a, Xi.a, tb.a, ALU.add)
        Gr, Gi = big(), big()
        Hr, Hi = BU[0], BU[1]
        Ptr = s5_P[:, 0, :, :]
        Pti = s5_P[:, 1, :, :]
        for hf, rs in halves():
            sin_, sout = S[hf]["s5"]
            ts_ = slice(hf * 64, (hf + 1) * 64)
            ci = cols()
            ir, ii_, t1_, t2_ = ci[:, 0:4], ci[:, 4:8], cols(), cols()
            k.tt(t1_[:, 0:4], sin_[:, 0, :], s5_lb[:, 0, :], ALU.mult)
            k.tt(t2_[:, 0:4], sin_[:, 1, :], s5_lb[:, 1, :], ALU.mult)
            k.tt(ir, t1_[:, 0:4], t2_[:, 0:4], ALU.subtract)
            k.tt(t1_[:, 4:8], sin_[:, 0, :], s5_lb[:, 1, :], ALU.mult)
            k.tt(t2_[:, 4:8], sin_[:, 1, :], s5_lb[:, 0, :], ALU.mult)
            k.tt(ii_, t1_[:, 4:8], t2_[:, 4:8], ALU.add)
            for j in range(4):
                fs = slice(j * 128 + hf * 64, j * 128 + hf * 64 + 64)
                k.scan(Gr[:, fs], C["ones"][:, 0:64], Xr[:, fs], ir[:, j:j + 1])
                k.scan(Gi[:, fs], C["ones"][:, 0:64], Xi[:, fs], ii_[:, j:j + 1])
            G3r = Gr.a.re("p (j t) -> p j t", j=4)[:, :, ts_]
            G3i = Gi.a.re("p (j t) -> p j t", j=4)[:, :, ts_]
            H3r = Hr.a.re("p (j t) -> p j t", j=4)[:, :, ts_]
            H3i = Hi.a.re("p (j t) -> p j t", j=4)[:, :, ts_]
            T3 = tb.a.re("p (j t) -> p j t", j=4)[:, :, ts_]
            k.tt(H3r, G3r, Ptr[:, :, ts_], ALU.mult)
            k.tt(T3, G3i, Pti[:, :, ts_], ALU.mult)
            k.tt(H3r, H3r, T3, ALU.subtract)
            k.tt(H3i, G3r, Pti[:, :, ts_], ALU.mult)
            k.tt(T3, G3i, Ptr[:, :, ts_], ALU.mult)
            k.tt(H3i, H3i, T3, ALU.add)
            last = hf * 64 + 63
            k.cp(sout[:, 0, :], Hr.a.re("p (j t) -> p j t", j=4)[:, :, last])
            k.cp(sout[:, 1, :], Hi.a.re("p (j t) -> p j t", j=4)[:, :, last])
        py = k.ps()
        n_ = 0
        for ri, Hh in ((0, Hr), (1, Hi)):
            for j in range(4):
                k.mm(py[:, 0:128], s5_Cb[:, ri, j, :], Hh[:, j * 128:(j + 1) * 128], start=(n_ == 0), stop=(n_ == 7))
                n_ += 1
        yT = tmp()
        k.stt(yT.a, uT.a, pc[:, 12:13], py[:, 0:128], ALU.mult, ALU.add)
        g1 = tmp()
        k.tt(g1.a, yT.a, yT.a, ALU.mult)
        k.ts(g1.a, g1.a, 0.044715, 1.0, ALU.mult, ALU.add)
        k.tt(g1.a, g1.a, yT.a, ALU.mult)
        k.actf(g1.a, g1.a, AF.Sigmoid, scale=1.5957691216057308)
        k.tt(yT.a, yT.a, g1.a, ALU.mult)
        pyt = k.ps()
        k.tr(pyt[:, 0:128], yT.a, ident)
        k.cp(mix[:, 128:256], pyt[:, 0:128], E=k.act)

        if "m2" in skip:
            return
        def shifted(pso, src_v, width, d, halo_v, hrows):
            sh = C[("shS%d" if sample else "shP%d") % d]
            k.mm(pso, sh, src_v, start=True, stop=False)
            k.mm(pso, hsel[0:hrows, (d - 1) * 128:d * 128] if hrows == 6 else hsel1[0:2, :], halo_v, start=False, stop=True)

        tpi[0] = 0
        R0 = O_R
        halo_in = S[0]["rsh_in"]
        for (c0, c1) in ((0, 512), (512, 640)):
            pp = k.ps()
            shifted(pp[:, 0:c1 - c0], P[:, R0 + c0:R0 + c1], c1 - c0, 1, halo_in[0:2, c0:c1], 2)
            k.tt(sh1[:, c0:c1], pp[:, 0:c1 - c0], P[:, R0 + c0:R0 + c1], ALU.subtract)
        k.tt(sh1.a, sh1.a, prb[:, 1152:1792], ALU.mult)
        k.tt(sh1.a, sh1.a, P[:, R0:R0 + 640], ALU.add)
        xr, xk, xv = sh1[:, 0:128], sh1[:, 128:256], sh1[:, 256:384]
        pl1 = k.ps()
        k.tr(pl1[:, 0:128], sh1[:, 384:512], ident)
        l1 = tmp()
        k.actf(l1[0:64, :], pl1[0:64, 0:128], AF.Tanh)
        k.cp(l1[64:128, :], pl1[64:128, 0:128], E=k.act)
        pl2 = k.ps()
        k.tr(pl2[:, 0:128], sh1[:, 512:640], ident)
        l2 = tmp()
        k.actf(l2.a, pl2[:, 0:128], AF.Sigmoid)
        pw = k.ps()
        k.mm(pw[:, 0:128], l1[0:64, :], w2a2[0:64, :])
        pa = k.ps()
        k.mm(pa[:, 0:128], l1[64:128, :], w2a2[64:128, :])
        pg = k.ps()
        k.mm(pg[:, 0:128], l2.a, g2.a)
        ld = tmp()
        k.tt(ld.a, pw[:, 0:128], PR(2), ALU.add)
        k.actf(ld.a, ld.a, AF.Sigmoid)
        k.ts(ld.a, ld.a, -float(np.exp(-0.5)), None, ALU.mult)
        av = tmp()
        k.tt(av.a, pa[:, 0:128], PR(3), ALU.add)
        k.actf(av.a, av.a, AF.Sigmoid)
        gg = tmp()
        k.cp(gg.a, pg[:, 0:128], E=k.act)
        if "r1" in skip:
            return
        kk = tmp()
        k.tt(kk.a, xk, PR(4), ALU.mult)
        sq = tmp()
        k.tt(sq.a, kk.a, kk.a, ALU.mult)
        cr = cols()
        k.red(cr[:, 0:2], sq.a.re("p (h c) -> p h c", h=2), ALU.add)
        k.rsqrt(cr[:, 0:2], cr[:, 0:2], 1.0, 1e-6)
        k.tt(kk.a.re("p (h c) -> p h c", h=2), kk.a.re("p (h c) -> p h c", h=2),
             cr[:, 0:2].re("p (h o) -> p h o", o=1).bc([128, 2, 64]), ALU.mult)
        kf = tmp()
        k.ts(kf.a, av.a, -1.0, None, ALU.add)
        k.tt(kf.a, kf.a, PR(5), ALU.mult)
        k.ts(kf.a, kf.a, 1.0, None, ALU.add)
        k.tt(kf.a, kf.a, xk, ALU.mult)
        bn = tmp()
        k.tt(bn.a, xr, kf.a, ALU.mult)
        k.tt(bn.a, bn.a, PR(6), ALU.mult)
        k.red(cr[:, 2:4], bn.a.re("p (h c) -> p h c", h=2), ALU.add)
        pgm = k.ps()
        k.mm(pgm[:, 0:128], C["triBD"], ld.a)
        Gm, Gi_, Gp = tmp(), tmp(), tmp()
        k.actf(Gm.a, pgm[:, 0:128], AF.Exp)
        k.actf(Gi_.a, pgm[:, 0:128], AF.Exp, scale=-1.0)
        k.tt(Gp.a, pgm[:, 0:128], ld.a, ALU.subtract)
        k.actf(Gp.a, Gp.a, AF.Exp)
        at, bt, kt, rt = tmp(), tmp(), tmp(), tmp()
        k.stt(at.a, kk.a, -1.0, Gp.a, ALU.mult, ALU.mult)
        k.tt(bt.a, kk.a, av.a, ALU.mult)
        k.tt(bt.a, bt.a, Gi_.a, ALU.mult)
        k.tt(kt.a, kf.a, Gi_.a, ALU.mult)
        k.tt(rt.a, xr, Gm.a, ALU.mult)
        if "r2" in skip:
            return
        y_r = tmp()
        fm = {}
        for nm, src in (("a", at), ("b", bt), ("k", kt), ("r", rt)):
            pp = k.ps()
            k.tr(pp[:, 0:128], src.a, ident)
            d_ = tmp()
            k.cp(d_.a, pp[:, 0:128], E=k.act)
            fm[nm] = d_
        tbase = tpi[0]
        for h in range(2):
            tpi[0] = tbase
            hs_ = slice(h * 64, (h + 1) * 64)
            hp = hs_
            pn, pnt, pak, prb_, prk = k.ps(), k.ps(), k.ps(), k.ps(), k.ps()
            k.mm(pn[:, 0:128], fm["b"][hp, :], fm["a"][hp, :])
            k.mm(pnt[:, 0:128], fm["a"][hp, :], fm["b"][hp, :])
            k.mm(pak[:, 0:128], fm["k"][hp, :], fm["a"][hp, :])
            k.mm(prb_[:, 0:128], fm["b"][hp, :], fm["r"][hp, :])
            k.mm(prk[:, 0:128], fm["k"][hp, :], fm["r"][hp, :])
            N, NT, AkT, RBT, RKT = tmp(), tmp(), tmp(), tmp(), tmp()
            k.tt(N.a, pn[:, 0:128], C["m01_st_strict"], ALU.mult)
            k.tt(NT.a, pnt[:, 0:128], C["m01_ts_strict"], ALU.mult)
            k.tt(AkT.a, pak[:, 0:128], C["m01_st_strict"], ALU.mult)
            k.tt(RBT.a, prb_[:, 0:128], C["m01_st_incl"], ALU.mult)
            k.tt(RKT.a, prk[:, 0:128], C["m01_st_incl"], ALU.mult)
            if "r3" in skip:
                return
            TT = neumann(N, NT)
            vh = xv[:, hs_]
            pav = k.ps()
            k.mm(pav[:, 0:64], AkT.a, vh)
            AkV = tmp()
            k.cp(AkV[:, 0:64], pav[:, 0:64], E=k.act)
            pw1 = k.ps()
            k.mm(pw1[hp, 0:128], at[:, hs_], TT.a)
            W1T = tmp()
            k.cp(W1T[hp, :], pw1[hp, 0:128], E=k.act)
            pU = k.ps()
            k.mm(pU[:, 0:64], TT.a, AkV[:, 0:64])
            U0 = tmp()
            k.cp(U0[:, 0:64], pU[:, 0:64])
            pY0 = k.ps()
            k.mm(pY0[:, 0:64], RKT.a, vh)
            Y0 = tmp()
            k.cp(Y0[:, 0:64], pY0[:, 0:64], E=k.act)
            if "r4" in skip:
                return
            U = tmp()
            for hf, rs in halves():
                Sin, Sout = S[hf]["rS"]
                pUs = k.ps()
                k.mm(pUs[rs, 0:64], W1T[hp, rs], Sin[hp, :])
                k.tt(U[rs, 0:64], U0[rs, 0:64], pUs[rs, 0:64], ALU.add)
                if "r4a" in skip:
                    return
                pYa, pYb = k.ps(), k.ps()
                k.mm(pYa[rs, 0:64], RBT[rs, rs], U[rs, 0:64])
                k.mm(pYb[rs, 0:64], fm["r"][hp, rs], Sin[hp, :])
                k.tt(y_r[rs, hs_], Y0[rs, 0:64], pYa[rs, 0:64], ALU.add)
                k.tt(y_r[rs, hs_], y_r[rs, hs_], pYb[rs, 0:64], ALU.add)
                if "r4b" in skip:
                    return
                pS = k.ps()
                k.mm(pS[hp, 0:64], bt[rs, hs_], U[rs, 0:64], start=True, stop=False)
                k.mm(pS[hp, 0:64], kt[rs, hs_], vh[rs, :], start=False, stop=True)
                if "r4c" in skip:
                    return
                pgl = k.ps()
                k.mm(pgl[hp, 0:1], ld[rs, hs_], onescol[rs, :])
                gl_ = cols()
                k.actf(gl_[hp, 0:1], pgl[hp, 0:1], AF.Exp)
                if "r4d" in skip:
                    return
                tS = tmp()
                k.ts(tS[hp, 0:64], Sin[hp, :], gl_[hp, 0:1], None, ALU.mult)
                k.stt(Sout[hp, :], pS[hp, 0:64], gl_[hp, 0:1], tS[hp, 0:64], ALU.mult, ALU.add)
        if "r5" in skip:
            return
        k.red(cr[:, 4:6], y_r.a.re("p (h c) -> p h c", h=2), ALU.add)
        k.ts(cr[:, 4:6], cr[:, 4:6], -1.0 / 64, None, ALU.mult)
        yc = tmp()
        k.tt(yc.a.re("p (h c) -> p h c", h=2), y_r.a.re("p (h c) -> p h c", h=2),
             cr[:, 4:6].re("p (h o) -> p h o", o=1).bc([128, 2, 64]), ALU.add)
        k.tt(sq.a, yc.a, yc.a, ALU.mult)
        k.red(cr[:, 6:8], sq.a.re("p (h c) -> p h c", h=2), ALU.add)
        k.rsqrt(cr[:, 6:8], cr[:, 6:8], 1.0 / 64, 64e-5)
        k.tt(yc.a.re("p (h c) -> p h c", h=2), yc.a.re("p (h c) -> p h c", h=2),
             cr[:, 6:8].re("p (h o) -> p h o", o=1).bc([128, 2, 64]), ALU.mult)
        k.tt(yc.a, yc.a, PR(7), ALU.mult)
        k.tt(yc.a, yc.a, PR(8), ALU.add)
        if "r5b" in skip:
            return
        bv = tmp()
        k.tt(bv.a.re("p (h c) -> p h c", h=2), xv.re("p (h c) -> p h c", h=2),
             cr[:, 2:4].re("p (h o) -> p h o", o=1).bc([128, 2, 64]), ALU.mult)
        k.tt(yc.a, yc.a, bv.a, ALU.add)
        k.tt(mix[:, 256:384], yc.a, gg.a, ALU.mult)
        if "r6" in skip:
            return
        halo_out = S[0]["rsh_out"]
        for (c0, c1) in ((0, 512), (512, 640)):
            pp = k.ps()
            k.mm(pp[0:32, 0:c1 - c0], tsel_r(sample), P[:, R0 + c0:R0 + c1])
            k.cp(halo_out[0:2, c0:c1], pp[0:2, 0:c1 - c0], E=k.act)

        if "m3" in skip:
            return
        tpi[0] = 0
        gin = S[0]["gcv_in"]
        gout = S[0]["gcv_out"]
        acc = gacc
        raw = P[:, O_GQ:O_GQ + 384]
        k.tt(acc.a, raw, prb[:, 1792 + 3 * 384:1792 + 4 * 384], ALU.mult)
        for d in (1, 2, 3):
            pp = k.ps()
            shifted(pp[:, 0:384], raw, 384, d, gin[0:6, :], 6)
            t_ = gtmp
            k.tt(t_.a, pp[:, 0:384], prb[:, 1792 + (3 - d) * 384:1792 + (4 - d) * 384], ALU.mult)
            k.tt(acc.a, acc.a, t_.a, ALU.add)
        pp = k.ps()
        k.mm(pp[0:32, 0:384], tails[:, 1 if sample else 0, :], raw)
        k.cp(gout[0:6, :], pp[0:6, 0:384], E=k.act)
        k.actf(acc.a, acc.a, AF.Silu)
        gq, gk, gv = acc[:, 0:128], acc[:, 128:256], acc[:, 256:384]
        cg = cols()
        jq = tmp()
        k.ttr(jq.a, gq, gq, ALU.mult, ALU.add, 0.0, cg[:, 0:1])
        k.ttr(jq.a, gk, gk, ALU.mult, ALU.add, 0.0, cg[:, 1:2])
        k.rsqrt(cg[:, 0:2], cg[:, 0:2], 1.0, 1e-6)
        k.ts(cg[:, 0:1], cg[:, 0:1], 128 ** -0.5, None, ALU.mult)
        qn, kn = tmp(), tmp()
        k.ts(qn.a, gq, cg[:, 0:1], None, ALU.mult)
        k.ts(kn.a, gk, cg[:, 1:2], None, ALU.mult)
        xg_, ab, gcol, beta = cg[:, 2:3], cg[:, 3:4], cg[:, 4:5], cg[:, 5:6]
        k.ts(xg_, P[:, O_GA:O_GA + 1], pc[:, 16:17], None, ALU.add)
        k.ts(ab, xg_, -1.0, None, ALU.mult)
        k.tt(ab, ab, xg_, ALU.min)
        k.actf(ab, ab, AF.Exp)
        k.actf(ab, ab, AF.Ln, bias=onescol)
        k.ts(xg_, xg_, 0.0, None, ALU.max)
        k.tt(xg_, xg_, ab, ALU.add)
        k.actf(gcol, pc[:, 15:16], AF.Exp)
        k.tt(gcol, gcol, xg_, ALU.mult)
        k.ts(gcol, gcol, -1.0, None, ALU.mult)
        k.actf(beta, P[:, O_GB:O_GB + 1], AF.Sigmoid)
        pG = k.ps()
        k.mm(pG[:, 0:1], C["triBD"], gcol)
        Gc, eG = cg[:, 6:7], cg[:, 7:8]
        k.cp(Gc, pG[:, 0:1])
        k.actf(eG, Gc, AF.Exp)
        dG = tmp()
        k.ts(dG.a, ident, Gc, None, ALU.mult)
        pGb = k.ps()
        k.mm(pGb[:, 0:128], C["ones"], dG.a)
        z1, z2, z3 = tmp(), tmp(), tmp()
        k.stt(z1.a, pGb[:, 0:128], Gc, C["mneg_st_strict"], ALU.subtract, ALU.min)
        k.actf(z1.a, z1.a, AF.Exp)
        k.stt(z2.a, pGb[:, 0:128], Gc, C["mneg_st_incl"], ALU.subtract, ALU.min)
        k.actf(z2.a, z2.a, AF.Exp)
        k.stt(z3.a, pGb[:, 0:128], Gc, C["mpos_ts_strict"], ALU.subtract, ALU.max)
        k.actf(z3.a, z3.a, AF.Exp, scale=-1.0)
        kb = tmp()
        k.ts(kb.a, kn.a, beta, None, ALU.mult)
        qg = tmp()
        k.ts(qg.a, qn.a, eG, None, ALU.mult)
        fT = {}
        for nm, src in (("k", kn), ("kb", kb), ("q", qn), ("qg", qg)):
            pp = k.ps()
            k.tr(pp[:, 0:128], src.a, ident)
            d_ = tmp()
            k.cp(d_.a, pp[:, 0:128], E=k.act)
            fT[nm] = d_
        pn, pnt, pqk = k.ps(), k.ps(), k.ps()
        k.mm(pn[:, 0:128], fT["k"].a, fT["kb"].a)
        k.mm(pnt[:, 0:128], fT["kb"].a, fT["k"].a)
        k.mm(pqk[:, 0:128], fT["k"].a, fT["q"].a)
        N, NT, QKT = tmp(), tmp(), tmp()
        k.stt(N.a, pn[:, 0:128], -1.0, z1.a, ALU.mult, ALU.mult)
        k.stt(NT.a, pnt[:, 0:128], -1.0, z3.a, ALU.mult, ALU.mult)
        k.tt(QKT.a, pqk[:, 0:128], z2.a, ALU.mult)
        TT = neumann(N, NT)
        rhs = grhs
        k.ts(rhs[:, 0:128], gv, beta, None, ALU.mult)
        k.ts(rhs[:, 128:256], kb.a, eG, None, ALU.mult)
        psv = k.ps()
        k.mm(psv[:, 0:128], TT.a, rhs[:, 0:128])
        solV = tmp()
        k.cp(solV.a, psv[:, 0:128], E=k.act)
        pkt = k.ps()
        k.mm(pkt[:, 0:128], rhs[:, 128:256], TT.a)
        solKT = tmp()
        k.cp(solKT.a, pkt[:, 0:128], E=k.act)
        U = tmp()
        o_g = tmp()
        for hf, rs in halves():
            Sin, Sout = S[hf]["gS"]
            pu_ = k.ps()
            k.mm(pu_[:, 0:128], solKT.a, Sin.a)
            k.tt(U[rs, :], solV[rs, :], pu_[rs, 0:128], ALU.subtract)
            poa, pob = k.ps(), k.ps()
            k.mm(poa[rs, 0:128], QKT[rs, rs], U[rs, :])
            k.mm(pob[:, 0:128], fT["qg"].a, Sin.a)
            k.cp(o_g[rs, :], poa[rs, 0:128], E=k.act)
            k.tt(o_g[rs, :], o_g[rs, :], pob[rs, 0:128], ALU.add)
            pgl = k.ps()
            k.mm(pgl[:, 0:1], sel[hf], gcol)
            cgl = cols()
            GL, eGL, kdc = cgl[:, 0:1], cgl[:, 1:2], cgl[:, 2:3]
            k.cp(GL, pgl[:, 0:1])
            k.actf(eGL, GL, AF.Exp)
            k.tt(kdc[rs], GL[rs], Gc[rs], ALU.subtract)
            k.actf(kdc[rs], kdc[rs], AF.Exp)
            kd = tmp()
            k.ts(kd[rs, :], kn[rs, :], kdc[rs], None, ALU.mult)
            pS = k.ps()
            k.mm(pS[:, 0:128], kd[rs, :], U[rs, :])
            k.stt(Sout.a, Sin.a, eGL, pS[:, 0:128], ALU.mult, ALU.add)
        k.actf(jq.a, o_g.a, AF.Square, accum=cg[:, 0:1])
        k.rsqrt(cg[:, 0:1], cg[:, 0:1], 1.0 / 128, EPS)
        k.stt(o_g.a, o_g.a, cg[:, 0:1], PR(13), ALU.mult, ALU.mult)
        sz = tmp()
        k.actf(sz.a, P[:, O_GZ:O_GZ + 128], AF.Silu)
        k.tt(mix[:, 384:512], o_g.a, sz.a, ALU.mult)

    def neumann(N, NT):
        Pm = ntmp()
        k.tt(Pm.a, N.a, ident, ALU.add)
        cur, curT = N, NT
        for j in range(1, 6):
            pnT = k.ps()
            k.mm(pnT[:, 0:128], cur.a, curT.a)
            nT = ntmp()
            k.cp(nT.a, pnT[:, 0:128], E=k.act)
            if j < 5:
                pn_ = k.ps()
                k.mm(pn_[:, 0:128], curT.a, cur.a)
                n_ = ntmp()
                k.cp(n_.a, pn_[:, 0:128])
            pP = k.ps()
            k.mm(pP[:, 0:128], nT.a, Pm.a)
            Pn = ntmp()
            k.tt(Pn.a, pP[:, 0:128], Pm.a, ALU.add)
            Pm = Pn
            if j < 5:
                cur, curT = n_, nT
        return Pm

    hsel1 = k.sb([2, 128], name="hsel1")
    k.memset(hsel1.a, 0.0)
    k.cp(hsel1[0:1, 0:1], onescol[0:1, :])
    tselP = k.sb([128, 32], name="tselP")
    tselS = k.sb([128, 32], name="tselS")
    k.memset(tselP.a, 0.0)
    k.memset(tselS.a, 0.0)
    k.cp(tselP[0:128, 0:1], tails[:, 0, 2:3])
    k.cp(tselS[0:128, 0:1], tails[:, 1, 2:3])
    k.cp(tselS[0:128, 1:2], tails[:, 1, 5:6])
    hsel1b = k.sb([2, 128], name="hsel1b")
    k.dma(hsel1b.a, hsel_d[2:6:3, 0:128])
    k.cp(hsel1.a, hsel1b.a)

    def tsel_r(sample):
        return (tselS if sample else tselP).a

    def phase_a_and_mix(l, xsrc):
        ti = 0
        for s in range(4):
            for j in range(NT_S):
                sample = j >= NT_P
                row0 = s * SL + j * 128
                x = xt[ti % 2]
                P = Pj[0]
                mix = mixt[0]
                ti += 1
                xrow = row0 if l == 0 else (j * 4 + s) * 128
                k.dma(x.a, xsrc[xrow:xrow + 128, :])
                cs_ = cols()
                k.actf(junk.a, x.a, AF.Square, accum=cs_[:, 0:1])
                k.rsqrt(cs_[:, 0:1], cs_[:, 0:1], 1.0 / D, EPS)
                k.ts(hb.a, x.a, cs_[:, 0:1], None, ALU.mult)
                for half in range(2):
                    for c8 in range(8):
                        kc = half * 8 + c8
                        k.tr(pst[:, c8 * 128:(c8 + 1) * 128], hb[:, kc * 128:(kc + 1) * 128], identb)
                    for c8 in range(8):
                        kc = half * 8 + c8
                        k.actf(hT[:, kc, :], pst[:, c8 * 128:(c8 + 1) * 128], AF.Copy, scale=gpre[:, kc:kc + 1])
                for nb in range(4):
                    pp = k.ps()
                    for kc in range(16):
                        k.mm(pp[:, 0:449], hT[:, kc, :], win[:, kc, nb * 449:(nb + 1) * 449], start=(kc == 0), stop=(kc == 15))
                    k.cp(P[:, nb * 449:(nb + 1) * 449], pp[:, 0:449], E=(k.act if nb % 2 else k.dve))
                if not sample:
                    S = [dict(mC=(st_mC, st_mC), mm=(st_mm, st_mm), s5=(st_s5, st_s5), rS=(st_rS, st_rS), gS=(st_gS, st_gS))
                         for _ in range(2)]
                    S[0].update(rsh_in=st_rsh.a, rsh_out=st_rsh.a, gcv_in=st_gcv.a, gcv_out=st_gcv.a)
                else:
                    S = []
                    for hf in range(2):
                        q = 4 * s + 2 * (j - NT_P) + hf
                        k.dma(ss_mC[hf].a, i_mCn[l, q])
                        k.dma(ss_mm[hf].a, i_mm[l, q])
                        k.dma(ss_s5[hf].a.re("p r j -> p (r j)"), i_s5[l, q])
                        k.dma(ss_rS[hf].a, i_rS[l, q])
                        k.dma(ss_rsh[hf:hf + 1, :], i_rsh[l, q:q + 1, :])
                        k.dma(ss_gS[hf].a, i_gS[l, q])
                        k.dma(ss_gcv[3 * hf:3 * hf + 3, :], i_gcv[l, q])
                        S.append(dict(mC=(ss_mC[hf], ss_mC[hf]), mm=(ss_mm[hf], ss_mm[hf]), s5=(ss_s5[hf], ss_s5[hf]),
                                      rS=(ss_rS[hf], ss_rS[hf]), gS=(ss_gS[hf], ss_gS[hf])))
                    S[0].update(rsh_in=ss_rsh.a, rsh_out=ss_rsh.a, gcv_in=ss_gcv.a, gcv_out=ss_gcv.a)
                if "mix" not in skip:
                    mixer_tile(l, P, mix, S, sample)
                k.dma(mixp[l][row0:row0 + 128, :], mix.a)
                if sample:
                    for hf in range(2):
                        q = 1 + 4 * s + 2 * (j - NT_P) + hf
                        k.dma(o_mCn[l, q], ss_mC[hf].a)
                        k.dma(o_mm[l, q], ss_mm[hf].a)
                        k.dma(o_s5[l, q], ss_s5[hf].a.re("p r j -> p (r j)"))
                        k.dma(o_rS[l, q], ss_rS[hf].a)
                        k.dma(o_rsh[l, q:q + 1, :], ss_rsh[hf:hf + 1, :])
                        k.dma(o_gS[l, q], ss_gS[hf].a)
                        k.dma(o_gcv[l, q], ss_gcv[3 * hf:3 * hf + 3, :])
        k.dma(o_mCn[l, 0], st_mC.a)
        k.dma(o_mm[l, 0], st_mm.a)
        k.dma(o_s5[l, 0], st_s5.a.re("p r j -> p (r j)"))
        k.dma(o_rS[l, 0], st_rS.a)
        k.dma(o_rsh[l, 0:1, :], st_rsh[0:1, :])
        k.dma(o_gS[l, 0], st_gS.a)
        k.dma(o_gcv[l, 0], st_gcv[0:3, :])


    def phase_b(l, xres, xdst):
        k.dma(gbc[:, 0, :], gvec[l, 1, :].pbc(128))
        k.dma(gbc[:, 1, :], gvec[l, 3, :].pbc(128))
        k.dma(bglu.a, gvec[l, 4, 0:512].pbc(128))
        k.dma(gffn.a, gcols[l, 1])
        for kc in range(4):
            i = cnt[0] % 2
            cnt[0] += 1
            k.dma(stg[i][:, 0:512], w_glu_d[l, kc * 128:(kc + 1) * 128, :])
            k.cp(wglu[:, kc, :], stg[i][:, 0:512], E=cast_engs[i])
        nblk = (NT_S + 2) // 3
        for blk in range(nblk):
            t0 = blk * 3
            nt = min(3, NT_S - t0)
            TB = nt * 128
            for ti in range(nt):
                msel = scr[:, 0, :]
                x1t = scr[:, 2, :]
                for q in range(4):
                    cand = scr[:, 1, :]
                    r0 = q * SL + (t0 + ti) * 128
                    ch, wi = r0 // 512, r0 % 512
                    k.dma(cand.re("p (s c) -> p s c", s=4),
                          mixa[l].a[ch * 2048:(ch + 1) * 2048, :].re("(s t) c -> t s c", s=4)[wi:wi + 128, :, :])
                    if q == 0:
                        k.ts(msel, cand, qm[:, 0:1], None, ALU.mult)
                    else:
                        k.stt(msel, cand, qm[:, q:q + 1], msel, ALU.mult, ALU.add)
                m4 = msel.re("p (s c) -> p s c", s=4)
                k.cp(ys32.a.re("p (s c) -> p s c", s=4), m4[:, :, 128:256])
                k.cp(ysb.a, ys32.a, E=k.act)
                for c4 in range(4):
                    k.tr(pst[:, c4 * 128:(c4 + 1) * 128], ysb[:, c4 * 128:(c4 + 1) * 128], identb)
                k.cp(ysT.a.re("p c t -> p (c t)"), pst[:, 0:512], E=k.act)
                pp = k.ps()
                for c4 in range(4):
                    k.mm(pp.a, ysT[:, c4, :], wglu[:, c4, :], start=(c4 == 0), stop=(c4 == 3))
                k.tt(sgl.a, pp.a, bglu.a, ALU.add)
                k.actf(sgl.a, sgl.a, AF.Sigmoid)
                k.tt(m4[:, :, 128:256], ys32.a.re("p (s c) -> p s c", s=4), sgl.a.re("p (s c) -> p s c", s=4), ALU.mult)
                k.cp(hb.a, msel, E=k.act)
                for half in range(2):
                    for c8 in range(8):
                        kc = half * 8 + c8
                        k.tr(pst[:, c8 * 128:(c8 + 1) * 128], hb[:, kc * 128:(kc + 1) * 128], identb)
                    k.cp(mixT[:, half * 8:(half + 1) * 8, ti * 128:(ti + 1) * 128],
                         pst.a.re("p (c t) -> p c t", c=8), E=(k.act if half else k.dve))
            if stop == "b1":
                return
            t1all = scr
            wob = [wbA[:, 0:8192].re("p (c f) -> p c f", c=16), wbB[:, 0:8192].re("p (c f) -> p c f", c=16)]
            for nb in range(4):
                k.dma(wob[nb % 2], wo_s[l][nb])
                for ti in range(nt):
                    pp = k.ps()
                    for kc in range(16):
                        k.mm(pp.a, mixT[:, kc, ti * 128:(ti + 1) * 128], wob[nb % 2][:, kc, :], start=(kc == 0), stop=(kc == 15))
                    k.cp(t1all[:, ti, nb * 512:(nb + 1) * 512], pp.a, E=(k.act if ti % 2 else k.dve))
            if stop == "b2":
                return
            for ti in range(nt):
                r0 = (t0 + ti) * 128
                cs_ = cols()
                k.actf(junk.a, t1all[:, ti, :], AF.Square, accum=cs_[:, 0:1])
                k.rsqrt(cs_[:, 0:1], cs_[:, 0:1], 1.0 / D, EPS)
                k.dma(xt[0].a, xres[r0:r0 + 128, :])
                k.stt(t1all[:, ti, :], t1all[:, ti, :], cs_[:, 0:1], gbc[:, 0, :], ALU.mult, ALU.mult)
                k.tt(t1all[:, ti, :], t1all[:, ti, :], xt[0].a, ALU.add)
                k.dma(xmid[r0:r0 + 128, :], t1all[:, ti, :])
                k.actf(junk.a, t1all[:, ti, :], AF.Square, accum=cs_[:, 1:2])
                k.rsqrt(cs_[:, 1:2], cs_[:, 1:2], 1.0 / D, EPS)
                k.ts(hb.a, t1all[:, ti, :], cs_[:, 1:2], None, ALU.mult)
                for half in range(2):
                    for c8 in range(8):
                        kc = half * 8 + c8
                        k.tr(pst[:, c8 * 128:(c8 + 1) * 128], hb[:, kc * 128:(kc + 1) * 128], identb)
                    for c8 in range(8):
                        kc = half * 8 + c8
                        k.actf(mixT[:, kc, ti * 128:(ti + 1) * 128], pst[:, c8 * 128:(c8 + 1) * 128], AF.Copy,
                               scale=gffn[:, kc:kc + 1])
            if stop == "b3":
                return
            for fc in range(44):
                wb = wgu[fc % 2]
                k.dma(wb[:, 0, :, :], wg_s[l][fc])
                k.dma(wb[:, 1, :, :], wu_s[l][fc])
                pg_, pu_ = k.ps(), k.ps()
                for kc in range(16):
                    k.mm(pg_[:, 0:TB], wb[:, 0, kc, :], mixT[:, kc, 0:TB], start=(kc == 0), stop=(kc == 15))
                for kc in range(16):
                    k.mm(pu_[:, 0:TB], wb[:, 1, kc, :], mixT[:, kc, 0:TB], start=(kc == 0), stop=(kc == 15))
                sg_ = sgT
                k.actf(sg_[:, 0:TB], pg_[:, 0:TB], AF.Silu)
                k.tt(actT[:, fc, 0:TB], sg_[:, 0:TB], pu_[:, 0:TB], ALU.mult)
            if stop == "b4":
                return
            wdh = [wbA.a.re("p (c f) -> p c f", c=22), wbB.a.re("p (c f) -> p c f", c=22)]
            for nb in range(4):
                k.dma(wdh[0], wd_s[l][nb, :, 0:22, :])
                k.dma(wdh[1], wd_s[l][nb, :, 22:44, :])
                pps = [k.ps() for _ in range(nt)]
                for half in range(2):
                    for ti in range(nt):
                        for f_ in range(22):
                            fc = half * 22 + f_
                            k.mm(pps[ti].a, actT[:, fc, ti * 128:(ti + 1) * 128], wdh[half][:, f_, :], start=(fc == 0), stop=(fc == 43))
                for ti in range(nt):
                    k.cp(scr[:, ti, nb * 512:(nb + 1) * 512], pps[ti].a, E=(k.act if ti % 2 else k.dve))
            if stop == "b5":
                return
            for ti in range(nt):
                r0 = (t0 + ti) * 128
                cs_ = cols()
                k.actf(junk.a, scr[:, ti, :], AF.Square, accum=cs_[:, 0:1])
                k.rsqrt(cs_[:, 0:1], cs_[:, 0:1], 1.0 / D, EPS)
                k.dma(xt[1].a, xmid[r0:r0 + 128, :])
                k.stt(scr[:, ti, :], scr[:, ti, :], cs_[:, 0:1], gbc[:, 1, :], ALU.mult, ALU.mult)
                k.tt(scr[:, ti, :], scr[:, ti, :], xt[1].a, ALU.add)
                k.dma(xdst[r0:r0 + 128, :], scr[:, ti, :])

    for l in range(DEPTH):
        with ExitStack() as es:
            k.stack = es
            win = k.sb([128, 16, NQ], BF16, "win")
            pc = k.sb([128, 32], name="pc")
            prb = k.sb([128, 3328], name="prb")
            gpre = k.sb([128, 16], name="gpre")
            s5_P = k.sb([128, 2, 4, 128], name="s5P")
            s5_Pi = k.sb([128, 2, 4, 128], name="s5Pi")
            s5_lb = k.sb([128, 2, 4], name="s5lb")
            s5_Bb = k.sb([128, 2, 4, 128], name="s5Bb")
            s5_Cb = k.sb([128, 2, 4, 128], name="s5Cb")
            w2a2 = k.sb([128, 128], name="w2a2")
            g2 = k.sb([128, 128], name="g2")
            st_mC = k.sb([128, 129], name="st_mC")
            st_mm = k.sb([128, 1], name="st_mm")
            st_s5 = k.sb([128, 2, 4], name="st_s5")
            st_rS = k.sb([128, 64], name="st_rS")
            st_rsh = k.sb([2, 640], name="st_rsh")
            st_gS = k.sb([128, 128], name="st_gS")
            st_gcv = k.sb([6, 384], name="st_gcv")
            ss_mC = [k.sb([128, 129], name="ss_mC%d" % i) for i in range(2)]
            ss_mm = [k.sb([128, 1], name="ss_mm%d" % i) for i in range(2)]
            ss_s5 = [k.sb([128, 2, 4], name="ss_s5%d" % i) for i in range(2)]
            ss_rS = [k.sb([128, 64], name="ss_rS%d" % i) for i in range(2)]
            ss_rsh = k.sb([2, 640], name="ss_rsh")
            ss_gS = [k.sb([128, 128], name="ss_gS%d" % i) for i in range(2)]
            ss_gcv = k.sb([6, 384], name="ss_gcv")

            hT = k.sb([128, 16, 128], BF16, "hT")
            Pj = [k.sb([128, NQ], name="P%d" % i) for i in range(1)]
            mixt = [k.sb([128, 512], name="mixt%d" % i) for i in range(1)]

            npool = [T(n="np%d" % i) for i in range(8)]

            big_pool = [T((128, 512), n="bp%d" % i) for i in range(7)]


            vaug = k.sb([128, 129], name="vaug")
            intra = k.sb([128, 129], name="intra")
            mt1 = k.sb([128, 129], name="mt1")
            sh1 = k.sb([128, 640], name="sh1")
            gacc = k.sb([128, 384], name="gacc")
            gtmp = k.sb([128, 384], name="gtmp")
            grhs = k.sb([128, 256], name="grhs")
            del tmp_pool[:]
            tpi[0] = 0
            layer_setup(l)
            if stop == "setup":
                k.final_wait()
                return nc
            phase_a_and_mix(l, xg if l == 0 else x1all)
            if stop == "A":
                k.final_wait()
                return nc
            k.barrier()
            k.stack = None
        k.allgather(mixa[l], mixp[l], 512)
        if stop == "AG":
            k.final_wait()
            return nc
        with ExitStack() as es:
            k.stack = es
            mixT = k.sb([128, 16, 384], BF16, "mixT")
            scr = k.sb([128, 3, 2048], name="scr")
            actT = k.sb([128, 44, 384], BF16, "actT")
            wbA = k.sb([128, 11264], BF16, "wbA")
            wbB = k.sb([128, 11264], BF16, "wbB")
            wgu = [k.sb([128, 2, 16, 128], BF16, "wgu%d" % i) for i in range(2)]
            gbc = k.sb([128, 2, 2048], name="gbc")
            bglu = k.sb([128, 512], name="bglu")
            gffn = k.sb([128, 16], name="gffn")
            wglu = k.sb([128, 4, 512], BF16, "wglu")
            ysb = k.sb([128, 512], BF16, "ysb")
            ysT = k.sb([128, 4, 128], BF16, "ysT")
            ys32 = k.sb([128, 512], name="ys32")
            sgl = k.sb([128, 512], name="sgl")
            sgT = k.sb([128, 384], name="sgT")
            phase_b(l, xo if l == 0 else x1own, x1own if l == 0 else y_own)
            if stop in ("B", "b1", "b2", "b3", "b4", "b5"):
                k.final_wait()
                return nc
            k.barrier()
            k.stack = None
        if l == 0:
            k.allgather(x1all, x1own, 128)
    k.final_wait()
    return nc


def _qcols(r):
    W = 512
    NM = 4 * W + 8
    NSs = W
    NR = 3 * W + 256
    m0, s0, r0 = 0, NM, NM + NSs
    g0 = NM + NSs + NR
    hd = lambda base, h, w=128: list(range(base + h * w, base + (h + 1) * w))
    cols = []
    for blk in range(4):
        cols += hd(m0 + blk * W, r)
    cols += [m0 + 4 * W + r, m0 + 4 * W + 4 + r]
    cols += hd(s0, r)
    for blk in range(3):
        cols += hd(r0 + blk * W, r)
    cols += list(range(r0 + 3 * W, r0 + 3 * W + 256))
    for blk in range(4):
        cols += hd(g0 + blk * W, r)
    cols += [g0 + 4 * W + r, g0 + 4 * W + 4 + r]
    assert len(cols) == NQ
    return np.array(cols)


_CACHE = {}


def _run(inp, SEQ, stop=None, skip=()):
    f = lambda n: np.asarray(inp[n], np.float32)
    PT = SEQ // 4
    SL = PT + 256
    if SEQ not in _CACHE:
        _CACHE[SEQ] = build(SEQ, stop, skip)
    nc = _CACHE[SEQ]
    cst = _consts()
    cmat = np.stack([cst[n] for n in CONST_ORDER]).astype(np.float32)
    tails = np.stack([cst["tailP"], cst["tailS"]]).astype(np.float32)
    xp, xs = f("x_prompt"), f("x_sample")
    in_maps = []
    h = lambda a, r: a[..., r * 128:(r + 1) * 128]
    for c in range(8):
        b, r = c // 4, c % 4
        slabs = []
        for s in range(4):
            slabs.append(np.concatenate([xp[b, s * PT:(s + 1) * PT],
                                         xs[16 * b + 4 * s:16 * b + 4 * s + 4].reshape(256, D)], 0))
        xgrp = np.concatenate(slabs, 0)
        m = {"xg": xgrp, "xo": slabs[r], "cmat": cmat, "hsel": cst["hsel"], "tails": tails}
        qm = np.zeros((128, 4), np.float32)
        qm[:, r] = 1.0
        m["qmask"] = qm
        qc = _qcols(r)
        m["w_in"] = np.ascontiguousarray(f("w_in")[:, :, qc])
        perm = np.concatenate([np.concatenate([np.arange(g * 512 + q * 128, g * 512 + (q + 1) * 128) for g in range(4)])
                               for q in range(4)])
        m["w_out"] = np.ascontiguousarray(f("w_out")[:, perm, :])
        m["w_gate"], m["w_up"], m["w_down"] = f("w_gate"), f("w_up"), f("w_down")
        m["w_glu"] = f("s5_w_glu")
        gv = np.zeros((DEPTH, 5, D), np.float32)
        gv[:, 0], gv[:, 1], gv[:, 2], gv[:, 3] = f("g_pre_mix"), f("g_post_mix"), f("g_pre_ffn"), f("g_post_ffn")
        gv[:, 4, 0:512] = f("s5_b_glu")
        m["gvec"] = gv
        m["gcols"] = np.stack([f("g_pre_mix").reshape(DEPTH, 16, 128).transpose(0, 2, 1), f("g_pre_ffn").reshape(DEPTH, 16, 128).transpose(0, 2, 1)], 1)
        pcol = np.zeros((DEPTH, 128, 32), np.float32)
        gs = slice(8 * r, 8 * r + 8)
        tile4 = lambda a: a.reshape(DEPTH, 4, 128).transpose(0, 2, 1)
        pcol[:, :, 0:4] = tile4(f("s5_lam_re")[:, gs].reshape(DEPTH, 512))
        pcol[:, :, 4:8] = tile4(f("s5_lam_im")[:, gs].reshape(DEPTH, 512))
        pcol[:, :, 8:12] = tile4(np.repeat(f("s5_log_dt")[:, gs], 64, axis=1))
        pcol[:, :, 12] = h(f("s5_D"), r)
        pcol[:, :, 13] = f("mlstm_gate_bias")[:, 0, r][:, None]
        pcol[:, :, 14] = f("mlstm_gate_bias")[:, 1, r][:, None]
        pcol[:, :, 15] = f("gdn_A_log")[:, r][:, None]
        pcol[:, :, 16] = f("gdn_dt_bias")[:, r][:, None]
        m["pcol"] = pcol
        prow = np.zeros((DEPTH, 1, 3328), np.float32)
        names9 = [h(f("mlstm_norm_g"), r), h(f("rwkv_w0"), r), h(f("rwkv_a0"), r), h(f("rwkv_k_k"), r), h(f("rwkv_k_a"), r),
                  f("rwkv_r_k").reshape(DEPTH, 512)[:, r * 128:(r + 1) * 128], h(f("rwkv_ln_w"), r), h(f("rwkv_ln_b"), r),
                  f("gdn_norm_g")]
        for i, a in enumerate(names9):
            prow[:, 0, i * 128:(i + 1) * 128] = a
        mu = f("rwkv_mu")
        prow[:, 0, 1152:1792] = np.concatenate([h(mu[:, 0:512], r), h(mu[:, 512:1024], r), h(mu[:, 1024:1536], r), mu[:, 1536:]], 1)
        cw = f("gdn_conv_w")
        for j in range(4):
            prow[:, 0, 1792 + j * 384:1792 + (j + 1) * 384] = np.concatenate([h(cw[:, j, 0:512], r), h(cw[:, j, 512:1024], r), h(cw[:, j, 1024:1536], r)], 1)
        m["prow"] = prow
        def blk(a_gpc):
            o = np.zeros((DEPTH, 128, 4, 128), np.float32)
            for g in range(8):
                j, hh = g // 2, g % 2
                o[:, hh * 64:(hh + 1) * 64, j, g * 16:(g + 1) * 16] = a_gpc[:, g]
            return o
        m["s5b"] = np.stack([blk(f("s5_B_re")[:, gs]), blk(f("s5_B_im")[:, gs])], 1)
        m["s5c"] = np.stack([blk(f("s5_C_re")[:, gs].transpose(0, 1, 3, 2)), blk(f("s5_C_im")[:, gs].transpose(0, 1, 3, 2))], 1)
        m["rw2"] = np.concatenate([h(f("rwkv_w2"), r), h(f("rwkv_a2"), r)], 1)
        m["rg2"] = np.ascontiguousarray(h(f("rwkv_g2"), r))
        sq = slice(16 * b, 16 * b + 16)
        m["i_mCn"] = np.concatenate([f("state_mlstm_C")[:, sq, r], f("state_mlstm_n")[:, sq, r][..., None]], -1)
        m["i_mm"] = np.repeat(f("state_mlstm_m")[:, sq, r][..., None, None], 128, axis=2)
        sre = f("state_s5_re")[:, sq, gs].reshape(DEPTH, 16, 4, 128).transpose(0, 1, 3, 2)
        sim = f("state_s5_im")[:, sq, gs].reshape(DEPTH, 16, 4, 128).transpose(0, 1, 3, 2)
        m["i_s5"] = np.concatenate([sre, sim], -1)
        rS = f("state_rwkv_S")[:, sq, 2 * r:2 * r + 2]
        m["i_rS"] = np.ascontiguousarray(rS.transpose(0, 1, 2, 4, 3)).reshape(DEPTH, 16, 128, 64)
        rsh = f("state_rwkv_shift")[:, sq]
        m["i_rsh"] = np.concatenate([h(rsh[..., 0:512], r), h(rsh[..., 512:1024], r), h(rsh[..., 1024:1536], r), rsh[..., 1536:]], -1)
        m["i_gS"] = np.ascontiguousarray(f("state_gdn_S")[:, sq, r])
        gc = f("state_gdn_conv")[:, sq]
        m["i_gcv"] = np.concatenate([h(gc[..., 0:512], r), h(gc[..., 512:1024], r), h(gc[..., 1024:1536], r)], -1)
        in_maps.append({kk_: np.ascontiguousarray(v, dtype=np.float32) for kk_, v in m.items()})
    res = run_bass_kernel_spmd(nc, in_maps, core_ids=list(range(8)))
    R = res.results
    B, NSQ = 2, 32
    yp = np.zeros((B, SEQ, D), np.float32)
    ys = np.zeros((NSQ, 64, D), np.float32)
    def mk(bn):
        return (np.zeros((DEPTH, bn, 4, 128, 128), np.float32), np.zeros((DEPTH, bn, 4, 128), np.float32),
                np.zeros((DEPTH, bn, 4), np.float32), np.zeros((DEPTH, bn, 32, 64), np.float32),
                np.zeros((DEPTH, bn, 32, 64), np.float32), np.zeros((DEPTH, bn, 8, 64, 64), np.float32),
                np.zeros((DEPTH, bn, 1792), np.float32), np.zeros((DEPTH, bn, 4, 128, 128), np.float32),
                np.zeros((DEPTH, bn, 3, 1536), np.float32))
    Pp, Ps = mk(B), mk(NSQ)
    for c in range(8):
        b, r = c // 4, c % 4
        o = R[c]
        y = o["y_own"]
        yp[b, r * PT:(r + 1) * PT] = y[:PT]
        ys[16 * b + 4 * r:16 * b + 4 * r + 4] = y[PT:].reshape(4, 64, D)
        for (dst, idx, src_sl) in ((Pp, b, slice(0, 1)), (Ps, slice(16 * b, 16 * b + 16), slice(1, 17))):
            def put(arr, val):
                if isinstance(idx, int):
                    arr[:, idx] = val[:, 0]
                else:
                    arr[:, idx] = val
            return_none = None
            mCn = o["o_mCn"][:, src_sl]
            tgt = (lambda a: a[:, idx]) if not isinstance(idx, int) else (lambda a: a[:, idx:idx + 1])
            tgt(dst[0])[:, :, r] = mCn[..., 0:128]
            tgt(dst[1])[:, :, r] = mCn[..., 128]
            tgt(dst[2])[:, :, r] = o["o_mm"][:, src_sl][:, :, 0, 0]
            s5 = o["o_s5"][:, src_sl]
            nn = s5.shape[1]
            tgt(dst[3])[:, :, 8 * r:8 * r + 8] = s5[..., 0:4].transpose(0, 1, 3, 2).reshape(DEPTH, nn, 8, 64)
            tgt(dst[4])[:, :, 8 * r:8 * r + 8] = s5[..., 4:8].transpose(0, 1, 3, 2).reshape(DEPTH, nn, 8, 64)
            rS = o["o_rS"][:, src_sl].reshape(DEPTH, nn, 2, 64, 64)
            tgt(dst[5])[:, :, 2 * r:2 * r + 2] = rS.transpose(0, 1, 2, 4, 3)
            rsh = o["o_rsh"][:, src_sl]
            for i3 in range(3):
                tgt(dst[6])[:, :, i3 * 512 + r * 128:i3 * 512 + (r + 1) * 128] = rsh[..., i3 * 128:(i3 + 1) * 128]
            if r == 0:
                tgt(dst[6])[:, :, 1536:] = rsh[..., 384:]
            tgt(dst[7])[:, :, r] = o["o_gS"][:, src_sl]
            gcv = o["o_gcv"][:, src_sl]
            for i3 in range(3):
                tgt(dst[8])[:, :, :, i3 * 512 + r * 128:i3 * 512 + (r + 1) * 128] = gcv[..., i3 * 128:(i3 + 1) * 128]
    return (yp, ys) + Pp + Ps


def kernel(**inputs):
    return _run(inputs, 16384)
```

```python
import numpy as np
import ml_dtypes
import concourse.bass as bass
import concourse.mybir as mybir
from concourse.bass_utils import run_bass_kernel_spmd
from contextlib import ExitStack

F32 = mybir.dt.float32
BF16 = mybir.dt.bfloat16
ALU = mybir.AluOpType
AF = mybir.ActivationFunctionType
AX = mybir.AxisListType.X

D = 2048
DFF = 5632
NQ = 1796
NS = 16
DEPTH = 2
EPS = 1e-6
O_MQ, O_MK, O_MV, O_MO, O_MI, O_MF = 0, 128, 256, 384, 512, 513
O_SU = 514
O_R = 642
O_GQ, O_GK, O_GV, O_GZ, O_GA, O_GB = 1282, 1410, 1538, 1666, 1794, 1795
NEG = -1.0e30


class Eng:
    def __init__(self, nc, e, name):
        self.e = e
        self.name = name
        self.sem = nc.alloc_semaphore(name + "_prog")
        self.cnt = 0
        self.seen = {}


class Tile:
    def __init__(self, t):
        self.t = t
        self.w = None
        self.r = {}

    def __getitem__(self, idx):
        return V(self, self.t.ap()[idx] if hasattr(self.t, "ap") else self.t[idx])

    @property
    def a(self):
        return V(self, self.t.ap() if hasattr(self.t, "ap") else self.t[:])


class V:
    def __init__(self, tile, ap):
        self.tile = tile
        self.ap = ap

    def __getitem__(self, idx):
        return V(self.tile, self.ap[idx])

    def re(self, pat, **kw):
        return V(self.tile, self.ap.rearrange(pat, **kw))

    def bc(self, shape):
        return V(self.tile, self.ap.to_broadcast(shape))

    def pbc(self, n):
        return V(self.tile, self.ap.partition_broadcast(n))


class KB:
    def __init__(self, nc):
        self.nc = nc
        self.pe = Eng(nc, nc.tensor, "pe")
        self.act = Eng(nc, nc.scalar, "act")
        self.dve = Eng(nc, nc.vector, "dve")
        self.pool = Eng(nc, nc.gpsimd, "pool")
        self.sp = Eng(nc, nc.sync, "sp")
        self.nds = 40
        self.dsem = [nc.alloc_semaphore("dq%d" % i) for i in range(self.nds)]
        self.dval = [0] * self.nds
        self.dnext = 0
        self.ccsem = nc.alloc_semaphore("ccs")
        self.ccval = 0
        self.uid = 0
        self.psb = []
        self.psi = 0

    def sb(self, shape, dt=F32, name=None):
        self.uid += 1
        nm = "%s_%d" % (name or "t", self.uid)
        if getattr(self, "stack", None) is not None:
            return Tile(self.stack.enter_context(self.nc.sbuf_tensor(nm, list(shape), dt)))
        return Tile(self.nc.alloc_sbuf_tensor(nm, list(shape), dt))

    def dram(self, name, shape, dt=F32, kind=None):
        if kind:
            return Tile(self.nc.dram_tensor(name, list(shape), dt, kind=kind))
        return Tile(self.nc.dram_tensor(name, list(shape), dt))

    def ps(self):
        t = self.psb[self.psi % len(self.psb)]
        self.psi += 1
        return t

    def _wait(self, E, tok):
        sem, val, key = tok
        if E.seen.get(key, 0) >= val:
            return
        E.e.wait_ge(sem, val)
        E.seen[key] = val

    def _deps(self, E, reads, writes, skip_self=False):
        for v in reads:
            t = v.tile
            if t.w is not None and not (skip_self and t.w[2] == E.name):
                self._wait(E, t.w)
        for v in writes:
            t = v.tile
            if t.w is not None and not (skip_self and t.w[2] == E.name):
                self._wait(E, t.w)
            for k, tok in t.r.items():
                if not (skip_self and k == E.name):
                    self._wait(E, tok)

    def _done(self, E, ins, reads, writes):
        E.cnt += 1
        ins.then_inc(E.sem, 1)
        tok = (E.sem, E.cnt, E.name)
        E.seen[E.name] = E.cnt if E is self.pe else E.seen.get(E.name, 0)
        for v in reads:
            v.tile.r[E.name] = tok
        for v in writes:
            v.tile.w = tok
            v.tile.r = {}

    def op(self, E, fn, reads, writes):
        reads = [v for v in reads if isinstance(v, V)]
        self._deps(E, reads, writes, skip_self=(E is self.pe))
        ins = fn()
        self._done(E, ins, reads, writes)

    def dma(self, out, in_, E=None):
        E = E or self.sp
        self._deps(E, [in_], [out])
        i = self.dnext % self.nds
        self.dnext += 1
        if self.dval[i] > 0:
            self._wait(E, (self.dsem[i], self.dval[i], "d%d" % i))
        self.dval[i] += 16
        E.e.dma_start(out=out.ap, in_=in_.ap).then_inc(self.dsem[i], 16)
        tok = (self.dsem[i], self.dval[i], "d%d" % i)
        in_.tile.r["d%d" % i] = tok
        out.tile.w = tok
        out.tile.r = {}

    def allgather(self, out_t, in_t, R):
        E = self.pool
        self._deps(E, [in_t.a], [out_t.a])
        n = in_t.t.ap().shape[0] // R
        for i in range(n):
            self.ccval += 1
            E.e.collective_compute("AllGather", ALU.bypass, replica_groups=[[0, 1, 2, 3], [4, 5, 6, 7]],
                                   ins=[in_t.t.ap()[i * R:(i + 1) * R, :]],
                                   outs=[out_t.t.ap()[i * 4 * R:(i + 1) * 4 * R, :]]).then_inc(self.ccsem)
        tok = (self.ccsem, self.ccval, "cc")
        in_t.r["cc"] = tok
        out_t.w = tok
        out_t.r = {}

    def mm(self, out, lhsT, rhs, start=True, stop=True):
        self.op(self.pe, lambda: self.nc.tensor.matmul(out.ap, lhsT.ap, rhs.ap, start=start, stop=stop),
                [lhsT, rhs], [out])

    def tr(self, out, in_, ident):
        self.op(self.pe, lambda: self.nc.tensor.transpose(out.ap, in_.ap, ident.ap), [in_, ident], [out])

    def actf(self, out, in_, func, bias=None, scale=1.0, accum=None):
        kw = {}
        if bias is not None:
            kw["bias"] = bias.ap if isinstance(bias, V) else bias
        if isinstance(scale, V):
            kw["scale"] = scale.ap
        else:
            kw["scale"] = scale
        if accum is not None:
            kw["accum_out"] = accum.ap
        wr = [out] + ([accum] if accum is not None else [])
        self.op(self.act, lambda: self.nc.scalar.activation(out.ap, in_.ap, func, **kw),
                [in_, bias, scale], wr)

    def _sv(self, s):
        return s.ap if isinstance(s, V) else s

    def tt(self, out, a, b, op, E=None):
        E = E or self.dve
        self.op(E, lambda: E.e.tensor_tensor(out.ap, a.ap, b.ap, op), [a, b], [out])

    def ts(self, out, a, s1, s2, op0, op1=None, E=None):
        E = E or self.dve
        if op1 is None:
            self.op(E, lambda: E.e.tensor_scalar(out.ap, a.ap, self._sv(s1), None, op0), [a, s1], [out])
        else:
            self.op(E, lambda: E.e.tensor_scalar(out.ap, a.ap, self._sv(s1), self._sv(s2), op0, op1),
                    [a, s1, s2], [out])

    def stt(self, out, a, s, b, op0, op1):
        self.op(self.dve, lambda: self.nc.vector.scalar_tensor_tensor(out.ap, a.ap, self._sv(s), b.ap, op0, op1),
                [a, s, b], [out])

    def ttr(self, out, a, b, op0, op1, init, accum):
        if op0 == ALU.mult and a is b:
            self.actf(out, a, AF.Square, accum=accum)
        else:
            self.tt(out, a, b, op0)
            self.red(accum, out, op1)

    def red(self, out, a, op):
        self.op(self.dve, lambda: self.nc.vector.tensor_reduce(out.ap, a.ap, AX, op), [a], [out])

    def scan(self, out, d0, d1, init):
        self.op(self.dve, lambda: self.nc.vector.tensor_tensor_scan(out.ap, d0.ap, d1.ap, self._sv(init), ALU.mult, ALU.add),
                [d0, d1, init], [out])

    def recip(self, out, a):
        self.op(self.dve, lambda: self.nc.vector.reciprocal(out.ap, a.ap), [a], [out])

    def cp(self, out, a, E=None):
        E = E or self.dve
        if E is self.act:
            self.op(E, lambda: self.nc.scalar.copy(out.ap, a.ap), [a], [out])
        else:
            self.op(E, lambda: E.e.tensor_copy(out.ap, a.ap), [a], [out])

    def memset(self, out, val, E=None):
        E = E or self.dve
        self.op(E, lambda: E.e.memset(out.ap, val), [], [out])

    def rsqrt(self, out, ssq, scale, eps):
        self.ts(out, ssq, scale, eps, ALU.mult, ALU.add)
        self.actf(out, out, AF.Sqrt)
        self.recip(out, out)

    def barrier(self):
        engs = (self.pe, self.act, self.dve, self.pool, self.sp)
        snap = [(X.sem, X.cnt, X.name) for X in engs if X.cnt > 0]
        dsn = [(self.dsem[i], self.dval[i], "d%d" % i) for i in range(self.nds) if self.dval[i] > 0]
        cc = [(self.ccsem, self.ccval, "cc")] if self.ccval > 0 else []
        for E in engs:
            for tok in snap + dsn + cc:
                if tok[2] != E.name:
                    self._wait(E, tok)

    def final_wait(self):
        E = self.sp
        for i in range(self.nds):
            if self.dval[i] > 0:
                self._wait(E, (self.dsem[i], self.dval[i], "d%d" % i))
        for X in (self.pe, self.act, self.dve, self.pool):
            if X.cnt > 0:
                self._wait(E, (X.sem, X.cnt, X.name))


def _consts():
    c = {}
    i = np.arange(128)
    same = (i[:, None] // 64) == (i[None, :] // 64)
    c["ident"] = np.eye(128, dtype=np.float32)
    c["ones"] = np.ones((128, 128), np.float32)
    c["triBD"] = (same & (i[:, None] <= i[None, :])).astype(np.float32)
    c["sel0"] = np.repeat((i[:, None] < 64), 128, 1).astype(np.float32)
    c["sel1"] = np.repeat((i[:, None] >= 64), 128, 1).astype(np.float32)
    c["mneg_ts_incl"] = np.where(same & (i[None, :] <= i[:, None]), 0.0, NEG).astype(np.float32)
    c["mpos_st_incl"] = np.where(same & (i[:, None] <= i[None, :]), 0.0, -NEG).astype(np.float32)
    c["mneg_st_incl"] = np.where(same & (i[:, None] <= i[None, :]), 0.0, NEG).astype(np.float32)
    c["mneg_st_strict"] = np.where(same & (i[:, None] < i[None, :]), 0.0, NEG).astype(np.float32)
    c["mpos_ts_strict"] = np.where(same & (i[None, :] < i[:, None]), 0.0, -NEG).astype(np.float32)
    c["m01_st_strict"] = (same & (i[:, None] < i[None, :])).astype(np.float32)
    c["m01_st_incl"] = (same & (i[:, None] <= i[None, :])).astype(np.float32)
    c["m01_ts_strict"] = (same & (i[None, :] < i[:, None])).astype(np.float32)
    for d in (1, 2, 3):
        sh = (i[:, None] == i[None, :] - d)
        c["shP%d" % d] = sh.astype(np.float32)
        c["shS%d" % d] = (sh & same).astype(np.float32)
    hs = np.zeros((6, 3, 128), np.float32)
    for d in (1, 2, 3):
        for t in range(d):
            hs[3 + t - d, d - 1, t] = 1.0
            hs[3 + 3 + t - d, d - 1, 64 + t] = 1.0
    c["hsel"] = hs.reshape(6, 384)
    tp = np.zeros((128, 32), np.float32)
    ts_ = np.zeros((128, 32), np.float32)
    for r in range(3):
        tp[125 + r, r] = 1.0
        ts_[61 + r, r] = 1.0
        ts_[125 + r, 3 + r] = 1.0
    c["tailP"] = tp
    c["tailS"] = ts_
    return c


CONST_ORDER = ["ident", "ones", "triBD", "sel0", "sel1", "mneg_ts_incl", "mpos_st_incl", "mneg_st_incl",
               "mneg_st_strict", "mpos_ts_strict", "m01_st_strict", "m01_st_incl", "m01_ts_strict",
               "shP1", "shP2", "shP3", "shS1", "shS2", "shS3"]


def build(SEQ, stop=None, skip=()):
    PT = SEQ // 4
    SL = PT + 256
    NT_P = PT // 128
    NT_S = NT_P + 2
    GT = 4 * SL
    nc = bass.Bass("TRN2", target_bir_lowering=False)
    k = KB(nc)
    ein = lambda n, s, dt=F32: k.dram(n, s, dt, kind="ExternalInput")
    eout = lambda n, s: k.dram(n, s, F32, kind="ExternalOutput")
    xg = ein("xg", [GT, D])
    xo = ein("xo", [SL, D])
    qmask = ein("qmask", [128, 4])
    cmat = ein("cmat", [len(CONST_ORDER), 128, 128])
    hsel_d = ein("hsel", [6, 384])
    tail_d = ein("tails", [2, 128, 32])
    w_in_d = ein("w_in", [DEPTH, D, NQ])
    w_out_d = ein("w_out", [DEPTH, D, D])
    w_gate_d = ein("w_gate", [DEPTH, D, DFF])
    w_up_d = ein("w_up", [DEPTH, D, DFF])
    w_down_d = ein("w_down", [DEPTH, DFF, D])
    w_glu_d = ein("w_glu", [DEPTH, 512, 512])
    gvec = ein("gvec", [DEPTH, 5, D])
    gcols = ein("gcols", [DEPTH, 2, 128, 16])
    pcol = ein("pcol", [DEPTH, 128, 32])
    prow = ein("prow", [DEPTH, 1, 3328])
    s5b = ein("s5b", [DEPTH, 2, 128, 4, 128])
    s5c = ein("s5c", [DEPTH, 2, 128, 4, 128])
    rw2 = ein("rw2", [DEPTH, 128, 128])
    rg2 = ein("rg2", [DEPTH, 128, 128])
    i_mCn = ein("i_mCn", [DEPTH, NS, 128, 129])
    i_mm = ein("i_mm", [DEPTH, NS, 128, 1])
    i_s5 = ein("i_s5", [DEPTH, NS, 128, 8])
    i_rS = ein("i_rS", [DEPTH, NS, 128, 64])
    i_rsh = ein("i_rsh", [DEPTH, NS, 640])
    i_gS = ein("i_gS", [DEPTH, NS, 128, 128])
    i_gcv = ein("i_gcv", [DEPTH, NS, 3, 384])
    y_own = eout("y_own", [SL, D])
    o_mCn = eout("o_mCn", [DEPTH, NS + 1, 128, 129])
    o_mm = eout("o_mm", [DEPTH, NS + 1, 128, 1])
    o_s5 = eout("o_s5", [DEPTH, NS + 1, 128, 8])
    o_rS = eout("o_rS", [DEPTH, NS + 1, 128, 64])
    o_rsh = eout("o_rsh", [DEPTH, NS + 1, 640])
    o_gS = eout("o_gS", [DEPTH, NS + 1, 128, 128])
    o_gcv = eout("o_gcv", [DEPTH, NS + 1, 3, 384])
    mixp = [k.dram("mixp%d" % l, [GT, 512]) for l in range(DEPTH)]
    mixa = [k.dram("mixa%d" % l, [4 * GT, 512]) for l in range(DEPTH)]
    x1own = k.dram("x1own", [SL, D])
    x1all = k.dram("x1all", [GT, D])
    xmid = k.dram("xmid", [SL, D])
    wo_s = [k.dram("wo_s%d" % l, [4, 128, 16, 512], BF16) for l in range(DEPTH)]
    wg_s = [k.dram("wg_s%d" % l, [44, 128, 16, 128], BF16) for l in range(DEPTH)]
    wu_s = [k.dram("wu_s%d" % l, [44, 128, 16, 128], BF16) for l in range(DEPTH)]
    wd_s = [k.dram("wd_s%d" % l, [4, 128, 44, 512], BF16) for l in range(DEPTH)]

    for i in range(7):
        k.psb.append(Tile(nc.alloc_psum_tensor("psb%d" % i, [128, 512], F32)))
    pst = Tile(nc.alloc_psum_tensor("pst", [128, 1024], BF16))

    C = {}
    cs = k.sb([128, len(CONST_ORDER), 128], name="consts")
    k.dma(cs.a, cmat.a.re("n p f -> p n f"))
    for i, n in enumerate(CONST_ORDER):
        C[n] = cs[:, i, :]
    ident = C["ident"]
    identb_t = k.sb([128, 128], BF16, "identb")
    k.cp(identb_t.a, ident)
    identb = identb_t.a
    hsel = k.sb([6, 384], name="hsel")
    k.dma(hsel.a, hsel_d.a)
    tails = k.sb([128, 2, 32], name="tails")
    k.dma(tails.a, tail_d.a.re("n p f -> p n f"))
    qm = k.sb([128, 4], name="qm")
    k.dma(qm.a, qmask.a)
    onescol = C["ones"][:, 0:1]

    xt = [k.sb([128, D], name="xt%d" % i) for i in range(2)]
    hb = k.sb([128, D], BF16, "hb")
    stg = [xt[0], xt[1]]
    stb = [hb, k.sb([128, 2048], BF16, "stb0")]
    junk = stb[1]
    cast_engs = [k.dve, k.act]
    cnt = [0]

    def castcopy(dst_v_fn, src_v, cw=2048):
        i = cnt[0] % 2
        cnt[0] += 1
        k.dma(stg[i][:, 0:cw], src_v)
        k.cp(stb[i][:, 0:cw], stg[i][:, 0:cw], E=cast_engs[i])
        dst_v_fn(stb[i])

    for l in range(DEPTH if "prep" not in skip else 0):
        for kc in range(16):
            castcopy(lambda sb_, kc=kc, l=l: k.dma(wo_s[l].a.re("n p c f -> p n c f")[:, :, kc, :],
                                                   sb_.a.re("p (n f) -> p n f", n=4)),
                     w_out_d[l, kc * 128:(kc + 1) * 128, :])
        for (src, dst) in ((w_gate_d, wg_s), (w_up_d, wu_s)):
            for kc in range(16):
                for cb in range(3):
                    c0 = cb * 2048
                    cw = min(2048, DFF - c0)
                    nf = cw // 128
                    def wr_(sb_, kc=kc, l=l, c0=c0, cw=cw, nf=nf, dst=dst):
                        for q0 in range(0, nf, 4):
                            q1 = min(nf, q0 + 4)
                            k.dma(dst[l].a.re("n p c f -> p n c f")[:, c0 // 128 + q0:c0 // 128 + q1, kc, :],
                                  sb_[:, q0 * 128:q1 * 128].re("p (n f) -> p n f", f=128))
                    castcopy(wr_,
                             src[l, kc * 128:(kc + 1) * 128, c0:c0 + cw], cw)
        for fc in range(44):
            castcopy(lambda sb_, fc=fc, l=l: k.dma(wd_s[l].a.re("n p c f -> p n c f")[:, :, fc, :],
                                                   sb_.a.re("p (n f) -> p n f", n=4)),
                     w_down_d[l, fc * 128:(fc + 1) * 128, :])

    if stop == "prep":
        k.final_wait()
        return nc
    col = lambda n="c": k.sb([128, 8], name=n)

    def T(shape=(128, 128), n="tmp"):
        return k.sb(list(shape), name=n)

    def tmp():
        if tpi[0] >= len(tmp_pool):
            tmp_pool.append(T(n="tp%d" % len(tmp_pool)))
        t = tmp_pool[tpi[0]]
        tpi[0] += 1
        return t

    def ntmp():
        t = npool[npi[0] % 8]
        npi[0] += 1
        return t

    def big():
        t = big_pool[bpi[0] % len(big_pool)]
        bpi[0] += 1
        return t

    def cols():
        t = col_pool[cpi[0] % len(col_pool)]
        cpi[0] += 1
        return t

    tmp_pool = []
    tpi = [0]
    npi = [0]
    bpi = [0]
    col_pool = [col("cp%d" % i) for i in range(24)]
    cpi = [0]

    def layer_setup(l):
        tpi[0] = 0
        k.dma(pc.a, pcol[l])
        k.dma(prb.a, prow[l, 0, :].pbc(128))
        k.dma(gpre.a, gcols[l, 0])
        k.dma(w2a2.a, rw2[l])
        k.dma(g2.a, rg2[l])
        for kc in range(16):
            i = cnt[0] % 2
            cnt[0] += 1
            k.dma(stg[i][:, 0:NQ], w_in_d[l, kc * 128:(kc + 1) * 128, :])
            k.cp(win[:, kc, :], stg[i][:, 0:NQ], E=cast_engs[i])
        c = cols()
        dt = c[:, 0:4]
        k.actf(dt, pc[:, 8:12], AF.Exp)
        c2 = cols()
        mag = c2[:, 0:4]
        k.tt(mag, pc[:, 0:4], dt, ALU.mult)
        k.actf(mag, mag, AF.Exp)
        th = c2[:, 4:8]
        k.tt(th, pc[:, 4:8], dt, ALU.mult)
        c3 = cols()
        sn, cn = c3[:, 0:4], c3[:, 4:8]
        k.ts(sn, th, 1.0 / 32, None, ALU.mult)
        k.ts(cn, th, 1.0 / 32, float(0.5 * np.pi), ALU.mult, ALU.add)
        k.actf(sn, sn, AF.Sin)
        k.actf(cn, cn, AF.Sin)
        for _ in range(5):
            dd_ = cols()
            k.tt(dd_[:, 0:4], cn, cn, ALU.mult)
            k.tt(dd_[:, 4:8], sn, sn, ALU.mult)
            k.tt(sn, sn, cn, ALU.mult)
            k.ts(sn, sn, 2.0, None, ALU.mult)
            k.tt(cn, dd_[:, 0:4], dd_[:, 4:8], ALU.subtract)
        lbr, lbi = s5_lb[:, 0, :], s5_lb[:, 1, :]
        k.tt(lbr, mag, cn, ALU.mult)
        k.tt(lbi, mag, sn, ALU.mult)
        c4 = cols()
        nr, den = c4[:, 0:4], c4[:, 4:8]
        k.ts(nr, lbr, -1.0, None, ALU.add)
        c5 = cols()
        t1, t2 = c5[:, 0:4], c5[:, 4:8]
        k.tt(t1, pc[:, 0:4], pc[:, 0:4], ALU.mult)
        k.tt(t2, pc[:, 4:8], pc[:, 4:8], ALU.mult)
        k.tt(den, t1, t2, ALU.add)
        k.recip(den, den)
        c6 = cols()
        fre, fim = c6[:, 0:4], c6[:, 4:8]
        k.tt(t1, nr, pc[:, 0:4], ALU.mult)
        k.tt(t2, lbi, pc[:, 4:8], ALU.mult)
        k.tt(fre, t1, t2, ALU.add)
        k.tt(fre, fre, den, ALU.mult)
        k.tt(t1, lbi, pc[:, 0:4], ALU.mult)
        k.tt(t2, nr, pc[:, 4:8], ALU.mult)
        k.tt(fim, t1, t2, ALU.subtract)
        k.tt(fim, fim, den, ALU.mult)
        braw = big()
        biraw = big()
        k.dma(braw.a.re("p (j f) -> p j f", j=4), s5b[l, 0])
        k.dma(biraw.a.re("p (j f) -> p j f", j=4), s5b[l, 1])
        for j in range(4):
            bre = ntmp()
            bim = ntmp()
            t3 = ntmp()
            k.ts(bre.a, braw[:, j * 128:(j + 1) * 128], fre[:, j:j + 1], None, ALU.mult)
            k.ts(t3.a, biraw[:, j * 128:(j + 1) * 128], fim[:, j:j + 1], None, ALU.mult)
            k.tt(bre.a, bre.a, t3.a, ALU.subtract)
            k.ts(bim.a, biraw[:, j * 128:(j + 1) * 128], fre[:, j:j + 1], None, ALU.mult)
            k.ts(t3.a, braw[:, j * 128:(j + 1) * 128], fim[:, j:j + 1], None, ALU.mult)
            k.tt(bim.a, bim.a, t3.a, ALU.add)
            for ri, src in ((0, bre), (1, bim)):
                p = k.ps()
                k.tr(p[:, 0:128], src.a, ident)
                k.cp(s5_Bb[:, ri, j, :], p[:, 0:128])
        k.dma(s5_Cb[:, 0, :, :], s5c[l, 0])
        k.dma(s5_Cb[:, 1, :, :], s5c[l, 1])
        k.ts(s5_Cb[:, 1, :, :], s5_Cb[:, 1, :, :], -1.0, None, ALU.mult)
        linv = cols()
        lir, lii = linv[:, 0:4], linv[:, 4:8]
        k.tt(t1, lbr, lbr, ALU.mult)
        k.tt(t2, lbi, lbi, ALU.mult)
        k.tt(t1, t1, t2, ALU.add)
        k.recip(t1, t1)
        k.tt(lir, lbr, t1, ALU.mult)
        k.tt(lii, lbi, t1, ALU.mult)
        k.ts(lii, lii, -1.0, None, ALU.mult)
        for (tab, br0, bi0) in ((s5_P, lbr, lbi), (s5_Pi, lir, lii)):
            pw = cols()
            pr_, pi_ = pw[:, 0:4], pw[:, 4:8]
            k.cp(pr_, br0)
            k.cp(pi_, bi0)
            k.memset(tab[:, 0, :, 0:1], 1.0)
            k.memset(tab[:, 1, :, 0:1], 0.0)
            n = 1
            while n < 64:
                for j in range(4):
                    a_r, a_i = tab[:, 0, j, 0:n], tab[:, 1, j, 0:n]
                    o_r, o_i = tab[:, 0, j, n:2 * n], tab[:, 1, j, n:2 * n]
                    tq = ntmp()
                    k.ts(tq[:, 0:n], a_i, pi_[:, j:j + 1], None, ALU.mult)
                    k.stt(o_r, a_r, pr_[:, j:j + 1], tq[:, 0:n], ALU.mult, ALU.subtract)
                    k.ts(tq[:, 0:n], a_i, pr_[:, j:j + 1], None, ALU.mult)
                    k.stt(o_i, a_r, pi_[:, j:j + 1], tq[:, 0:n], ALU.mult, ALU.add)
                sq = cols()
                k.tt(sq[:, 0:4], pr_, pr_, ALU.mult)
                k.tt(sq[:, 4:8], pi_, pi_, ALU.mult)
                nr2 = cols()
                k.tt(nr2[:, 0:4], sq[:, 0:4], sq[:, 4:8], ALU.subtract)
                k.tt(nr2[:, 4:8], pr_, pi_, ALU.mult)
                k.ts(pi_, nr2[:, 4:8], 2.0, None, ALU.mult)
                k.cp(pr_, nr2[:, 0:4])
                n *= 2
            k.cp(tab[:, :, :, 64:128], tab[:, :, :, 0:64])
        for s_ in (st_mC, st_mm, st_s5, st_rS, st_rsh, st_gS, st_gcv):
            k.memset(s_.a, 0.0)

    def halves():
        return ((0, slice(0, 64)), (1, slice(64, 128)))

    def mixer_tile(l, P, mix, S, sample):
        tpi[0] = 0
        PRI = {0: 0, 2: 1, 3: 2, 4: 3, 5: 4, 6: 5, 7: 6, 8: 7, 13: 8}
        PR = lambda r, w=128: prb[:, PRI[r] * 128:PRI[r] * 128 + w]
        sel = (C["sel0"], C["sel1"])

        c = cols()
        li, lf, bcs, acol = c[:, 0:1], c[:, 1:2], c[:, 2:3], c[:, 3:4]
        k.ts(li, P[:, O_MI:O_MI + 1], pc[:, 13:14], None, ALU.add)
        e = c[:, 4:5]
        k.ts(e, P[:, O_MF:O_MF + 1], pc[:, 14:15], -1.0, ALU.add, ALU.mult)
        k.actf(e, e, AF.Exp)
        k.actf(e, e, AF.Ln, bias=onescol)
        k.ts(lf, e, -1.0, None, ALU.mult)
        p = k.ps()
        k.mm(p[:, 0:1], C["triBD"], lf)
        k.tt(acol, li, p[:, 0:1], ALU.subtract)
        k.cp(bcs, p[:, 0:1])
        da = tmp()
        k.ts(da.a, ident, acol, None, ALU.mult)
        pA = k.ps()
        k.mm(pA[:, 0:128], C["ones"], da.a)
        cm = c[:, 5:6]
        cmL2 = c[:, 6:8]
        k.red(cmL2, pA[:, 0:128].re("p (h s) -> p h s", h=2), ALU.max)
        jk = tmp()
        k.ttr(jk.a, pA[:, 0:128], C["mneg_ts_incl"], ALU.add, ALU.max, NEG, cm)
        dcm = tmp()
        k.ts(dcm.a, ident, cm, None, ALU.mult)
        pB = k.ps()
        k.mm(pB[:, 0:128], C["ones"], dcm.a)
        z = tmp()
        k.stt(z.a, pB[:, 0:128], acol, C["mpos_st_incl"], ALU.subtract, ALU.max)
        DT = tmp()
        k.actf(DT.a, z.a, AF.Exp, scale=-1.0)
        qT, kT = tmp(), tmp()
        for (dst, off) in ((qT, O_MQ), (kT, O_MK)):
            pp = k.ps()
            k.tr(pp[:, 0:128], P[:, off:off + 128], ident)
            k.cp(dst.a, pp[:, 0:128], E=k.act)
        k.ts(qT.a, qT.a, 128 ** -0.5, None, ALU.mult)
        pkq = k.ps()
        k.mm(pkq[:, 0:128], kT.a, qT.a)
        SmT = tmp()
        k.tt(SmT.a, pkq[:, 0:128], DT.a, ALU.mult)
        k.cp(vaug[:, 0:128], P[:, O_MV:O_MV + 128], E=k.pool)
        k.memset(vaug[:, 128:129], 1.0, E=k.pool)
        pin = k.ps()
        k.mm(pin[:, 0:129], SmT.a, vaug.a)
        k.cp(intra.a, pin[:, 0:129], E=k.act)
        hnum = tmp()
        for hf, rs in halves():
            Cin, Cout = S[hf]["mC"]
            min_, mout = S[hf]["mm"]
            cc = cols()
            cmL, bL, ML, Mt, al, om = cc[:, 0:1], cc[:, 1:2], cc[:, 2:3], cc[:, 3:4], cc[:, 4:5], cc[:, 5:6]
            k.cp(cmL, cmL2[:, hf:hf + 1])
            pb = k.ps()
            k.mm(pb[:, 0:1], sel[hf], lf)
            k.cp(bL, pb[:, 0:1])
            k.tt(ML, cmL, min_.a, ALU.max)
            k.tt(Mt[rs], cm[rs], min_[rs, :], ALU.max)
            k.tt(al[rs], cm[rs], Mt[rs], ALU.subtract)
            k.actf(al[rs], al[rs], AF.Exp)
            k.tt(om[rs], min_[rs, :], Mt[rs], ALU.subtract)
            k.actf(om[rs], om[rs], AF.Exp)
            pq = k.ps()
            k.mm(pq[:, 0:129], qT.a, Cin.a)
            t1 = mt1
            k.ts(t1[rs, :], pq[rs, 0:129], om[rs], None, ALU.mult)
            k.stt(t1[rs, :], intra[rs, :], al[rs], t1[rs, :], ALU.mult, ALU.add)
            dd = cols()
            dn, ex = dd[:, 0:1], dd[:, 1:2]
            k.ts(dn[rs], t1[rs, 128:129], -1.0, None, ALU.mult)
            k.tt(dn[rs], dn[rs], t1[rs, 128:129], ALU.max)
            k.tt(ex[rs], bcs[rs], Mt[rs], ALU.add)
            k.actf(ex[rs], ex[rs], AF.Exp, scale=-1.0)
            k.tt(dn[rs], dn[rs], ex[rs], ALU.max)
            k.recip(dn[rs], dn[rs])
            k.ts(hnum[rs, :], t1[rs, 0:128], dn[rs], None, ALU.mult)
            wk, sc = dd[:, 2:3], dd[:, 3:4]
            k.tt(wk[rs], acol[rs], ML[rs], ALU.subtract)
            k.actf(wk[rs], wk[rs], AF.Exp)
            k.tt(sc, min_.a, ML, ALU.subtract)
            k.actf(sc, sc, AF.Exp)
            kw = tmp()
            k.ts(kw[rs, :], P[rs, O_MK:O_MK + 128], wk[rs], None, ALU.mult)
            pd = k.ps()
            k.mm(pd[:, 0:129], kw[rs, :], vaug[rs, :])
            k.stt(Cout.a, Cin.a, sc, pd[:, 0:129], ALU.mult, ALU.add)
            k.tt(mout.a, bL, ML, ALU.add)
        cst = cols()
        mu, var = cst[:, 0:1], cst[:, 1:2]
        k.red(mu, hnum.a, ALU.add)
        k.ts(mu, mu, -1.0 / 128, None, ALU.mult)
        hc = tmp()
        k.ts(hc.a, hnum.a, mu, None, ALU.add)
        sqj = tmp()
        k.actf(sqj.a, hc.a, AF.Square, accum=var)
        k.rsqrt(var, var, 1.0 / 128, EPS)
        sg = tmp()
        k.actf(sg.a, P[:, O_MO:O_MO + 128], AF.Sigmoid)
        k.stt(hc.a, hc.a, var, PR(0), ALU.mult, ALU.mult)
        k.tt(mix[:, 0:128], hc.a, sg.a, ALU.mult)

        if "m1" in skip:
            return
        tpi[0] = 0
        pu = k.ps()
        k.tr(pu[:, 0:128], P[:, O_SU:O_SU + 128], ident)
        uT = tmp()
        k.cp(uT.a, pu[:, 0:128], E=k.act)
        BU = [big(), big()]
        for ri in range(2):
            pp = k.ps()
            for j in range(4):
                k.mm(pp[:, j * 128:(j + 1) * 128], s5_Bb[:, ri, j, :], uT.a)
            k.cp(BU[ri].a, pp.a, E=k.act)
        Xr, Xi, tb = big(), big(), big()
        Pir = s5_Pi[:, 0, :, :].re("p j t -> p (j t)")
        Pii = s5_Pi[:, 1, :, :].re("p j t -> p (j t)")
        k.tt(Xr.a, BU[0].a, Pir, ALU.mult)
        k.tt(tb.a, BU[1].a, Pii, ALU.mult, E=k.pool)
        k.tt(Xr.a, Xr.a, tb.a, ALU.subtract)
        k.tt(Xi.a, BU[0].a, Pii, ALU.mult)
        k.tt(tb.a, BU[1].a, Pir, ALU.mult, E=k.pool)
        k.tt(Xi.a, Xi.a, tb.a, ALU.add)
        Gr, Gi = big(), big()
        Hr, Hi = BU[0], BU[1]
        Ptr = s5_P[:, 0, :, :]
        Pti = s5_P[:, 1, :, :]
        for hf, rs in halves():
            sin_, sout = S[hf]["s5"]
            ts_ = slice(hf * 64, (hf + 1) * 64)
            ci = cols()
            ir, ii_, t1_, t2_ = ci[:, 0:4], ci[:, 4:8], cols(), cols()
            k.tt(t1_[:, 0:4], sin_[:, 0, :], s5_lb[:, 0, :], ALU.mult)
            k.tt(t2_[:, 0:4], sin_[:, 1, :], s5_lb[:, 1, :], ALU.mult)
            k.tt(ir, t1_[:, 0:4], t2_[:, 0:4], ALU.subtract)
            k.tt(t1_[:, 4:8], sin_[:, 0, :], s5_lb[:, 1, :], ALU.mult)
            k.tt(t2_[:, 4:8], sin_[:, 1, :], s5_lb[:, 0, :], ALU.mult)
            k.tt(ii_, t1_[:, 4:8], t2_[:, 4:8], ALU.add)
            for j in range(4):
                fs = slice(j * 128 + hf * 64, j * 128 + hf * 64 + 64)
                k.scan(Gr[:, fs], C["ones"][:, 0:64], Xr[:, fs], ir[:, j:j + 1])
                k.scan(Gi[:, fs], C["ones"][:, 0:64], Xi[:, fs], ii_[:, j:j + 1])
            G3r = Gr.a.re("p (j t) -> p j t", j=4)[:, :, ts_]
            G3i = Gi.a.re("p (j t) -> p j t", j=4)[:, :, ts_]
            H3r = Hr.a.re("p (j t) -> p j t", j=4)[:, :, ts_]
            H3i = Hi.a.re("p (j t) -> p j t", j=4)[:, :, ts_]
            T3 = tb.a.re("p (j t) -> p j t", j=4)[:, :, ts_]
            k.tt(H3r, G3r, Ptr[:, :, ts_], ALU.mult)
            k.tt(T3, G3i, Pti[:, :, ts_], ALU.mult)
            k.tt(H3r, H3r, T3, ALU.subtract)
            k.tt(H3i, G3r, Pti[:, :, ts_], ALU.mult)
            k.tt(T3, G3i, Ptr[:, :, ts_], ALU.mult)
            k.tt(H3i, H3i, T3, ALU.add)
            last = hf * 64 + 63
            k.cp(sout[:, 0, :], Hr.a.re("p (j t) -> p j t", j=4)[:, :, last])
            k.cp(sout[:, 1, :], Hi.a.re("p (j t) -> p j t", j=4)[:, :, last])
        py = k.ps()
        n_ = 0
        for ri, Hh in ((0, Hr), (1, Hi)):
            for j in range(4):
                k.mm(py[:, 0:128], s5_Cb[:, ri, j, :], Hh[:, j * 128:(j + 1) * 128], start=(n_ == 0), stop=(n_ == 7))
                n_ += 1
        yT = tmp()
        k.stt(yT.a, uT.a, pc[:, 12:13], py[:, 0:128], ALU.mult, ALU.add)
        g1 = tmp()
        k.tt(g1.a, yT.a, yT.a, ALU.mult)
        k.ts(g1.a, g1.a, 0.044715, 1.0, ALU.mult, ALU.add)
        k.tt(g1.a, g1.a, yT.a, ALU.mult)
        k.actf(g1.a, g1.a, AF.Sigmoid, scale=1.5957691216057308)
        k.tt(yT.a, yT.a, g1.a, ALU.mult)
        pyt = k.ps()
        k.tr(pyt[:, 0:128], yT.a, ident)
        k.cp(mix[:, 128:256], pyt[:, 0:128], E=k.act)

        if "m2" in skip:
            return
        def shifted(pso, src_v, width, d, halo_v, hrows):
            sh = C[("shS%d" if sample else "shP%d") % d]
            k.mm(pso, sh, src_v, start=True, stop=False)
            k.mm(pso, hsel[0:hrows, (d - 1) * 128:d * 128] if hrows == 6 else hsel1[0:2, :], halo_v, start=False, stop=True)

        tpi[0] = 0
        R0 = O_R
        halo_in = S[0]["rsh_in"]
        for (c0, c1) in ((0, 512), (512, 640)):
            pp = k.ps()
            shifted(pp[:, 0:c1 - c0], P[:, R0 + c0:R0 + c1], c1 - c0, 1, halo_in[0:2, c0:c1], 2)
            k.tt(sh1[:, c0:c1], pp[:, 0:c1 - c0], P[:, R0 + c0:R0 + c1], ALU.subtract)
        k.tt(sh1.a, sh1.a, prb[:, 1152:1792], ALU.mult)
        k.tt(sh1.a, sh1.a, P[:, R0:R0 + 640], ALU.add)
        xr, xk, xv = sh1[:, 0:128], sh1[:, 128:256], sh1[:, 256:384]
        pl1 = k.ps()
        k.tr(pl1[:, 0:128], sh1[:, 384:512], ident)
        l1 = tmp()
        k.actf(l1[0:64, :], pl1[0:64, 0:128], AF.Tanh)
        k.cp(l1[64:128, :], pl1[64:128, 0:128], E=k.act)
        pl2 = k.ps()
        k.tr(pl2[:, 0:128], sh1[:, 512:640], ident)
        l2 = tmp()
        k.actf(l2.a, pl2[:, 0:128], AF.Sigmoid)
        pw = k.ps()
        k.mm(pw[:, 0:128], l1[0:64, :], w2a2[0:64, :])
        pa = k.ps()
        k.mm(pa[:, 0:128], l1[64:128, :], w2a2[64:128, :])
        pg = k.ps()
        k.mm(pg[:, 0:128], l2.a, g2.a)
        ld = tmp()
        k.tt(ld.a, pw[:, 0:128], PR(2), ALU.add)
        k.actf(ld.a, ld.a, AF.Sigmoid)
        k.ts(ld.a, ld.a, -float(np.exp(-0.5)), None, ALU.mult)
        av = tmp()
        k.tt(av.a, pa[:, 0:128], PR(3), ALU.add)
        k.actf(av.a, av.a, AF.Sigmoid)
        gg = tmp()
        k.cp(gg.a, pg[:, 0:128], E=k.act)
        if "r1" in skip:
            return
        kk = tmp()
        k.tt(kk.a, xk, PR(4), ALU.mult)
        sq = tmp()
        k.tt(sq.a, kk.a, kk.a, ALU.mult)
        cr = cols()
        k.red(cr[:, 0:2], sq.a.re("p (h c) -> p h c", h=2), ALU.add)
        k.rsqrt(cr[:, 0:2], cr[:, 0:2], 1.0, 1e-6)
        k.tt(kk.a.re("p (h c) -> p h c", h=2), kk.a.re("p (h c) -> p h c", h=2),
             cr[:, 0:2].re("p (h o) -> p h o", o=1).bc([128, 2, 64]), ALU.mult)
        kf = tmp()
        k.ts(kf.a, av.a, -1.0, None, ALU.add)
        k.tt(kf.a, kf.a, PR(5), ALU.mult)
        k.ts(kf.a, kf.a, 1.0, None, ALU.add)
        k.tt(kf.a, kf.a, xk, ALU.mult)
        bn = tmp()
        k.tt(bn.a, xr, kf.a, ALU.mult)
        k.tt(bn.a, bn.a, PR(6), ALU.mult)
        k.red(cr[:, 2:4], bn.a.re("p (h c) -> p h c", h=2), ALU.add)
        pgm = k.ps()
        k.mm(pgm[:, 0:128], C["triBD"], ld.a)
        Gm, Gi_, Gp = tmp(), tmp(), tmp()
        k.actf(Gm.a, pgm[:, 0:128], AF.Exp)
        k.actf(Gi_.a, pgm[:, 0:128], AF.Exp, scale=-1.0)
        k.tt(Gp.a, pgm[:, 0:128], ld.a, ALU.subtract)
        k.actf(Gp.a, Gp.a, AF.Exp)
        at, bt, kt, rt = tmp(), tmp(), tmp(), tmp()
        k.stt(at.a, kk.a, -1.0, Gp.a, ALU.mult, ALU.mult)
        k.tt(bt.a, kk.a, av.a, ALU.mult)
        k.tt(bt.a, bt.a, Gi_.a, ALU.mult)
        k.tt(kt.a, kf.a, Gi_.a, ALU.mult)
        k.tt(rt.a, xr, Gm.a, ALU.mult)
        if "r2" in skip:
            return
        y_r = tmp()
        fm = {}
        for nm, src in (("a", at), ("b", bt), ("k", kt), ("r", rt)):
            pp = k.ps()
            k.tr(pp[:, 0:128], src.a, ident)
            d_ = tmp()
            k.cp(d_.a, pp[:, 0:128], E=k.act)
            fm[nm] = d_
        tbase = tpi[0]
        for h in range(2):
            tpi[0] = tbase
            hs_ = slice(h * 64, (h + 1) * 64)
            hp = hs_
            pn, pnt, pak, prb_, prk = k.ps(), k.ps(), k.ps(), k.ps(), k.ps()
            k.mm(pn[:, 0:128], fm["b"][hp, :], fm["a"][hp, :])
            k.mm(pnt[:, 0:128], fm["a"][hp, :], fm["b"][hp, :])
            k.mm(pak[:, 0:128], fm["k"][hp, :], fm["a"][hp, :])
            k.mm(prb_[:, 0:128], fm["b"][hp, :], fm["r"][hp, :])
            k.mm(prk[:, 0:128], fm["k"][hp, :], fm["r"][hp, :])
            N, NT, AkT, RBT, RKT = tmp(), tmp(), tmp(), tmp(), tmp()
            k.tt(N.a, pn[:, 0:128], C["m01_st_strict"], ALU.mult)
            k.tt(NT.a, pnt[:, 0:128], C["m01_ts_strict"], ALU.mult)
            k.tt(AkT.a, pak[:, 0:128], C["m01_st_strict"], ALU.mult)
            k.tt(RBT.a, prb_[:, 0:128], C["m01_st_incl"], ALU.mult)
            k.tt(RKT.a, prk[:, 0:128], C["m01_st_incl"], ALU.mult)
            if "r3" in skip:
                return
            TT = neumann(N, NT)
            vh = xv[:, hs_]
            pav = k.ps()
            k.mm(pav[:, 0:64], AkT.a, vh)
            AkV = tmp()
            k.cp(AkV[:, 0:64], pav[:, 0:64], E=k.act)
            pw1 = k.ps()
            k.mm(pw1[hp, 0:128], at[:, hs_], TT.a)
            W1T = tmp()
            k.cp(W1T[hp, :], pw1[hp, 0:128], E=k.act)
            pU = k.ps()
            k.mm(pU[:, 0:64], TT.a, AkV[:, 0:64])
            U0 = tmp()
            k.cp(U0[:, 0:64], pU[:, 0:64])
            pY0 = k.ps()
            k.mm(pY0[:, 0:64], RKT.a, vh)
            Y0 = tmp()
            k.cp(Y0[:, 0:64], pY0[:, 0:64], E=k.act)
            if "r4" in skip:
                return
            U = tmp()
            for hf, rs in halves():
                Sin, Sout = S[hf]["rS"]
                pUs = k.ps()
                k.mm(pUs[rs, 0:64], W1T[hp, rs], Sin[hp, :])
                k.tt(U[rs, 0:64], U0[rs, 0:64], pUs[rs, 0:64], ALU.add)
                if "r4a" in skip:
                    return
                pYa, pYb = k.ps(), k.ps()
                k.mm(pYa[rs, 0:64], RBT[rs, rs], U[rs, 0:64])
                k.mm(pYb[rs, 0:64], fm["r"][hp, rs], Sin[hp, :])
                k.tt(y_r[rs, hs_], Y0[rs, 0:64], pYa[rs, 0:64], ALU.add)
                k.tt(y_r[rs, hs_], y_r[rs, hs_], pYb[rs, 0:64], ALU.add)
                if "r4b" in skip:
                    return
                pS = k.ps()
                k.mm(pS[hp, 0:64], bt[rs, hs_], U[rs, 0:64], start=True, stop=False)
                k.mm(pS[hp, 0:64], kt[rs, hs_], vh[rs, :], start=False, stop=True)
                if "r4c" in skip:
                    return
                pgl = k.ps()
                k.mm(pgl[hp, 0:1], ld[rs, hs_], onescol[rs, :])
                gl_ = cols()
                k.actf(gl_[hp, 0:1], pgl[hp, 0:1], AF.Exp)
                if "r4d" in skip:
                    return
                tS = tmp()
                k.ts(tS[hp, 0:64], Sin[hp, :], gl_[hp, 0:1], None, ALU.mult)
                k.stt(Sout[hp, :], pS[hp, 0:64], gl_[hp, 0:1], tS[hp, 0:64], ALU.mult, ALU.add)
        if "r5" in skip:
            return
        k.red(cr[:, 4:6], y_r.a.re("p (h c) -> p h c", h=2), ALU.add)
        k.ts(cr[:, 4:6], cr[:, 4:6], -1.0 / 64, None, ALU.mult)
        yc = tmp()
        k.tt(yc.a.re("p (h c) -> p h c", h=2), y_r.a.re("p (h c) -> p h c", h=2),
             cr[:, 4:6].re("p (h o) -> p h o", o=1).bc([128, 2, 64]), ALU.add)
        k.tt(sq.a, yc.a, yc.a, ALU.mult)
        k.red(cr[:, 6:8], sq.a.re("p (h c) -> p h c", h=2), ALU.add)
        k.rsqrt(cr[:, 6:8], cr[:, 6:8], 1.0 / 64, 64e-5)
        k.tt(yc.a.re("p (h c) -> p h c", h=2), yc.a.re("p (h c) -> p h c", h=2),
             cr[:, 6:8].re("p (h o) -> p h o", o=1).bc([128, 2, 64]), ALU.mult)
        k.tt(yc.a, yc.a, PR(7), ALU.mult)
        k.tt(yc.a, yc.a, PR(8), ALU.add)
        if "r5b" in skip:
            return
        bv = tmp()
        k.tt(bv.a.re("p (h c) -> p h c", h=2), xv.re("p (h c) -> p h c", h=2),
             cr[:, 2:4].re("p (h o) -> p h o", o=1).bc([128, 2, 64]), ALU.mult)
        k.tt(yc.a, yc.a, bv.a, ALU.add)
        k.tt(mix[:, 256:384], yc.a, gg.a, ALU.mult)
        if "r6" in skip:
            return
        halo_out = S[0]["rsh_out"]
        for (c0, c1) in ((0, 512), (512, 640)):
            pp = k.ps()
            k.mm(pp[0:32, 0:c1 - c0], tsel_r(sample), P[:, R0 + c0:R0 + c1])
            k.cp(halo_out[0:2, c0:c1], pp[0:2, 0:c1 - c0], E=k.act)

        if "m3" in skip:
            return
        tpi[0] = 0
        gin = S[0]["gcv_in"]
        gout = S[0]["gcv_out"]
        acc = gacc
        raw = P[:, O_GQ:O_GQ + 384]
        k.tt(acc.a, raw, prb[:, 1792 + 3 * 384:1792 + 4 * 384], ALU.mult)
        for d in (1, 2, 3):
            pp = k.ps()
            shifted(pp[:, 0:384], raw, 384, d, gin[0:6, :], 6)
            t_ = gtmp
            k.tt(t_.a, pp[:, 0:384], prb[:, 1792 + (3 - d) * 384:1792 + (4 - d) * 384], ALU.mult)
            k.tt(acc.a, acc.a, t_.a, ALU.add)
        pp = k.ps()
        k.mm(pp[0:32, 0:384], tails[:, 1 if sample else 0, :], raw)
        k.cp(gout[0:6, :], pp[0:6, 0:384], E=k.act)
        k.actf(acc.a, acc.a, AF.Silu)
        gq, gk, gv = acc[:, 0:128], acc[:, 128:256], acc[:, 256:384]
        cg = cols()
        jq = tmp()
        k.ttr(jq.a, gq, gq, ALU.mult, ALU.add, 0.0, cg[:, 0:1])
        k.ttr(jq.a, gk, gk, ALU.mult, ALU.add, 0.0, cg[:, 1:2])
        k.rsqrt(cg[:, 0:2], cg[:, 0:2], 1.0, 1e-6)
        k.ts(cg[:, 0:1], cg[:, 0:1], 128 ** -0.5, None, ALU.mult)
        qn, kn = tmp(), tmp()
        k.ts(qn.a, gq, cg[:, 0:1], None, ALU.mult)
        k.ts(kn.a, gk, cg[:, 1:2], None, ALU.mult)
        xg_, ab, gcol, beta = cg[:, 2:3], cg[:, 3:4], cg[:, 4:5], cg[:, 5:6]
        k.ts(xg_, P[:, O_GA:O_GA + 1], pc[:, 16:17], None, ALU.add)
        k.ts(ab, xg_, -1.0, None, ALU.mult)
        k.tt(ab, ab, xg_, ALU.min)
        k.actf(ab, ab, AF.Exp)
        k.actf(ab, ab, AF.Ln, bias=onescol)
        k.ts(xg_, xg_, 0.0, None, ALU.max)
        k.tt(xg_, xg_, ab, ALU.add)
        k.actf(gcol, pc[:, 15:16], AF.Exp)
        k.tt(gcol, gcol, xg_, ALU.mult)
        k.ts(gcol, gcol, -1.0, None, ALU.mult)
        k.actf(beta, P[:, O_GB:O_GB + 1], AF.Sigmoid)
        pG = k.ps()
        k.mm(pG[:, 0:1], C["triBD"], gcol)
        Gc, eG = cg[:, 6:7], cg[:, 7:8]
        k.cp(Gc, pG[:, 0:1])
        k.actf(eG, Gc, AF.Exp)
        dG = tmp()
        k.ts(dG.a, ident, Gc, None, ALU.mult)
        pGb = k.ps()
        k.mm(pGb[:, 0:128], C["ones"], dG.a)
        z1, z2, z3 = tmp(), tmp(), tmp()
        k.stt(z1.a, pGb[:, 0:128], Gc, C["mneg_st_strict"], ALU.subtract, ALU.min)
        k.actf(z1.a, z1.a, AF.Exp)
        k.stt(z2.a, pGb[:, 0:128], Gc, C["mneg_st_incl"], ALU.subtract, ALU.min)
        k.actf(z2.a, z2.a, AF.Exp)
        k.stt(z3.a, pGb[:, 0:128], Gc, C["mpos_ts_strict"], ALU.subtract, ALU.max)
        k.actf(z3.a, z3.a, AF.Exp, scale=-1.0)
        kb = tmp()
        k.ts(kb.a, kn.a, beta, None, ALU.mult)
        qg = tmp()
        k.ts(qg.a, qn.a, eG, None, ALU.mult)
        fT = {}
        for nm, src in (("k", kn), ("kb", kb), ("q", qn), ("qg", qg)):
            pp = k.ps()
            k.tr(pp[:, 0:128], src.a, ident)
            d_ = tmp()
            k.cp(d_.a, pp[:, 0:128], E=k.act)
            fT[nm] = d_
        pn, pnt, pqk = k.ps(), k.ps(), k.ps()
        k.mm(pn[:, 0:128], fT["k"].a, fT["kb"].a)
        k.mm(pnt[:, 0:128], fT["kb"].a, fT["k"].a)
        k.mm(pqk[:, 0:128], fT["k"].a, fT["q"].a)
        N, NT, QKT = tmp(), tmp(), tmp()
        k.stt(N.a, pn[:, 0:128], -1.0, z1.a, ALU.mult, ALU.mult)
        k.stt(NT.a, pnt[:, 0:128], -1.0, z3.a, ALU.mult, ALU.mult)
        k.tt(QKT.a, pqk[:, 0:128], z2.a, ALU.mult)
        TT = neumann(N, NT)
        rhs = grhs
        k.ts(rhs[:, 0:128], gv, beta, None, ALU.mult)
        k.ts(rhs[:, 128:256], kb.a, eG, None, ALU.mult)
        psv = k.ps()
        k.mm(psv[:, 0:128], TT.a, rhs[:, 0:128])
        solV = tmp()
        k.cp(solV.a, psv[:, 0:128], E=k.act)
        pkt = k.ps()
        k.mm(pkt[:, 0:128], rhs[:, 128:256], TT.a)
        solKT = tmp()
        k.cp(solKT.a, pkt[:, 0:128], E=k.act)
        U = tmp()
        o_g = tmp()
        for hf, rs in halves():
            Sin, Sout = S[hf]["gS"]
            pu_ = k.ps()
            k.mm(pu_[:, 0:128], solKT.a, Sin.a)
            k.tt(U[rs, :], solV[rs, :], pu_[rs, 0:128], ALU.subtract)
            poa, pob = k.ps(), k.ps()
            k.mm(poa[rs, 0:128], QKT[rs, rs], U[rs, :])
            k.mm(pob[:, 0:128], fT["qg"].a, Sin.a)
            k.cp(o_g[rs, :], poa[rs, 0:128], E=k.act)
            k.tt(o_g[rs, :], o_g[rs, :], pob[rs, 0:128], ALU.add)
            pgl = k.ps()
            k.mm(pgl[:, 0:1], sel[hf], gcol)
            cgl = cols()
            GL, eGL, kdc = cgl[:, 0:1], cgl[:, 1:2], cgl[:, 2:3]
            k.cp(GL, pgl[:, 0:1])
            k.actf(eGL, GL, AF.Exp)
            k.tt(kdc[rs], GL[rs], Gc[rs], ALU.subtract)
            k.actf(kdc[rs], kdc[rs], AF.Exp)
            kd = tmp()
            k.ts(kd[rs, :], kn[rs, :], kdc[rs], None, ALU.mult)
            pS = k.ps()
            k.mm(pS[:, 0:128], kd[rs, :], U[rs, :])
            k.stt(Sout.a, Sin.a, eGL, pS[:, 0:128], ALU.mult, ALU.add)
        k.actf(jq.a, o_g.a, AF.Square, accum=cg[:, 0:1])
        k.rsqrt(cg[:, 0:1], cg[:, 0:1], 1.0 / 128, EPS)
        k.stt(o_g.a, o_g.a, cg[:, 0:1], PR(13), ALU.mult, ALU.mult)
        sz = tmp()
        k.actf(sz.a, P[:, O_GZ:O_GZ + 128], AF.Silu)
        k.tt(mix[:, 384:512], o_g.a, sz.a, ALU.mult)

    def neumann(N, NT):
        Pm = ntmp()
        k.tt(Pm.a, N.a, ident, ALU.add)
        cur, curT = N, NT
        for j in range(1, 6):
            pnT = k.ps()
            k.mm(pnT[:, 0:128], cur.a, curT.a)
            nT = ntmp()
            k.cp(nT.a, pnT[:, 0:128], E=k.act)
            if j < 5:
                pn_ = k.ps()
                k.mm(pn_[:, 0:128], curT.a, cur.a)
                n_ = ntmp()
                k.cp(n_.a, pn_[:, 0:128])
            pP = k.ps()
            k.mm(pP[:, 0:128], nT.a, Pm.a)
            Pn = ntmp()
            k.tt(Pn.a, pP[:, 0:128], Pm.a, ALU.add)
            Pm = Pn
            if j < 5:
                cur, curT = n_, nT
        return Pm

    hsel1 = k.sb([2, 128], name="hsel1")
    k.memset(hsel1.a, 0.0)
    k.cp(hsel1[0:1, 0:1], onescol[0:1, :])
    tselP = k.sb([128, 32], name="tselP")
    tselS = k.sb([128, 32], name="tselS")
    k.memset(tselP.a, 0.0)
    k.memset(tselS.a, 0.0)
    k.cp(tselP[0:128, 0:1], tails[:, 0, 2:3])
    k.cp(tselS[0:128, 0:1], tails[:, 1, 2:3])
    k.cp(tselS[0:128, 1:2], tails[:, 1, 5:6])
    hsel1b = k.sb([2, 128], name="hsel1b")
    k.dma(hsel1b.a, hsel_d[2:6:3, 0:128])
    k.cp(hsel1.a, hsel1b.a)

    def tsel_r(sample):
        return (tselS if sample else tselP).a

    def phase_a_and_mix(l, xsrc):
        ti = 0
        xr_ = lambda s_, j_: (s_ * SL + j_ * 128) if l == 0 else (j_ * 4 + s_) * 128
        k.dma(xt[0].a, xsrc[xr_(0, 0):xr_(0, 0) + 128, :])
        for s in range(4):
            for j in range(NT_S):
                sample = j >= NT_P
                row0 = s * SL + j * 128
                x = xt[ti % 2]
                P = Pj[0]
                mix = mixt[0]
                ti += 1
                cs_ = cols()
                k.actf(junk.a, x.a, AF.Square, accum=cs_[:, 0:1])
                k.rsqrt(cs_[:, 0:1], cs_[:, 0:1], 1.0 / D, EPS)
                k.ts(hb.a, x.a, cs_[:, 0:1], None, ALU.mult)
                if ti < 4 * NT_S:
                    s2_, j2_ = divmod(ti, NT_S)
                    k.dma(xt[ti % 2].a, xsrc[xr_(s2_, j2_):xr_(s2_, j2_) + 128, :])
                for half in range(2):
                    for c8 in range(8):
                        kc = half * 8 + c8
                        k.tr(pst[:, c8 * 128:(c8 + 1) * 128], hb[:, kc * 128:(kc + 1) * 128], identb)
                    for c8 in range(8):
                        kc = half * 8 + c8
                        k.actf(hT[:, kc, :], pst[:, c8 * 128:(c8 + 1) * 128], AF.Copy, scale=gpre[:, kc:kc + 1])
                for nb in range(4):
                    pp = k.ps()
                    for kc in range(16):
                        k.mm(pp[:, 0:449], hT[:, kc, :], win[:, kc, nb * 449:(nb + 1) * 449], start=(kc == 0), stop=(kc == 15))
                    k.cp(P[:, nb * 449:(nb + 1) * 449], pp[:, 0:449], E=(k.act if nb % 2 else k.dve))
                if not sample:
                    S = [dict(mC=(st_mC, st_mC), mm=(st_mm, st_mm), s5=(st_s5, st_s5), rS=(st_rS, st_rS), gS=(st_gS, st_gS))
                         for _ in range(2)]
                    S[0].update(rsh_in=st_rsh.a, rsh_out=st_rsh.a, gcv_in=st_gcv.a, gcv_out=st_gcv.a)
                else:
                    S = []
                    for hf in range(2):
                        q = 4 * s + 2 * (j - NT_P) + hf
                        k.dma(ss_mC[hf].a, i_mCn[l, q])
                        k.dma(ss_mm[hf].a, i_mm[l, q])
                        k.dma(ss_s5[hf].a.re("p r j -> p (r j)"), i_s5[l, q])
                        k.dma(ss_rS[hf].a, i_rS[l, q])
                        k.dma(ss_rsh[hf:hf + 1, :], i_rsh[l, q:q + 1, :])
                        k.dma(ss_gS[hf].a, i_gS[l, q])
                        k.dma(ss_gcv[3 * hf:3 * hf + 3, :], i_gcv[l, q])
                        S.append(dict(mC=(ss_mC[hf], ss_mC[hf]), mm=(ss_mm[hf], ss_mm[hf]), s5=(ss_s5[hf], ss_s5[hf]),
                                      rS=(ss_rS[hf], ss_rS[hf]), gS=(ss_gS[hf], ss_gS[hf])))
                    S[0].update(rsh_in=ss_rsh.a, rsh_out=ss_rsh.a, gcv_in=ss_gcv.a, gcv_out=ss_gcv.a)
                if "mix" not in skip:
                    mixer_tile(l, P, mix, S, sample)
                k.dma(mixp[l][row0:row0 + 128, :], mix.a)
                if sample:
                    for hf in range(2):
                        q = 1 + 4 * s + 2 * (j - NT_P) + hf
                        k.dma(o_mCn[l, q], ss_mC[hf].a)
                        k.dma(o_mm[l, q], ss_mm[hf].a)
                        k.dma(o_s5[l, q], ss_s5[hf].a.re("p r j -> p (r j)"))
                        k.dma(o_rS[l, q], ss_rS[hf].a)
                        k.dma(o_rsh[l, q:q + 1, :], ss_rsh[hf:hf + 1, :])
                        k.dma(o_gS[l, q], ss_gS[hf].a)
                        k.dma(o_gcv[l, q], ss_gcv[3 * hf:3 * hf + 3, :])
        k.dma(o_mCn[l, 0], st_mC.a)
        k.dma(o_mm[l, 0], st_mm.a)
        k.dma(o_s5[l, 0], st_s5.a.re("p r j -> p (r j)"))
        k.dma(o_rS[l, 0], st_rS.a)
        k.dma(o_rsh[l, 0:1, :], st_rsh[0:1, :])
        k.dma(o_gS[l, 0], st_gS.a)
        k.dma(o_gcv[l, 0], st_gcv[0:3, :])


    def phase_b(l, xres, xdst):
        k.dma(gbc[:, 0, :], gvec[l, 1, :].pbc(128))
        k.dma(gbc[:, 1, :], gvec[l, 3, :].pbc(128))
        k.dma(bglu.a, gvec[l, 4, 0:512].pbc(128))
        k.dma(gffn.a, gcols[l, 1])
        for kc in range(4):
            i = cnt[0] % 2
            cnt[0] += 1
            k.dma(stg[i][:, 0:512], w_glu_d[l, kc * 128:(kc + 1) * 128, :])
            k.cp(wglu[:, kc, :], stg[i][:, 0:512], E=cast_engs[i])
        nblk = (NT_S + 2) // 3
        for blk in range(nblk):
            t0 = blk * 3
            nt = min(3, NT_S - t0)
            TB = nt * 128
            for ti in range(nt):
                msel = scr[:, 0, :]
                x1t = scr[:, 2, :]
                for q in range(4):
                    cand = scr[:, 1, :]
                    r0 = q * SL + (t0 + ti) * 128
                    ch, wi = r0 // 512, r0 % 512
                    k.dma(cand.re("p (s c) -> p s c", s=4),
                          mixa[l].a[ch * 2048:(ch + 1) * 2048, :].re("(s t) c -> t s c", s=4)[wi:wi + 128, :, :])
                    if q == 0:
                        k.ts(msel, cand, qm[:, 0:1], None, ALU.mult)
                    else:
                        k.stt(msel, cand, qm[:, q:q + 1], msel, ALU.mult, ALU.add)
                m4 = msel.re("p (s c) -> p s c", s=4)
                k.cp(ys32.a.re("p (s c) -> p s c", s=4), m4[:, :, 128:256])
                k.cp(ysb.a, ys32.a, E=k.act)
                for c4 in range(4):
                    k.tr(pst[:, c4 * 128:(c4 + 1) * 128], ysb[:, c4 * 128:(c4 + 1) * 128], identb)
                k.cp(ysT.a.re("p c t -> p (c t)"), pst[:, 0:512], E=k.act)
                pp = k.ps()
                for c4 in range(4):
                    k.mm(pp.a, ysT[:, c4, :], wglu[:, c4, :], start=(c4 == 0), stop=(c4 == 3))
                k.tt(sgl.a, pp.a, bglu.a, ALU.add)
                k.actf(sgl.a, sgl.a, AF.Sigmoid)
                k.tt(m4[:, :, 128:256], ys32.a.re("p (s c) -> p s c", s=4), sgl.a.re("p (s c) -> p s c", s=4), ALU.mult)
                k.cp(hb.a, msel, E=k.act)
                for half in range(2):
                    for c8 in range(8):
                        kc = half * 8 + c8
                        k.tr(pst[:, c8 * 128:(c8 + 1) * 128], hb[:, kc * 128:(kc + 1) * 128], identb)
                    k.cp(mixT[:, half * 8:(half + 1) * 8, ti * 128:(ti + 1) * 128],
                         pst.a.re("p (c t) -> p c t", c=8), E=(k.act if half else k.dve))
            if stop == "b1":
                return
            t1all = scr
            wob = [wbA[:, 0:8192].re("p (c f) -> p c f", c=16), wbB[:, 0:8192].re("p (c f) -> p c f", c=16)]
            for nb in range(4):
                k.dma(wob[nb % 2], wo_s[l][nb])
                for ti in range(nt):
                    pp = k.ps()
                    for kc in range(16):
                        k.mm(pp.a, mixT[:, kc, ti * 128:(ti + 1) * 128], wob[nb % 2][:, kc, :], start=(kc == 0), stop=(kc == 15))
                    k.cp(t1all[:, ti, nb * 512:(nb + 1) * 512], pp.a, E=(k.act if ti % 2 else k.dve))
            if stop == "b2":
                return
            for ti in range(nt):
                r0 = (t0 + ti) * 128
                cs_ = cols()
                k.actf(junk.a, t1all[:, ti, :], AF.Square, accum=cs_[:, 0:1])
                k.rsqrt(cs_[:, 0:1], cs_[:, 0:1], 1.0 / D, EPS)
                k.dma(xt[0].a, xres[r0:r0 + 128, :])
                k.stt(t1all[:, ti, :], t1all[:, ti, :], cs_[:, 0:1], gbc[:, 0, :], ALU.mult, ALU.mult)
                k.tt(t1all[:, ti, :], t1all[:, ti, :], xt[0].a, ALU.add)
                k.dma(xmid[r0:r0 + 128, :], t1all[:, ti, :])
                k.actf(junk.a, t1all[:, ti, :], AF.Square, accum=cs_[:, 1:2])
                k.rsqrt(cs_[:, 1:2], cs_[:, 1:2], 1.0 / D, EPS)
                k.ts(hb.a, t1all[:, ti, :], cs_[:, 1:2], None, ALU.mult)
                for half in range(2):
                    for c8 in range(8):
                        kc = half * 8 + c8
                        k.tr(pst[:, c8 * 128:(c8 + 1) * 128], hb[:, kc * 128:(kc + 1) * 128], identb)
                    for c8 in range(8):
                        kc = half * 8 + c8
                        k.actf(mixT[:, kc, ti * 128:(ti + 1) * 128], pst[:, c8 * 128:(c8 + 1) * 128], AF.Copy,
                               scale=gffn[:, kc:kc + 1])
            if stop == "b3":
                return
            for fc in range(44):
                wb = wgu[fc % 2]
                k.dma(wb[:, 0, :, :], wg_s[l][fc])
                k.dma(wb[:, 1, :, :], wu_s[l][fc])
                pg_, pu_ = k.ps(), k.ps()
                for kc in range(16):
                    k.mm(pg_[:, 0:TB], wb[:, 0, kc, :], mixT[:, kc, 0:TB], start=(kc == 0), stop=(kc == 15))
                for kc in range(16):
                    k.mm(pu_[:, 0:TB], wb[:, 1, kc, :], mixT[:, kc, 0:TB], start=(kc == 0), stop=(kc == 15))
                sg_ = sgT
                k.actf(sg_[:, 0:TB], pg_[:, 0:TB], AF.Silu)
                k.tt(actT[:, fc, 0:TB], sg_[:, 0:TB], pu_[:, 0:TB], ALU.mult)
            if stop == "b4":
                return
            wdh = [wbA.a.re("p (c f) -> p c f", c=22), wbB.a.re("p (c f) -> p c f", c=22)]
            for nb in range(4):
                k.dma(wdh[0], wd_s[l][nb, :, 0:22, :])
                k.dma(wdh[1], wd_s[l][nb, :, 22:44, :])
                pps = [k.ps() for _ in range(nt)]
                for half in range(2):
                    for ti in range(nt):
                        for f_ in range(22):
                            fc = half * 22 + f_
                            k.mm(pps[ti].a, actT[:, fc, ti * 128:(ti + 1) * 128], wdh[half][:, f_, :], start=(fc == 0), stop=(fc == 43))
                for ti in range(nt):
                    k.cp(scr[:, ti, nb * 512:(nb + 1) * 512], pps[ti].a, E=(k.act if ti % 2 else k.dve))
            if stop == "b5":
                return
            for ti in range(nt):
                r0 = (t0 + ti) * 128
                cs_ = cols()
                k.actf(junk.a, scr[:, ti, :], AF.Square, accum=cs_[:, 0:1])
                k.rsqrt(cs_[:, 0:1], cs_[:, 0:1], 1.0 / D, EPS)
                k.dma(xt[1].a, xmid[r0:r0 + 128, :])
                k.stt(scr[:, ti, :], scr[:, ti, :], cs_[:, 0:1], gbc[:, 1, :], ALU.mult, ALU.mult)
                k.tt(scr[:, ti, :], scr[:, ti, :], xt[1].a, ALU.add)
                k.dma(xdst[r0:r0 + 128, :], scr[:, ti, :])

    for l in range(DEPTH):
        with ExitStack() as es:
            k.stack = es
            win = k.sb([128, 16, NQ], BF16, "win")
            pc = k.sb([128, 32], name="pc")
            prb = k.sb([128, 3328], name="prb")
            gpre = k.sb([128, 16], name="gpre")
            s5_P = k.sb([128, 2, 4, 128], name="s5P")
            s5_Pi = k.sb([128, 2, 4, 128], name="s5Pi")
            s5_lb = k.sb([128, 2, 4], name="s5lb")
            s5_Bb = k.sb([128, 2, 4, 128], name="s5Bb")
            s5_Cb = k.sb([128, 2, 4, 128], name="s5Cb")
            w2a2 = k.sb([128, 128], name="w2a2")
            g2 = k.sb([128, 128], name="g2")
            st_mC = k.sb([128, 129], name="st_mC")
            st_mm = k.sb([128, 1], name="st_mm")
            st_s5 = k.sb([128, 2, 4], name="st_s5")
            st_rS = k.sb([128, 64], name="st_rS")
            st_rsh = k.sb([2, 640], name="st_rsh")
            st_gS = k.sb([128, 128], name="st_gS")
            st_gcv = k.sb([6, 384], name="st_gcv")
            ss_mC = [k.sb([128, 129], name="ss_mC%d" % i) for i in range(2)]
            ss_mm = [k.sb([128, 1], name="ss_mm%d" % i) for i in range(2)]
            ss_s5 = [k.sb([128, 2, 4], name="ss_s5%d" % i) for i in range(2)]
            ss_rS = [k.sb([128, 64], name="ss_rS%d" % i) for i in range(2)]
            ss_rsh = k.sb([2, 640], name="ss_rsh")
            ss_gS = [k.sb([128, 128], name="ss_gS%d" % i) for i in range(2)]
            ss_gcv = k.sb([6, 384], name="ss_gcv")

            hT = k.sb([128, 16, 128], BF16, "hT")
            Pj = [k.sb([128, NQ], name="P%d" % i) for i in range(1)]
            mixt = [k.sb([128, 512], name="mixt%d" % i) for i in range(1)]

            npool = [T(n="np%d" % i) for i in range(8)]

            big_pool = [T((128, 512), n="bp%d" % i) for i in range(7)]


            vaug = k.sb([128, 129], name="vaug")
            intra = k.sb([128, 129], name="intra")
            mt1 = k.sb([128, 129], name="mt1")
            sh1 = k.sb([128, 640], name="sh1")
            gacc = k.sb([128, 384], name="gacc")
            gtmp = k.sb([128, 384], name="gtmp")
            grhs = k.sb([128, 256], name="grhs")
            del tmp_pool[:]
            tpi[0] = 0
            layer_setup(l)
            if stop == "setup":
                k.final_wait()
                return nc
            phase_a_and_mix(l, xg if l == 0 else x1all)
            if stop == "A":
                k.final_wait()
                return nc
            k.barrier()
            k.stack = None
        k.allgather(mixa[l], mixp[l], 512)
        if stop == "AG":
            k.final_wait()
            return nc
        with ExitStack() as es:
            k.stack = es
            mixT = k.sb([128, 16, 384], BF16, "mixT")
            scr = k.sb([128, 3, 2048], name="scr")
            actT = k.sb([128, 44, 384], BF16, "actT")
            wbA = k.sb([128, 11264], BF16, "wbA")
            wbB = k.sb([128, 11264], BF16, "wbB")
            wgu = [k.sb([128, 2, 16, 128], BF16, "wgu%d" % i) for i in range(2)]
            gbc = k.sb([128, 2, 2048], name="gbc")
            bglu = k.sb([128, 512], name="bglu")
            gffn = k.sb([128, 16], name="gffn")
            wglu = k.sb([128, 4, 512], BF16, "wglu")
            ysb = k.sb([128, 512], BF16, "ysb")
            ysT = k.sb([128, 4, 128], BF16, "ysT")
            ys32 = k.sb([128, 512], name="ys32")
            sgl = k.sb([128, 512], name="sgl")
            sgT = k.sb([128, 384], name="sgT")
            phase_b(l, xo if l == 0 else x1own, x1own if l == 0 else y_own)
            if stop in ("B", "b1", "b2", "b3", "b4", "b5"):
                k.final_wait()
                return nc
            k.barrier()
            k.stack = None
        if l == 0:
            k.allgather(x1all, x1own, 128)
    k.final_wait()
    return nc


def _qcols(r):
    W = 512
    NM = 4 * W + 8
    NSs = W
    NR = 3 * W + 256
    m0, s0, r0 = 0, NM, NM + NSs
    g0 = NM + NSs + NR
    hd = lambda base, h, w=128: list(range(base + h * w, base + (h + 1) * w))
    cols = []
    for blk in range(4):
        cols += hd(m0 + blk * W, r)
    cols += [m0 + 4 * W + r, m0 + 4 * W + 4 + r]
    cols += hd(s0, r)
    for blk in range(3):
        cols += hd(r0 + blk * W, r)
    cols += list(range(r0 + 3 * W, r0 + 3 * W + 256))
    for blk in range(4):
        cols += hd(g0 + blk * W, r)
    cols += [g0 + 4 * W + r, g0 + 4 * W + 4 + r]
    assert len(cols) == NQ
    return np.array(cols)


_CACHE = {}


def _run(inp, SEQ, stop=None, skip=()):
    f = lambda n: np.asarray(inp[n], np.float32)
    PT = SEQ // 4
    SL = PT + 256
    if SEQ not in _CACHE:
        _CACHE[SEQ] = build(SEQ, stop, skip)
    nc = _CACHE[SEQ]
    cst = _consts()
    cmat = np.stack([cst[n] for n in CONST_ORDER]).astype(np.float32)
    tails = np.stack([cst["tailP"], cst["tailS"]]).astype(np.float32)
    xp, xs = f("x_prompt"), f("x_sample")
    in_maps = []
    h = lambda a, r: a[..., r * 128:(r + 1) * 128]
    for c in range(8):
        b, r = c // 4, c % 4
        slabs = []
        for s in range(4):
            slabs.append(np.concatenate([xp[b, s * PT:(s + 1) * PT],
                                         xs[16 * b + 4 * s:16 * b + 4 * s + 4].reshape(256, D)], 0))
        xgrp = np.concatenate(slabs, 0)
        m = {"xg": xgrp, "xo": slabs[r], "cmat": cmat, "hsel": cst["hsel"], "tails": tails}
        qm = np.zeros((128, 4), np.float32)
        qm[:, r] = 1.0
        m["qmask"] = qm
        qc = _qcols(r)
        m["w_in"] = np.ascontiguousarray(f("w_in")[:, :, qc])
        perm = np.concatenate([np.concatenate([np.arange(g * 512 + q * 128, g * 512 + (q + 1) * 128) for g in range(4)])
                               for q in range(4)])
        m["w_out"] = np.ascontiguousarray(f("w_out")[:, perm, :])
        m["w_gate"], m["w_up"], m["w_down"] = f("w_gate"), f("w_up"), f("w_down")
        m["w_glu"] = f("s5_w_glu")
        gv = np.zeros((DEPTH, 5, D), np.float32)
        gv[:, 0], gv[:, 1], gv[:, 2], gv[:, 3] = f("g_pre_mix"), f("g_post_mix"), f("g_pre_ffn"), f("g_post_ffn")
        gv[:, 4, 0:512] = f("s5_b_glu")
        m["gvec"] = gv
        m["gcols"] = np.stack([f("g_pre_mix").reshape(DEPTH, 16, 128).transpose(0, 2, 1), f("g_pre_ffn").reshape(DEPTH, 16, 128).transpose(0, 2, 1)], 1)
        pcol = np.zeros((DEPTH, 128, 32), np.float32)
        gs = slice(8 * r, 8 * r + 8)
        tile4 = lambda a: a.reshape(DEPTH, 4, 128).transpose(0, 2, 1)
        pcol[:, :, 0:4] = tile4(f("s5_lam_re")[:, gs].reshape(DEPTH, 512))
        pcol[:, :, 4:8] = tile4(f("s5_lam_im")[:, gs].reshape(DEPTH, 512))
        pcol[:, :, 8:12] = tile4(np.repeat(f("s5_log_dt")[:, gs], 64, axis=1))
        pcol[:, :, 12] = h(f("s5_D"), r)
        pcol[:, :, 13] = f("mlstm_gate_bias")[:, 0, r][:, None]
        pcol[:, :, 14] = f("mlstm_gate_bias")[:, 1, r][:, None]
        pcol[:, :, 15] = f("gdn_A_log")[:, r][:, None]
        pcol[:, :, 16] = f("gdn_dt_bias")[:, r][:, None]
        m["pcol"] = pcol
        prow = np.zeros((DEPTH, 1, 3328), np.float32)
        names9 = [h(f("mlstm_norm_g"), r), h(f("rwkv_w0"), r), h(f("rwkv_a0"), r), h(f("rwkv_k_k"), r), h(f("rwkv_k_a"), r),
                  f("rwkv_r_k").reshape(DEPTH, 512)[:, r * 128:(r + 1) * 128], h(f("rwkv_ln_w"), r), h(f("rwkv_ln_b"), r),
                  f("gdn_norm_g")]
        for i, a in enumerate(names9):
            prow[:, 0, i * 128:(i + 1) * 128] = a
        mu = f("rwkv_mu")
        prow[:, 0, 1152:1792] = np.concatenate([h(mu[:, 0:512], r), h(mu[:, 512:1024], r), h(mu[:, 1024:1536], r), mu[:, 1536:]], 1)
        cw = f("gdn_conv_w")
        for j in range(4):
            prow[:, 0, 1792 + j * 384:1792 + (j + 1) * 384] = np.concatenate([h(cw[:, j, 0:512], r), h(cw[:, j, 512:1024], r), h(cw[:, j, 1024:1536], r)], 1)
        m["prow"] = prow
        def blk(a_gpc):
            o = np.zeros((DEPTH, 128, 4, 128), np.float32)
            for g in range(8):
                j, hh = g // 2, g % 2
                o[:, hh * 64:(hh + 1) * 64, j, g * 16:(g + 1) * 16] = a_gpc[:, g]
            return o
        m["s5b"] = np.stack([blk(f("s5_B_re")[:, gs]), blk(f("s5_B_im")[:, gs])], 1)
        m["s5c"] = np.stack([blk(f("s5_C_re")[:, gs].transpose(0, 1, 3, 2)), blk(f("s5_C_im")[:, gs].transpose(0, 1, 3, 2))], 1)
        m["rw2"] = np.concatenate([h(f("rwkv_w2"), r), h(f("rwkv_a2"), r)], 1)
        m["rg2"] = np.ascontiguousarray(h(f("rwkv_g2"), r))
        sq = slice(16 * b, 16 * b + 16)
        m["i_mCn"] = np.concatenate([f("state_mlstm_C")[:, sq, r], f("state_mlstm_n")[:, sq, r][..., None]], -1)
        m["i_mm"] = np.repeat(f("state_mlstm_m")[:, sq, r][..., None, None], 128, axis=2)
        sre = f("state_s5_re")[:, sq, gs].reshape(DEPTH, 16, 4, 128).transpose(0, 1, 3, 2)
        sim = f("state_s5_im")[:, sq, gs].reshape(DEPTH, 16, 4, 128).transpose(0, 1, 3, 2)
        m["i_s5"] = np.concatenate([sre, sim], -1)
        rS = f("state_rwkv_S")[:, sq, 2 * r:2 * r + 2]
        m["i_rS"] = np.ascontiguousarray(rS.transpose(0, 1, 2, 4, 3)).reshape(DEPTH, 16, 128, 64)
        rsh = f("state_rwkv_shift")[:, sq]
        m["i_rsh"] = np.concatenate([h(rsh[..., 0:512], r), h(rsh[..., 512:1024], r), h(rsh[..., 1024:1536], r), rsh[..., 1536:]], -1)
        m["i_gS"] = np.ascontiguousarray(f("state_gdn_S")[:, sq, r])
        gc = f("state_gdn_conv")[:, sq]
        m["i_gcv"] = np.concatenate([h(gc[..., 0:512], r), h(gc[..., 512:1024], r), h(gc[..., 1024:1536], r)], -1)
        in_maps.append({kk_: np.ascontiguousarray(v, dtype=np.float32) for kk_, v in m.items()})
    res = run_bass_kernel_spmd(nc, in_maps, core_ids=list(range(8)))
    R = res.results
    B, NSQ = 2, 32
    yp = np.zeros((B, SEQ, D), np.float32)
    ys = np.zeros((NSQ, 64, D), np.float32)
    def mk(bn):
        return (np.zeros((DEPTH, bn, 4, 128, 128), np.float32), np.zeros((DEPTH, bn, 4, 128), np.float32),
                np.zeros((DEPTH, bn, 4), np.float32), np.zeros((DEPTH, bn, 32, 64), np.float32),
                np.zeros((DEPTH, bn, 32, 64), np.float32), np.zeros((DEPTH, bn, 8, 64, 64), np.float32),
                np.zeros((DEPTH, bn, 1792), np.float32), np.zeros((DEPTH, bn, 4, 128, 128), np.float32),
                np.zeros((DEPTH, bn, 3, 1536), np.float32))
    Pp, Ps = mk(B), mk(NSQ)
    for c in range(8):
        b, r = c // 4, c % 4
        o = R[c]
        y = o["y_own"]
        yp[b, r * PT:(r + 1) * PT] = y[:PT]
        ys[16 * b + 4 * r:16 * b + 4 * r + 4] = y[PT:].reshape(4, 64, D)
        for (dst, idx, src_sl) in ((Pp, b, slice(0, 1)), (Ps, slice(16 * b, 16 * b + 16), slice(1, 17))):
            def put(arr, val):
                if isinstance(idx, int):
                    arr[:, idx] = val[:, 0]
                else:
                    arr[:, idx] = val
            return_none = None
            mCn = o["o_mCn"][:, src_sl]
            tgt = (lambda a: a[:, idx]) if not isinstance(idx, int) else (lambda a: a[:, idx:idx + 1])
            tgt(dst[0])[:, :, r] = mCn[..., 0:128]
            tgt(dst[1])[:, :, r] = mCn[..., 128]
            tgt(dst[2])[:, :, r] = o["o_mm"][:, src_sl][:, :, 0, 0]
            s5 = o["o_s5"][:, src_sl]
            nn = s5.shape[1]
            tgt(dst[3])[:, :, 8 * r:8 * r + 8] = s5[..., 0:4].transpose(0, 1, 3, 2).reshape(DEPTH, nn, 8, 64)
            tgt(dst[4])[:, :, 8 * r:8 * r + 8] = s5[..., 4:8].transpose(0, 1, 3, 2).reshape(DEPTH, nn, 8, 64)
            rS = o["o_rS"][:, src_sl].reshape(DEPTH, nn, 2, 64, 64)
            tgt(dst[5])[:, :, 2 * r:2 * r + 2] = rS.transpose(0, 1, 2, 4, 3)
            rsh = o["o_rsh"][:, src_sl]
            for i3 in range(3):
                tgt(dst[6])[:, :, i3 * 512 + r * 128:i3 * 512 + (r + 1) * 128] = rsh[..., i3 * 128:(i3 + 1) * 128]
            if r == 0:
                tgt(dst[6])[:, :, 1536:] = rsh[..., 384:]
            tgt(dst[7])[:, :, r] = o["o_gS"][:, src_sl]
            gcv = o["o_gcv"][:, src_sl]
            for i3 in range(3):
                tgt(dst[8])[:, :, :, i3 * 512 + r * 128:i3 * 512 + (r + 1) * 128] = gcv[..., i3 * 128:(i3 + 1) * 128]
    return (yp, ys) + Pp + Ps


def kernel(**inputs):
    return _run(inputs, 16384)
```
